# Optimizing a Trainium2 kernel written in Bass

```python
import jax
import jax.numpy as jnp
from jax import lax
import numpy as np

D_MODEL = 1024
BATCH = 4
SEQ = 4096
DEPTH = 4

GRID_W = 64
CTX_LEN = 256
QBLOCK = 128
ROPE_THETA = 10000.0
EPS = 1e-6

MLA_HEADS = 8
MLA_Q_RANK = 256
MLA_KV_RANK = 128
MLA_NOPE = 64
MLA_ROPE = 32
MLA_V = 64
MLA_WIDTH = MLA_HEADS * MLA_V

GQA_HEADS = 8
GQA_KV_HEADS = 2
GQA_HEAD_DIM = 64
GQA_WIDTH = GQA_HEADS * GQA_HEAD_DIM

GDN_HEADS = 4
GDN_HEAD_DIM = 128
GDN_WIDTH = GDN_HEADS * GDN_HEAD_DIM
GDN_CONV = 4
GDN_CHUNK = 64

LRU_WIDTH = 512
LRU_BLOCKS = 8
LRU_BLOCK_W = LRU_WIDTH // LRU_BLOCKS
LRU_CONV = 4
LRU_C = 8.0

DEEPNORM_ALPHA = (2 * DEPTH) ** 0.25
DEEPNORM_BETA = (8 * DEPTH) ** -0.25

N_ATT = (DEPTH + 1) // 2
N_REC = DEPTH // 2

ATT_SPLITS = (MLA_Q_RANK, MLA_KV_RANK, MLA_ROPE, MLA_WIDTH,
              GQA_WIDTH, GQA_KV_HEADS * GQA_HEAD_DIM, GQA_KV_HEADS * GQA_HEAD_DIM, GQA_WIDTH)
ATT_IN = sum(ATT_SPLITS)
ATT_MIX = MLA_WIDTH + GQA_WIDTH
REC_SPLITS = (3 * GDN_WIDTH, GDN_WIDTH, 2 * GDN_HEADS, 2 * GDN_HEADS, LRU_WIDTH, LRU_WIDTH)
REC_IN = sum(REC_SPLITS)
REC_MIX = GDN_WIDTH + LRU_WIDTH

kernel_name = 'hybrid_mla_gqa_gdn_rglru_dit_trunk'


def _split(p, sizes):
    idx = np.cumsum(sizes)[:-1].tolist()
    return jnp.split(p, idx, axis=-1)


def layer_norm(x, g, b):
    xf = x.astype(jnp.float32)
    mu = jnp.mean(xf, -1, keepdims=True)
    var = jnp.mean(jnp.square(xf - mu), -1, keepdims=True)
    return ((xf - mu) * lax.rsqrt(var + EPS) * g + b).astype(x.dtype)


def rms_norm(x, g):
    xf = x.astype(jnp.float32)
    return (xf * lax.rsqrt(jnp.mean(jnp.square(xf), -1, keepdims=True) + EPS) * g).astype(x.dtype)


def l2_norm(x):
    return x * lax.rsqrt(jnp.sum(jnp.square(x), -1, keepdims=True) + EPS)


def axial_rope_tables(row, col, dim):
    half = dim // 2
    inv = ROPE_THETA ** (-jnp.arange(0, half, 2, dtype=jnp.float32) / half)
    ang_r = row.astype(jnp.float32)[:, None] * inv
    ang_c = col.astype(jnp.float32)[:, None] * inv
    return (jnp.cos(ang_r), jnp.sin(ang_r), jnp.cos(ang_c), jnp.sin(ang_c))


def _rotate(x, cos, sin):
    x1, x2 = jnp.split(x, 2, axis=-1)
    cos = cos[None, :, None, :].astype(x.dtype)
    sin = sin[None, :, None, :].astype(x.dtype)
    return jnp.concatenate([x1 * cos - x2 * sin, x1 * sin + x2 * cos], axis=-1)


def apply_axial_rope(x, tabs):
    cr, sr, cc, sc = tabs
    xr, xc = jnp.split(x, 2, axis=-1)
    return jnp.concatenate([_rotate(xr, cr, sr), _rotate(xc, cc, sc)], axis=-1)


def block_attention(q, k, v):
    bsz, s, hk, grp, dk = q.shape
    scale = dk ** -0.5
    nb = s // QBLOCK
    qb = jnp.moveaxis(q.reshape(bsz, nb, QBLOCK, hk, grp, dk), 1, 0)

    def one_block(qblk):
        sc = jnp.einsum('bqhgd,bkhd->bhgqk', qblk, k, preferred_element_type=jnp.float32) * scale
        p = jax.nn.softmax(sc, axis=-1).astype(v.dtype)
        return jnp.einsum('bhgqk,bkhd->bqhgd', p, v)

    o = lax.map(one_block, qb)
    return jnp.moveaxis(o, 0, 1).reshape(bsz, s, hk, grp, v.shape[-1])


def _attn_project(h, w_in, q_norm_a, w_uq, kv_norm_a, w_ukv, q_norm_b, k_norm_b, rope_a, rope_b):
    bsz, t, _ = h.shape
    cq, ckv, kr, ga, qb, kb, vb, gb = _split(h @ w_in, ATT_SPLITS)
    qa = (rms_norm(cq, q_norm_a) @ w_uq).reshape(bsz, t, MLA_HEADS, MLA_NOPE + MLA_ROPE)
    kv = (rms_norm(ckv, kv_norm_a) @ w_ukv).reshape(bsz, t, MLA_HEADS, MLA_NOPE + MLA_V)
    q_nope, q_rope = qa[..., :MLA_NOPE], qa[..., MLA_NOPE:]
    k_nope, va = kv[..., :MLA_NOPE], kv[..., MLA_NOPE:]
    kr = kr[:, :, None, :]
    qb = rms_norm(qb.reshape(bsz, t, GQA_HEADS, GQA_HEAD_DIM), q_norm_b)
    kb = rms_norm(kb.reshape(bsz, t, GQA_KV_HEADS, GQA_HEAD_DIM), k_norm_b)
    vb = vb.reshape(bsz, t, GQA_KV_HEADS, GQA_HEAD_DIM)
    if rope_a is not None:
        q_rope = apply_axial_rope(q_rope, rope_a)
        kr = apply_axial_rope(kr, rope_a)
        qb = apply_axial_rope(qb, rope_b)
        kb = apply_axial_rope(kb, rope_b)
    qa = jnp.concatenate([q_nope, q_rope], -1)[:, :, :, None, :]
    ka = jnp.concatenate([k_nope, jnp.broadcast_to(kr, (bsz, t, MLA_HEADS, MLA_ROPE))], -1)
    qb = qb.reshape(bsz, t, GQA_KV_HEADS, GQA_HEADS // GQA_KV_HEADS, GQA_HEAD_DIM)
    return qa, ka, va, ga, qb, kb, vb, gb


def _attn_out(oa, ga, ob, gb, w_out):
    bsz, t = ga.shape[:2]
    mix = jnp.concatenate([oa.reshape(bsz, t, MLA_WIDTH) * jax.nn.silu(ga),
                           ob.reshape(bsz, t, GQA_WIDTH) * jax.nn.silu(gb)], -1)
    return mix @ w_out


def attention_mixer(h, h_ctx, w_in, w_out, q_norm_a, w_uq, kv_norm_a, w_ukv, q_norm_b, k_norm_b,
                    rope_a, rope_b, with_ctx):
    qa, ka, va, ga, qb, kb, vb, gb = _attn_project(h, w_in, q_norm_a, w_uq, kv_norm_a, w_ukv,
                                                   q_norm_b, k_norm_b, rope_a, rope_b)
    qa_c, ka_c, va_c, ga_c, qb_c, kb_c, vb_c, gb_c = _attn_project(h_ctx, w_in, q_norm_a, w_uq, kv_norm_a,
                                                                   w_ukv, q_norm_b, k_norm_b, None, None)
    oa = block_attention(qa, jnp.concatenate([ka_c, ka], 1), jnp.concatenate([va_c, va], 1))
    ob = block_attention(qb, jnp.concatenate([kb_c, kb], 1), jnp.concatenate([vb_c, vb], 1))
    y = _attn_out(oa, ga, ob, gb, w_out)
    y_ctx = None
    if with_ctx:
        y_ctx = _attn_out(block_attention(qa_c, ka_c, va_c), ga_c,
                          block_attention(qb_c, kb_c, vb_c), gb_c, w_out)
    return y, y_ctx


def dwconv_centered(x, w, b=None):
    ksz, ch = w.shape
    y = lax.conv_general_dilated(x, w[:, None, :].astype(x.dtype), window_strides=(1,),
                                 padding=[((ksz - 1) // 2, ksz // 2)],
                                 dimension_numbers=('NWC', 'WIO', 'NWC'), feature_group_count=ch)
    return y if b is None else y + b


def chunk_gated_delta(q, k, v, g, beta, s0):
    bsz, t, nh, dk = q.shape
    dv = v.shape[-1]
    csz = GDN_CHUNK
    nc = t // csz

    def chunks(a):
        a = a.reshape((bsz, nc, csz, nh) + a.shape[3:])
        return jnp.moveaxis(jnp.moveaxis(a, 1, 0), 2, 3)

    q = chunks(q) * (dk ** -0.5)
    k = chunks(k)
    v = chunks(v)
    beta = chunks(beta)
    g_cum = jnp.cumsum(chunks(g), axis=-1)
    idx = jnp.arange(csz)
    causal = idx[:, None] >= idx[None, :]
    strict = idx[:, None] > idx[None, :]
    decay = jnp.exp(jnp.where(causal, g_cum[..., :, None] - g_cum[..., None, :], -jnp.inf))
    kb = k * beta[..., None]
    m = jnp.where(strict, jnp.einsum('nbhid,nbhjd->nbhij', kb, k) * decay, 0.0)
    a_mat = m + jnp.eye(csz, dtype=m.dtype)
    rhs = jnp.concatenate([v * beta[..., None], kb * jnp.exp(g_cum)[..., None]], -1)
    sol = lax.linalg.triangular_solve(a_mat, rhs, left_side=True, lower=True, unit_diagonal=True)
    u, w = sol[..., :dv], sol[..., dv:]
    attn = jnp.einsum('nbhid,nbhjd->nbhij', q, k) * decay
    q_dec = q * jnp.exp(g_cum)[..., None]
    k_dec = k * jnp.exp(g_cum[..., -1:] - g_cum)[..., None]
    g_last = jnp.exp(g_cum[..., -1])

    def step(s, xs):
        qd, kd, u_n, w_n, at, gl = xs
        v_new = u_n - jnp.einsum('bhck,bhkv->bhcv', w_n, s)
        o = jnp.einsum('bhck,bhkv->bhcv', qd, s) + jnp.einsum('bhij,bhjv->bhiv', at, v_new)
        s = s * gl[..., None, None] + jnp.einsum('bhck,bhcv->bhkv', kd, v_new)
        return s, o

    s_fin, o = lax.scan(step, s0, (q_dec, k_dec, u, w, attn, g_last))
    o = jnp.transpose(o, (1, 0, 3, 2, 4)).reshape(bsz, t, nh, dv)
    return o, s_fin


def gdn_scan(q, k, v, g, beta, s0, reverse):
    if reverse:
        q, k, v, g, beta = (jnp.flip(a, 1) for a in (q, k, v, g, beta))
    o, s_fin = chunk_gated_delta(q, k, v, g, beta, s0)
    if reverse:
        o = jnp.flip(o, 1)
    return o, s_fin


def linear_scan(a, b, h0):
    def combine(l, r):
        return l[0] * r[0], r[0] * l[1] + r[1]
    a_cum, b_cum = lax.associative_scan(combine, (a, b), axis=1)
    return a_cum * h0[:, None, :] + b_cum


def lru_scan(a, b, h0, reverse):
    if reverse:
        a, b = jnp.flip(a, 1), jnp.flip(b, 1)
    h = linear_scan(a, b, h0)
    h_fin = h[:, -1]
    if reverse:
        h = jnp.flip(h, 1)
    return h, h_fin


def lru_coeffs(xl, w_gate, b_gate, lam):
    bsz, t, _ = xl.shape
    xb = xl.reshape(bsz, t, LRU_BLOCKS, LRU_BLOCK_W)
    gates = jnp.einsum('btnc,gncd->btgnd', xb, w_gate.astype(jnp.float32)).reshape(bsz, t, 2, LRU_WIDTH)
    gates = gates + b_gate.astype(jnp.float32)
    r = jax.nn.sigmoid(gates[:, :, 0])
    i = jax.nn.sigmoid(gates[:, :, 1])
    log_a = LRU_C * r * jax.nn.log_sigmoid(lam.astype(jnp.float32))
    a = jnp.exp(log_a)
    b = jnp.sqrt(-jnp.expm1(2.0 * log_a)) * (i * xl)
    return a, b


def _rec_inputs(h, w_in, gdn_conv_w, gdn_a_log, gdn_dt_bias, lru_conv_w, lru_conv_b):
    bsz, t, _ = h.shape
    qkv, z, b, a, xr, gr = _split(h @ w_in, REC_SPLITS)
    qkv = jax.nn.silu(dwconv_centered(qkv, gdn_conv_w)).astype(jnp.float32)
    q, k, v = (u.reshape(bsz, t, GDN_HEADS, GDN_HEAD_DIM) for u in jnp.split(qkv, 3, axis=-1))
    q, k = l2_norm(q), l2_norm(k)
    beta = jax.nn.sigmoid(b.astype(jnp.float32)).reshape(bsz, t, 2, GDN_HEADS)
    g = -jnp.exp(gdn_a_log.astype(jnp.float32)) * jax.nn.softplus(
        a.astype(jnp.float32).reshape(bsz, t, 2, GDN_HEADS) + gdn_dt_bias.astype(jnp.float32))
    xl = dwconv_centered(xr, lru_conv_w, lru_conv_b).astype(jnp.float32)
    return q, k, v, beta, g, z, xl, gr


def _rec_out(o, z, r, gr, gdn_norm, w_out, dtype):
    bsz, t = z.shape[:2]
    og = rms_norm(o, gdn_norm) * jax.nn.silu(z.astype(jnp.float32)).reshape(bsz, t, GDN_HEADS, GDN_HEAD_DIM)
    y_lru = r * jax.nn.silu(gr.astype(jnp.float32))
    mix = jnp.concatenate([og.reshape(bsz, t, GDN_WIDTH), y_lru], -1).astype(dtype)
    return mix @ w_out


def recurrent_mixer(h, h_ctx, w_in, w_out, gdn_conv_w, gdn_a_log, gdn_dt_bias, gdn_norm,
                    lru_conv_w, lru_conv_b, lru_gate_w, lru_gate_b, lru_lambda, with_ctx):
    q_l, k_l, v_l, beta_l, g_l, z_l, xl_l, gr_l = _rec_inputs(h, w_in, gdn_conv_w, gdn_a_log, gdn_dt_bias,
                                                              lru_conv_w, lru_conv_b)
    q_c, k_c, v_c, beta_c, g_c, z_c, xl_c, gr_c = _rec_inputs(h_ctx, w_in, gdn_conv_w, gdn_a_log, gdn_dt_bias,
                                                              lru_conv_w, lru_conv_b)
    bsz = h.shape[0]
    s0 = jnp.zeros((bsz, GDN_HEADS, GDN_HEAD_DIM, GDN_HEAD_DIM), jnp.float32)
    h0 = jnp.zeros((bsz, LRU_WIDTH), jnp.float32)
    o_l = o_c = r_l = r_c = 0.0
    for d in range(2):
        rev = d == 1
        oc, s_ctx = gdn_scan(q_c, k_c, v_c, g_c[:, :, d], beta_c[:, :, d], s0, rev)
        ol, _ = gdn_scan(q_l, k_l, v_l, g_l[:, :, d], beta_l[:, :, d], s_ctx, rev)
        a_c, b_c = lru_coeffs(xl_c, lru_gate_w[d], lru_gate_b[d], lru_lambda[d])
        hc, h_ctx_fin = lru_scan(a_c, b_c, h0, rev)
        a_l, b_l = lru_coeffs(xl_l, lru_gate_w[d], lru_gate_b[d], lru_lambda[d])
        hl, _ = lru_scan(a_l, b_l, h_ctx_fin, rev)
        o_l, o_c, r_l, r_c = o_l + ol, o_c + oc, r_l + hl, r_c + hc
    y = _rec_out(o_l, z_l, r_l, gr_l, gdn_norm, w_out, h.dtype)
    y_ctx = _rec_out(o_c, z_c, r_c, gr_c, gdn_norm, w_out, h.dtype) if with_ctx else None
    return y, y_ctx


def setup_inputs(seed: int = 0) -> dict:
    key = jax.random.key(seed)
    keys = iter(jax.random.split(key, 40))
    f32 = jnp.float32

    def nrm(shape, scale):
        return jax.random.normal(next(keys), shape, f32) * scale

    def unif(shape, lo, hi):
        return jax.random.uniform(next(keys), shape, f32, lo, hi)

    d = D_MODEL
    x = nrm((BATCH, SEQ, d), 1.0)
    c = nrm((BATCH, d), 1.0)
    ctx = nrm((BATCH, CTX_LEN, d), 1.0)
    c_ctx = nrm((d,), 1.0)
    mod_w = nrm((DEPTH, d, 3 * d), d ** -0.5)
    mod_b = nrm((DEPTH, 3 * d), 0.02)
    ln_g = 1.0 + nrm((DEPTH, d), 0.02)
    ln_b = nrm((DEPTH, d), 0.02)
    att_w_in = nrm((N_ATT, d, ATT_IN), d ** -0.5)
    att_w_out = nrm((N_ATT, ATT_MIX, d), ATT_MIX ** -0.5 * DEEPNORM_BETA)
    mla_q_norm = 1.0 + nrm((N_ATT, MLA_Q_RANK), 0.02)
    mla_w_uq = nrm((N_ATT, MLA_Q_RANK, MLA_HEADS * (MLA_NOPE + MLA_ROPE)), MLA_Q_RANK ** -0.5)
    mla_kv_norm = 1.0 + nrm((N_ATT, MLA_KV_RANK), 0.02)
    mla_w_ukv = nrm((N_ATT, MLA_KV_RANK, MLA_HEADS * (MLA_NOPE + MLA_V)), MLA_KV_RANK ** -0.5)
    gqa_q_norm = 1.0 + nrm((N_ATT, GQA_HEAD_DIM), 0.02)
    gqa_k_norm = 1.0 + nrm((N_ATT, GQA_HEAD_DIM), 0.02)
    rec_w_in = nrm((N_REC, d, REC_IN), d ** -0.5)
    rec_w_out = nrm((N_REC, REC_MIX, d), REC_MIX ** -0.5 * DEEPNORM_BETA)
    gdn_conv_w = nrm((N_REC, GDN_CONV, 3 * GDN_WIDTH), GDN_CONV ** -0.5)
    gdn_a_log = jnp.log(unif((N_REC, 2, GDN_HEADS), 1.0, 16.0))
    dt = jnp.exp(unif((N_REC, 2, GDN_HEADS), float(np.log(1e-3)), float(np.log(1e-1))))
    gdn_dt_bias = dt + jnp.log(-jnp.expm1(-dt))
    gdn_norm = 1.0 + nrm((N_REC, GDN_HEAD_DIM), 0.02)
    lru_conv_w = nrm((N_REC, LRU_CONV, LRU_WIDTH), LRU_CONV ** -0.5)
    lru_conv_b = nrm((N_REC, LRU_WIDTH), 0.02)
    lru_gate_w = nrm((N_REC, 2, 2, LRU_BLOCKS, LRU_BLOCK_W, LRU_BLOCK_W), LRU_BLOCK_W ** -0.5)
    lru_gate_b = nrm((N_REC, 2, 2, LRU_WIDTH), 0.02)
    a_c = unif((N_REC, 2, LRU_WIDTH), 0.9, 0.999) ** (1.0 / LRU_C)
    lru_lambda = jnp.log(a_c) - jnp.log1p(-a_c)
    return {'x': x, 'c': c, 'ctx': ctx, 'c_ctx': c_ctx, 'mod_w': mod_w, 'mod_b': mod_b,
            'ln_g': ln_g, 'ln_b': ln_b, 'att_w_in': att_w_in, 'att_w_out': att_w_out,
            'mla_q_norm': mla_q_norm, 'mla_w_uq': mla_w_uq, 'mla_kv_norm': mla_kv_norm,
            'mla_w_ukv': mla_w_ukv, 'gqa_q_norm': gqa_q_norm, 'gqa_k_norm': gqa_k_norm,
            'rec_w_in': rec_w_in, 'rec_w_out': rec_w_out, 'gdn_conv_w': gdn_conv_w,
            'gdn_a_log': gdn_a_log, 'gdn_dt_bias': gdn_dt_bias, 'gdn_norm': gdn_norm,
            'lru_conv_w': lru_conv_w, 'lru_conv_b': lru_conv_b, 'lru_gate_w': lru_gate_w,
            'lru_gate_b': lru_gate_b, 'lru_lambda': lru_lambda}


def reference(x, c, ctx, c_ctx, mod_w, mod_b, ln_g, ln_b, att_w_in, att_w_out, mla_q_norm, mla_w_uq,
              mla_kv_norm, mla_w_ukv, gqa_q_norm, gqa_k_norm, rec_w_in, rec_w_out, gdn_conv_w,
              gdn_a_log, gdn_dt_bias, gdn_norm, lru_conv_w, lru_conv_b, lru_gate_w, lru_gate_b, lru_lambda):
    n = x.shape[1]
    rows = n // GRID_W
    row = jnp.repeat(jnp.arange(rows), GRID_W)
    col = jnp.tile(jnp.arange(GRID_W), rows)
    rope_mla = axial_rope_tables(row, col, MLA_ROPE)
    rope_gqa = axial_rope_tables(row, col, GQA_HEAD_DIM)
    sc = jax.nn.silu(c)
    sc_ctx = jax.nn.silu(c_ctx)
    for layer in range(DEPTH):
        last = layer == DEPTH - 1
        shift, scale, gate = jnp.split(sc @ mod_w[layer] + mod_b[layer], 3, axis=-1)
        shift_c, scale_c, gate_c = jnp.split(sc_ctx @ mod_w[layer] + mod_b[layer], 3, axis=-1)
        h = x * (1.0 + scale[:, None]) + shift[:, None]
        h_ctx = ctx * (1.0 + scale_c) + shift_c
        li = layer // 2
        if layer % 2 == 0:
            y, y_ctx = attention_mixer(h, h_ctx, att_w_in[li], att_w_out[li], mla_q_norm[li], mla_w_uq[li],
                                       mla_kv_norm[li], mla_w_ukv[li], gqa_q_norm[li], gqa_k_norm[li],
                                       rope_mla, rope_gqa, not last)
        else:
            y, y_ctx = recurrent_mixer(h, h_ctx, rec_w_in[li], rec_w_out[li], gdn_conv_w[li], gdn_a_log[li],
                                       gdn_dt_bias[li], gdn_norm[li], lru_conv_w[li], lru_conv_b[li],
                                       lru_gate_w[li], lru_gate_b[li], lru_lambda[li], not last)
        x = layer_norm(DEEPNORM_ALPHA * x + gate[:, None] * y, ln_g[layer], ln_b[layer])
        if not last:
            ctx = layer_norm(DEEPNORM_ALPHA * ctx + gate_c * y_ctx, ln_g[layer], ln_b[layer])
    return x
```

```python
import numpy as np
from contextlib import ExitStack
import concourse.bass as bass
import concourse.mybir as mybir
from concourse.bass_utils import run_bass_kernel_spmd

F32 = mybir.dt.float32
BF16 = mybir.dt.bfloat16
AF = mybir.ActivationFunctionType
ALU = mybir.AluOpType

D = 1024
NB = 4
TL = 4096
TC = 256
TA = TL + TC
OWN = 2048
NOWN = OWN + TC
EPS = 1e-6
ALPHA = 8.0 ** 0.25
THETA = 10000.0


class Prog:
    def __init__(self, nc, stack, n_dma=12, same_engine_sync=True):
        self.nc = nc
        self.stack = stack
        self.eng = {'pe': nc.tensor, 'act': nc.scalar, 'dve': nc.vector, 'pool': nc.gpsimd, 'sp': nc.sync}
        self.sem = {}
        for e in ['pe', 'act', 'dve', 'pool']:
            self.sem[e] = stack.enter_context(nc.semaphore("s_" + e))
        self.cnt = {e: 0 for e in ['pe', 'act', 'dve', 'pool']}
        self.n_dma = n_dma
        self.dq = {}
        for q in ['sp', 'pool']:
            sems = [stack.enter_context(nc.semaphore("d_%s_%d" % (q, i))) for i in range(n_dma)]
            for i, s in enumerate(sems):
                self.sem[('d', q, i)] = s
            self.dq[q] = dict(targets=[0] * n_dma, next=0)
        self.waited = {e: {} for e in self.eng}
        self.lastw = {}
        self.readers = {}
        self.same = same_engine_sync
        self.n_inst = 0

    def _wait(self, E, tok):
        if tok is None:
            return
        key, val = tok
        if key == E and (E == 'pe' or not self.same):
            return
        if self.waited[E].get(key, 0) >= val:
            return
        self.eng[E].wait_ge(self.sem[key], val)
        self.waited[E][key] = val
        self.n_inst += 1

    def _deps(self, E, reads, writes):
        for b in reads:
            self._wait(E, self.lastw.get(b))
            if isinstance(b, str) and b.startswith('ps'):
                for t in self.readers.get(b, ()):
                    if t[0] != E:
                        self._wait(E, t)
        for b in writes:
            self._wait(E, self.lastw.get(b))
            for t in self.readers.get(b, ()):
                self._wait(E, t)

    def _record(self, tok, reads, writes):
        for b in reads:
            lst = self.readers.setdefault(b, [])
            lst[:] = [t for t in lst if t[0] != tok[0]]
            lst.append(tok)
        for b in writes:
            self.lastw[b] = tok
            self.readers[b] = []

    def op(self, E, name, *args, reads=(), writes=(), **kw):
        self._deps(E, reads, writes)
        inst = getattr(self.eng[E], name)(*args, **kw)
        self.cnt[E] += 1
        inst.then_inc(self.sem[E], 1)
        self.n_inst += 1
        self._record((E, self.cnt[E]), reads, writes)
        return inst

    def dma(self, q, out, in_, reads=(), writes=(), **kw):
        self._deps(q, reads, writes)
        d = self.dq[q]
        idx = d['next'] % self.n_dma
        d['next'] += 1
        key = ('d', q, idx)
        if d['targets'][idx] > 0:
            self._wait(q, (key, d['targets'][idx]))
        inst = self.eng[q].dma_start(out=out, in_=in_, **kw)
        d['targets'][idx] += 16
        inst.then_inc(self.sem[key], 16)
        self.n_inst += 1
        self._record((key, d['targets'][idx]), reads, writes)
        return inst

    def finish(self):
        for q in self.dq:
            d = self.dq[q]
            for idx in range(self.n_dma):
                if d['targets'][idx] > 0:
                    self._wait('sp', (('d', q, idx), d['targets'][idx]))
        for e in ['pe', 'act', 'dve', 'pool']:
            if self.cnt[e] > 0:
                self._wait('sp', (e, self.cnt[e]))


class Ring:
    def __init__(self, items):
        self.items = items
        self.i = 0

    def next(self):
        it = self.items[self.i % len(self.items)]
        self.i += 1
        return it


class Ctx:
    def __init__(self, nc, st):
        self.nc = nc
        self.st = st
        self.P = Prog(nc, st)
        self.psum = []
        for i in range(8):
            t = st.enter_context(nc.psum_tensor("ps%d" % i, [128, 512], F32))
            self.psum.append((t, "ps%d" % i))

    def sb(self, stack, name, shape, dt):
        self.uid = getattr(self, 'uid', 0) + 1
        return stack.enter_context(self.nc.sbuf_tensor("%s_%d" % (name, self.uid), shape, dt))

    SHARED = ('ident', 'cc')
    LAYER = ('modw', 'modb_col', 'modb_gate', 'lng', 'lnb')

    def _dram(self, name, shape, dt, kind):
        ov = getattr(self, 'override', {})
        if name in ov:
            return ov[name]
        if name in self.SHARED:
            full = name
        elif name in self.LAYER:
            full = name + getattr(self, 'lsfx', getattr(self, 'sfx', ''))
        else:
            full = name + getattr(self, 'sfx', '')
        cache = self.__dict__.setdefault('decl', {})
        if full not in cache:
            cache[full] = self.nc.dram_tensor(full, list(shape), dt, kind=kind).ap()
        return cache[full]

    def dram_in(self, name, shape, dt=F32):
        return self._dram(name, shape, dt, "ExternalInput")

    def dram_out(self, name, shape, dt=F32):
        return self._dram(name, shape, dt, "ExternalOutput")

    def mm(self, out, lhsT, rhs, start, stop, reads, writes):
        return self.P.op('pe', 'matmul', out, lhsT, rhs, start=start, stop=stop, reads=reads, writes=writes)

    def act(self, out, in_, func, reads, writes, **kw):
        return self.P.op('act', 'activation', out=out, in_=in_, func=func, reads=reads, writes=writes, **kw)

    def tt(self, E, out, in0, in1, op, reads, writes):
        return self.P.op(E, 'tensor_tensor', out=out, in0=in0, in1=in1, op=op, reads=reads, writes=writes)

    def stt(self, out, in0, scalar, in1, op0, op1, reads, writes):
        return self.P.op('dve', 'scalar_tensor_tensor', out=out, in0=in0, scalar=scalar, in1=in1,
                         op0=op0, op1=op1, reads=reads, writes=writes)

    def ts(self, E, out, in0, s1, s2, op0, op1, reads, writes):
        return self.P.op(E, 'tensor_scalar', out=out, in0=in0, scalar1=s1, scalar2=s2, op0=op0, op1=op1,
                         reads=reads, writes=writes)

    def copy(self, E, out, in_, reads, writes):
        return self.P.op(E, 'tensor_copy', out=out, in_=in_, reads=reads, writes=writes)

    def recip(self, out, in_, reads, writes):
        return self.P.op('dve', 'reciprocal', out=out, in_=in_, reads=reads, writes=writes)


def _rope_tabs(pos_row, pos_col, dim):
    half = dim // 2
    q = half // 2
    inv = THETA ** (-np.arange(0, half, 2, dtype=np.float32) / np.float32(half))
    inv = inv.astype(np.float32)
    ang_r = pos_row.astype(np.float32)[:, None] * inv
    ang_c = pos_col.astype(np.float32)[:, None] * inv
    cr, sr, cc_, sc_ = np.cos(ang_r), np.sin(ang_r), np.cos(ang_c), np.sin(ang_c)
    T = pos_row.shape[0]
    cos = np.zeros((dim, T), np.float32)
    sin = np.zeros((dim, T), np.float32)
    partner = np.zeros(dim, np.int64)
    for d in range(dim):
        grp, i = d // q, d % q
        c, s = (cr, sr) if grp < 2 else (cc_, sc_)
        cos[d] = c[:, i]
        if grp % 2 == 0:
            sin[d] = -s[:, i]
            partner[d] = d + q
        else:
            sin[d] = s[:, i]
            partner[d] = d - q
    return cos, sin, partner


def _att_tables():
    t = np.arange(TL)
    row, col = t // 64, t % 64
    cg, sg, pg = _rope_tabs(row, col, 64)
    cm, sm, pm = _rope_tabs(row, col, 32)
    return cg, sg, pg, cm, sm, pm


def prep_att_layer(inp, l):
    li = l // 2
    cg, sg, pg, cm, sm, pm = _att_tables()
    w_in = np.asarray(inp['att_w_in'][li], np.float32)
    o = np.cumsum([0, 256, 128, 32, 512, 512, 128, 128, 512])
    cq, ckv, kr, ga, qb, kb, vb, gb = [w_in[:, o[i]:o[i + 1]] for i in range(8)]
    hperm = np.concatenate([np.concatenate([np.arange(64) + 64 * c, np.arange(64) + 64 * (4 + c)]) for c in range(4)])
    sw64 = np.concatenate([pg + 64 * h for h in range(8)])
    qb_sw = qb[:, sw64]
    wq = np.concatenate([cq, ga, gb[:, hperm], qb[:, hperm], qb_sw[:, hperm]], axis=1)
    kb_sw = kb[:, np.concatenate([pg, pg + 64])]
    z64 = np.zeros((D, 64), np.float32)
    wkv = np.concatenate([ckv, kb, kb_sw, z64, kr, z64, kr[:, pm], vb], axis=1)
    w_uq = np.asarray(inp['mla_w_uq'][li], np.float32).reshape(256, 8, 96)
    wuqA = w_uq.reshape(256, 768)
    wuqB = np.concatenate([w_uq[:, :, :64], w_uq[:, :, 64:][:, :, pm]], axis=2).reshape(256, 768)
    w_ukv = np.asarray(inp['mla_w_ukv'][li], np.float32).reshape(128, 8, 128)
    wuk = w_ukv[:, :, :64].reshape(128, 512)
    wuv = w_ukv[:, :, 64:].reshape(128, 512)
    w_out = np.asarray(inp['att_w_out'][li], np.float32)
    wout = np.concatenate([w_out[:512], w_out[512:][hperm]], axis=0)
    gq = np.asarray(inp['gqa_q_norm'][li], np.float32)
    gk = np.asarray(inp['gqa_k_norm'][li], np.float32)
    gcols = np.zeros((128, 8), np.float32)
    gcols[:, 0:2] = np.asarray(inp['mla_q_norm'][li], np.float32).reshape(2, 128).T
    gcols[:, 2] = np.asarray(inp['mla_kv_norm'][li], np.float32)
    gcols[:, 3] = np.tile(gq, 2)
    gcols[:, 4] = np.tile(gq[pg], 2)
    gcols[:, 5] = np.tile(gk, 2)
    gcols[:, 6] = np.tile(gk[pg], 2)
    tabKg = np.zeros((2, 128, TA), np.float32)
    tabKg[0, :, :TC] = 1.0
    tabKg[0, :, TC:] = np.tile(cg, (2, 1))
    tabKg[1, :, TC:] = np.tile(sg, (2, 1))
    tabKm = np.zeros((2, 96, TA), np.float32)
    tabKm[0, 64:, :TC] = 1.0
    tabKm[0, 64:, TC:] = cm
    tabKm[1, 64:, TC:] = sm
    sh = dict(wq=wq, wkv=wkv, wuqA=wuqA, wuqB=wuqB, wuk=wuk, wuv=wuv, wout=wout, gcols=gcols)
    sh.update(_common_layer(inp, l))
    return sh, (tabKg, tabKm)


def _common_layer(inp, l):
    mod_b = np.asarray(inp['mod_b'][l], np.float32)
    return dict(
        modw=np.asarray(inp['mod_w'][l], np.float32),
        modb_col=np.ascontiguousarray(mod_b.reshape(24, 128).T),
        modb_gate=np.ascontiguousarray(mod_b[None, 2048:]),
        lng=np.asarray(inp['ln_g'][l], np.float32)[None, :],
        lnb=np.asarray(inp['ln_b'][l], np.float32)[None, :],
        ident=np.eye(128, dtype=np.float32),
    )


def _cc_cols(c_b, c_ctx):
    cc = np.zeros((128, 16), np.float32)
    cc[:, 0::2] = np.asarray(c_b, np.float32).reshape(8, 128).T
    cc[:, 1::2] = np.asarray(c_ctx, np.float32).reshape(8, 128).T
    return cc


def emit_consts(C, st, ident_d):
    P = C.P
    k = {}
    k['ident'] = C.sb(st, "ident", [128, 128], F32)
    P.dma('sp', k['ident'][:], ident_d, writes=['ident'])
    k['ones_f'] = C.sb(st, "ones_f", [128, 128], F32)
    P.op('dve', 'memset', k['ones_f'][:], 1.0, writes=['ones_f'])
    k['ones_bf'] = C.sb(st, "ones_bf", [128, 128], BF16)
    P.op('dve', 'memset', k['ones_bf'][:], 1.0, writes=['ones_bf'])
    k['onesbd'] = C.sb(st, "onesbd", [128, 128], BF16)
    P.op('dve', 'memset', k['onesbd'][:], 0.0, writes=['onesbd'])
    P.op('dve', 'memset', k['onesbd'][0:64, 0:64], 1.0, reads=['onesbd'], writes=['onesbd'])
    P.op('dve', 'memset', k['onesbd'][64:128, 64:128], 1.0, reads=['onesbd'], writes=['onesbd'])
    return k


def emit_modulation(C, st, K, cc_d, modw_d, modbcol_d, modbgate_d, need_cols=True, need_gate=True):
    P = C.P
    modcol = C.sb(st, "modcol", [128, 16, 2], F32)
    gate_bc = C.sb(st, "gate_bc", [128, 2, 1024], F32) if need_gate else None
    with ExitStack() as s2:
        cc = C.sb(s2, "cc", [128, 16], F32)
        sc = C.sb(s2, "sc", [128, 16], F32)
        mbc = C.sb(s2, "mbc", [128, 24], F32)
        mbg = C.sb(s2, "mbg", [128, 1024], F32)
        rep = C.sb(s2, "rep", [128, 2, 8, 128], F32)
        mw = C.sb(s2, "mw", [128, 8, 1024], F32)
        P.dma('sp', cc[:], cc_d, writes=['cc'])
        P.dma('sp', mbc[:], modbcol_d, writes=['mbc'])
        P.dma('sp', mbg[:], modbgate_d[0:1, :].broadcast_to([128, 1024]), writes=['mbg'])
        C.act(sc[:], cc[:], AF.Silu, reads=['cc'], writes=['sc'])
        for j in range(2):
            for kc in range(8):
                C.ts('dve', rep[:, j, kc, :], K['ones_f'][:], sc[:, 2 * kc + j:2 * kc + j + 1], None, ALU.mult,
                     ALU.bypass, reads=['ones_f', 'sc'], writes=['rep'])
        for t in range(3):
            if (t < 2 and not need_cols) or (t == 2 and not need_gate):
                continue
            for kc in range(8):
                P.dma('sp', mw[:, kc, :], modw_d[kc * 128:(kc + 1) * 128, t * 1024:(t + 1) * 1024], writes=['mw'])
            if t < 2:
                ps, pk = C.psum[t]
                for oc in range(8):
                    for kc in range(8):
                        C.mm(ps[:, 2 * oc:2 * oc + 2], mw[:, kc, oc * 128:(oc + 1) * 128], sc[:, 2 * kc:2 * kc + 2],
                             kc == 0, kc == 7, reads=['mw', 'sc'], writes=[pk])
                    C.ts('dve', modcol[:, t * 8 + oc, :], ps[:, 2 * oc:2 * oc + 2],
                         mbc[:, t * 8 + oc:t * 8 + oc + 1], float(t), ALU.add, ALU.add,
                         reads=[pk, 'mbc'], writes=['modcol'])
            else:
                for j in range(2):
                    for half in range(2):
                        ps, pk = C.psum[2 + 2 * j + half]
                        for kc in range(8):
                            C.mm(ps[:, :], rep[:, j, kc, :], mw[:, kc, half * 512:(half + 1) * 512], kc == 0, kc == 7,
                                 reads=['mw', 'rep'], writes=[pk])
                        C.tt('dve', gate_bc[:, j, half * 512:(half + 1) * 512], ps[:, :],
                             mbg[:, half * 512:(half + 1) * 512], ALU.add, reads=[pk, 'mbg'], writes=['gate_bc'])
        _drain(C)
    return modcol, gate_bc


def _drain(C):
    P = C.P
    toks = [(e, P.cnt[e]) for e in ['pe', 'act', 'dve', 'pool'] if P.cnt[e] > 0]
    dtoks = []
    for q in P.dq:
        d = P.dq[q]
        for idx in range(P.n_dma):
            if d['targets'][idx] > 0:
                dtoks.append((('d', q, idx), d['targets'][idx]))
    for E in ['pe', 'act', 'dve', 'pool', 'sp']:
        for t in toks + dtoks:
            P._wait(E, t)


def emit_hT(C, K, xs, xs_key, x_rows, ntok, j, modcol, hT, hT_key, ring, dst=None):
    P = C.P
    nt = ntok // 128
    for tt in range(nt):
        P.dma('sp', xs[:, tt, :], x_rows[tt * 128:(tt + 1) * 128, :], writes=[(xs_key, tt)])
    for kc in range(8):
        ps, pk = ring.next()
        for tt in range(nt):
            P.op('pe', 'transpose', ps[:, tt * 128:(tt + 1) * 128], xs[:, tt, kc * 128:(kc + 1) * 128], K['ident'][:],
                 reads=[(xs_key, tt), 'ident'], writes=[pk])
        C.act(hT[:, kc, 0:ntok] if dst is None else dst(kc), ps[:, 0:ntok], AF.Identity, reads=[pk, 'modcol'], writes=[hT_key],
              scale=modcol[:, 8 + kc, j:j + 1], bias=modcol[:, kc, j:j + 1])


def emit_epilogue_tile(C, E, mixT_tile_fn, wout, x_rows, out_rows, j, gate_bc, lnbc, ring, tagi):
    P = C.P
    xt, tb, st6, mv, sm = E['xt'][tagi % 2], E['tb'][tagi % 2], E['st6'][tagi % 2], E['mv'][tagi % 2], E['sm'][tagi % 2]
    kx, kt, ks = ('xt', tagi % 2), ('tb', tagi % 2), ('sm', tagi % 2)
    P.dma('sp', xt[:], x_rows, writes=[kx])
    for half in range(2):
        ps, pk = ring.next()
        for c in range(8):
            lhsT, rk = mixT_tile_fn(c)
            C.mm(ps[:, :], lhsT, wout[:, c, half * 512:(half + 1) * 512], c == 0, c == 7, reads=rk + ['wout'], writes=[pk])
        C.tt('dve', tb[:, half * 512:(half + 1) * 512], ps[:, :], gate_bc[:, j, half * 512:(half + 1) * 512], ALU.mult,
             reads=[pk, 'gate_bc'], writes=[kt])
    C.stt(tb[:], xt[:], float(ALPHA), tb[:], ALU.mult, ALU.add, reads=[kx, kt], writes=[kt])
    for half in range(2):
        P.op('dve', 'bn_stats', out=st6[:, half, :], in_=tb[:, half * 512:(half + 1) * 512], reads=[kt], writes=[ks])
    P.op('dve', 'bn_aggr', out=mv[:], in_=st6[:].rearrange("p a b -> p (a b)"), reads=[ks], writes=[ks])
    C.act(sm[:, 0:1], mv[:, 1:2], AF.Sqrt, reads=[ks], writes=[ks], bias=float(EPS), scale=1.0)
    C.recip(sm[:, 1:2], sm[:, 0:1], reads=[ks], writes=[ks])
    C.ts('dve', sm[:, 2:3], mv[:, 0:1], sm[:, 1:2], -1.0, ALU.mult, ALU.mult, reads=[ks], writes=[ks])
    C.act(tb[:], tb[:], AF.Identity, reads=[kt, ks], writes=[kt], scale=sm[:, 1:2], bias=sm[:, 2:3])
    C.tt('pool', tb[:], tb[:], lnbc[:, 0, :], ALU.mult, reads=[kt, 'lnbc'], writes=[kt])
    C.tt('pool', tb[:], tb[:], lnbc[:, 1, :], ALU.add, reads=[kt, 'lnbc'], writes=[kt])
    P.dma('sp', out_rows, tb[:], reads=[kt])


def alloc_epilogue(C, st):
    E = dict(xt=[], tb=[], st6=[], mv=[], sm=[])
    for i in range(2):
        E['xt'].append(C.sb(st, "e_xt%d" % i, [128, 1024], F32))
        E['tb'].append(C.sb(st, "e_tb%d" % i, [128, 1024], F32))
        E['st6'].append(C.sb(st, "e_st%d" % i, [128, 2, 6], F32))
        E['mv'].append(C.sb(st, "e_mv%d" % i, [128, 2], F32))
        E['sm'].append(C.sb(st, "e_sm%d" % i, [128, 4], F32))
    return E


def load_w_bf16(C, dst, dst_key, src, kcs, ncols):
    for kc in range(kcs):
        for c0 in range(0, ncols, 1024):
            c1 = min(ncols, c0 + 1024)
            C.P.dma('pool', dst[:, kc, c0:c1], src[kc * 128:(kc + 1) * 128, c0:c1], writes=[dst_key])


KV_BLOCKS = [(0, 256, 1)] + [(256 + 512 * i, 512, 0) for i in range(8)]
Q_PASSES = [[(0, 256, 1), (256, 512, 0), (768, 512, 0)], [(1280, 512, 0), (1792, 512, 0)]]
FULL_PASSES = [[(0, 256, 1), (256, 512, 0), (768, 512, 0)]] + [[(1280 + 1024 * i, 512, 0), (1792 + 1024 * i, 512, 0)] for i in range(3)]
NKT = TA // 128


def build_att():
    nc = bass.Bass("TRN2", target_bir_lowering=False)
    with ExitStack() as st:
        C = Ctx(nc, st)
        emit_att(C, st)
        C.P.finish()
    return nc


def emit_att(C, st, K=None, full=False):
    if True:
        P = C.P
        Q_PASSES_ = FULL_PASSES if full else Q_PASSES
        xall = C.dram_in("xall", [TA, D])
        xown = C.dram_in("xown", [NOWN, D])
        cc_d = C.dram_in("cc", [128, 16])
        modw_d = C.dram_in("modw", [D, 3 * D])
        modbcol_d = C.dram_in("modb_col", [128, 24])
        modbgate_d = C.dram_in("modb_gate", [1, D])
        lng_d = C.dram_in("lng", [1, D])
        lnb_d = C.dram_in("lnb", [1, D])
        wkv_d = C.dram_in("wkv", [D, 704])
        wq_d = C.dram_in("wq", [D, 2304])
        wuqA_d = C.dram_in("wuqA", [256, 768])
        wuqB_d = C.dram_in("wuqB", [256, 768])
        wuk_d = C.dram_in("wuk", [128, 512])
        wuv_d = C.dram_in("wuv", [128, 512])
        wout_d = C.dram_in("wout", [D, D])
        gcols_d = C.dram_in("gcols", [128, 8])
        tabKg_d = C.dram_in("tabKg", [2, 128, TA])
        tabKm_d = C.dram_in("tabKm", [2, 96, TA])
        tabQg_d = None if full else C.dram_in("tabQg", [2, 128, NOWN])
        tabQm_d = None if full else C.dram_in("tabQm", [2, 96, NOWN])
        ident_d = C.dram_in("ident", [128, 128])
        xo = C.dram_out("xo", [NOWN, D])
        if full:
            tabQg_d, tabQm_d = tabKg_d, tabKm_d

        K = K or emit_consts(C, st, ident_d)
        gcols = C.sb(st, "gcols", [128, 8], F32)
        P.dma('sp', gcols[:], gcols_d, writes=['gcols'])
        lnbc = C.sb(st, "lnbc", [128, 2, 1024], F32)
        P.dma('sp', lnbc[:, 0, :], lng_d[0:1, :].broadcast_to([128, D]), writes=['lnbc'])
        P.dma('sp', lnbc[:, 1, :], lnb_d[0:1, :].broadcast_to([128, D]), writes=['lnbc'])
        modcol, gate_bc = emit_modulation(C, st, K, cc_d, modw_d, modbcol_d, modbgate_d)

        ckvT = C.sb(st, "ckvT", [128, TA], BF16)
        krT = C.sb(st, "krT", [96, TA], BF16)
        kTg = C.sb(st, "kTg", [128, TA], BF16)
        Vg = C.sb(st, "Vg", [128, NKT, 192], BF16)
        P.op('pool', 'memset', Vg[:, :, 64:128], 1.0, writes=['Vg'])
        allps = Ring(C.psum)

        def rms_rope_chunk(s2, psA, pkA, psB, pkB, n, gA, gB, tab, tabk, dst, dst_key, tmp):
            sq, rt, t1, t2 = tmp
            C.act(sq[:, 0:n], psA[:, 0:n], AF.Square, reads=[pkA], writes=['sq'])
            ps2, pk2 = allps.next()
            C.mm(ps2[:, 0:n], K['onesbd'][:], sq[:, 0:n], True, True, reads=['onesbd', 'sq'], writes=[pk2])
            C.act(rt[:, 0:n], ps2[:, 0:n], AF.Sqrt, reads=[pk2], writes=['rt'], bias=float(EPS), scale=1.0 / 64)
            C.recip(rt[:, 0:n], rt[:, 0:n], reads=['rt'], writes=['rt'])
            C.stt(t1[:, 0:n], psA[:, 0:n], gA, tab[:, 0, 0:n], ALU.mult, ALU.mult, reads=[pkA, tabk, 'gcols'], writes=['t1'])
            C.stt(t2[:, 0:n], psB[:, 0:n], gB, tab[:, 1, 0:n], ALU.mult, ALU.mult, reads=[pkB, tabk, 'gcols'], writes=['t2'])
            C.tt('pool', t1[:, 0:n], t1[:, 0:n], t2[:, 0:n], ALU.add, reads=['t1', 't2'], writes=['t1'])
            C.tt('pool', dst, t1[:, 0:n], rt[:, 0:n], ALU.mult, reads=['t1', 'rt'], writes=[dst_key])

        with ExitStack() as s2:
            wkv = C.sb(s2, "wkv", [128, 8, 704], BF16)
            load_w_bf16(C, wkv, 'wkv', wkv_d, 8, 704)
            xs = C.sb(s2, "xs", [128, 4, 1024], F32)
            hT = [C.sb(s2, "hT%d" % i, [128, 8, 512], BF16) for i in range(2)]
            tabg = [C.sb(s2, "tabg%d" % i, [128, 2, 512], F32) for i in range(2)]
            tabm = [C.sb(s2, "tabm%d" % i, [96, 2, 512], F32) for i in range(2)]
            tmp = (C.sb(s2, "sq", [128, 512], BF16), C.sb(s2, "rt", [128, 512], F32),
                   C.sb(s2, "t1", [128, 512], F32), C.sb(s2, "t2", [128, 512], F32))
            for bi, (r0, n, j) in enumerate(KV_BLOCKS):
                h, hk = hT[bi % 2], ('hT', bi % 2)
                tg, tgk = tabg[bi % 2], ('tabg', bi % 2)
                tm, tmk = tabm[bi % 2], ('tabm', bi % 2)
                for a in range(2):
                    P.dma('sp', tg[:, a, 0:n], tabKg_d[a, :, r0:r0 + n], writes=[tgk])
                    P.dma('sp', tm[64:96, a, 0:n], tabKm_d[a, 64:96, r0:r0 + n], writes=[tmk])
                emit_hT(C, K, xs, 'xs', xall[r0:r0 + n, :], n, j, modcol, h, hk, allps)

                def proj(off, M):
                    ps, pk = allps.next()
                    for kc in range(8):
                        C.mm(ps[0:M, 0:n], wkv[:, kc, off:off + M], h[:, kc, 0:n], kc == 0, kc == 7,
                             reads=['wkv', hk], writes=[pk])
                    return ps, pk
                ps, pk = proj(0, 128)
                sq, rt, t1, t2 = tmp
                C.act(sq[:, 0:n], ps[:, 0:n], AF.Square, reads=[pk], writes=['sq'])
                ps2, pk2 = allps.next()
                C.mm(ps2[:, 0:n], K['ones_bf'][:], sq[:, 0:n], True, True, reads=['ones_bf', 'sq'], writes=[pk2])
                C.act(rt[:, 0:n], ps2[:, 0:n], AF.Sqrt, reads=[pk2], writes=['rt'], bias=float(EPS), scale=1.0 / 128)
                C.recip(rt[:, 0:n], rt[:, 0:n], reads=['rt'], writes=['rt'])
                C.stt(ckvT[:, r0:r0 + n], ps[:, 0:n], gcols[:, 2:3], rt[:, 0:n], ALU.mult, ALU.mult,
                      reads=[pk, 'rt', 'gcols'], writes=['ckvT'])
                psA, pkA = proj(128, 128)
                psB, pkB = proj(256, 128)
                rms_rope_chunk(s2, psA, pkA, psB, pkB, n, gcols[:, 5:6], gcols[:, 6:7], tg, tgk, kTg[:, r0:r0 + n], 'kTg', tmp)
                psA, pkA = proj(384, 96)
                psB, pkB = proj(480, 96)
                C.tt('dve', t1[64:96, 0:n], psA[64:96, 0:n], tm[64:96, 0, 0:n], ALU.mult, reads=[pkA, tmk], writes=['t1'])
                C.tt('dve', t2[64:96, 0:n], psB[64:96, 0:n], tm[64:96, 1, 0:n], ALU.mult, reads=[pkB, tmk], writes=['t2'])
                C.tt('pool', krT[64:96, r0:r0 + n], t1[64:96, 0:n], t2[64:96, 0:n], ALU.add, reads=['t1', 't2'], writes=['krT'])
                nt = n // 128
                ps, pk = allps.next()
                for tt in range(nt):
                    for kc in range(8):
                        C.mm(ps[:, tt * 128:(tt + 1) * 128], h[:, kc, tt * 128:(tt + 1) * 128], wkv[:, kc, 576:704],
                             kc == 0, kc == 7, reads=['wkv', hk], writes=[pk])
                t0 = r0 // 128
                for tt in range(nt):
                    C.copy('dve', Vg[:, t0 + tt, 0:64], ps[:, tt * 128:tt * 128 + 64], reads=[pk], writes=['Vg'])
                    C.copy('dve', Vg[:, t0 + tt, 128:192], ps[:, tt * 128 + 64:tt * 128 + 128], reads=[pk], writes=['Vg'])
            _drain(C)

        for pi, blocks in enumerate(Q_PASSES_):
            base = blocks[0][0]
            npass = sum(b[1] for b in blocks)
            with ExitStack() as sp:
                cqT = C.sb(sp, "cqT", [128, 2, 1280], BF16)
                qTg = C.sb(sp, "qTg", [128, 4, 1280], BF16)
                mixT = C.sb(sp, "mixT", [128, 8, 1280], BF16)
                with ExitStack() as s2:
                    wq = C.sb(s2, "wq", [128, 8, 2304], BF16)
                    load_w_bf16(C, wq, 'wq', wq_d, 8, 2304)
                    xs = C.sb(s2, "xs", [128, 4, 1024], F32)
                    hT = [C.sb(s2, "hT%d" % i, [128, 8, 512], BF16) for i in range(2)]
                    tabg = [C.sb(s2, "tabg%d" % i, [128, 2, 512], F32) for i in range(2)]
                    tmp = (C.sb(s2, "sq", [128, 512], BF16), C.sb(s2, "rt", [128, 512], F32),
                           C.sb(s2, "t1", [128, 512], F32), C.sb(s2, "t2", [128, 512], F32))
                    sq2 = C.sb(s2, "sq2", [128, 512], BF16)
                    for bi, (r0, n, j) in enumerate(blocks):
                        h, hk = hT[bi % 2], ('hT', bi % 2)
                        tg, tgk = tabg[bi % 2], ('tabg', bi % 2)
                        l0 = r0 - base
                        for a in range(2):
                            P.dma('sp', tg[:, a, 0:n], tabQg_d[a, :, r0:r0 + n], writes=[tgk])
                        emit_hT(C, K, xs, 'xs', xown[r0:r0 + n, :], n, j, modcol, h, hk, allps)

                        def proj(ci):
                            ps, pk = allps.next()
                            for kc in range(8):
                                C.mm(ps[:, 0:n], wq[:, kc, ci * 128:(ci + 1) * 128], h[:, kc, 0:n], kc == 0, kc == 7,
                                     reads=['wq', hk], writes=[pk])
                            return ps, pk
                        sq, rt, t1, t2 = tmp
                        ps0, pk0 = proj(0)
                        ps1, pk1 = proj(1)
                        C.act(sq[:, 0:n], ps0[:, 0:n], AF.Square, reads=[pk0], writes=['sq'])
                        C.act(sq2[:, 0:n], ps1[:, 0:n], AF.Square, reads=[pk1], writes=['sq2'])
                        ps2, pk2 = allps.next()
                        C.mm(ps2[:, 0:n], K['ones_bf'][:], sq[:, 0:n], True, False, reads=['ones_bf', 'sq'], writes=[pk2])
                        C.mm(ps2[:, 0:n], K['ones_bf'][:], sq2[:, 0:n], False, True, reads=['ones_bf', 'sq2'], writes=[pk2])
                        C.act(rt[:, 0:n], ps2[:, 0:n], AF.Sqrt, reads=[pk2], writes=['rt'], bias=float(EPS), scale=1.0 / 256)
                        C.recip(rt[:, 0:n], rt[:, 0:n], reads=['rt'], writes=['rt'])
                        C.stt(cqT[:, 0, l0:l0 + n], ps0[:, 0:n], gcols[:, 0:1], rt[:, 0:n], ALU.mult, ALU.mult,
                              reads=[pk0, 'rt', 'gcols'], writes=['cqT'])
                        C.stt(cqT[:, 1, l0:l0 + n], ps1[:, 0:n], gcols[:, 1:2], rt[:, 0:n], ALU.mult, ALU.mult,
                              reads=[pk1, 'rt', 'gcols'], writes=['cqT'])
                        for c in range(8):
                            ps, pk = proj(2 + c)
                            C.act(mixT[:, c, l0:l0 + n], ps[:, 0:n], AF.Silu, reads=[pk], writes=[('mixT', c)])
                        for c in range(4):
                            psA, pkA = proj(10 + c)
                            psB, pkB = proj(14 + c)
                            rms_rope_chunk(s2, psA, pkA, psB, pkB, n, gcols[:, 3:4], gcols[:, 4:5], tg, tgk,
                                           qTg[:, c, l0:l0 + n], ('qTg', c), tmp)
                    _drain(C)

                with ExitStack() as s2:
                    wuqA = C.sb(s2, "wuqA", [128, 2, 768], BF16)
                    wuqB = C.sb(s2, "wuqB", [128, 2, 768], BF16)
                    wuk = C.sb(s2, "wuk", [128, 1, 512], BF16)
                    wuv = C.sb(s2, "wuv", [128, 1, 512], BF16)
                    load_w_bf16(C, wuqA, 'wuqA', wuqA_d, 2, 768)
                    load_w_bf16(C, wuqB, 'wuqB', wuqB_d, 2, 768)
                    load_w_bf16(C, wuk, 'wuk', wuk_d, 1, 512)
                    load_w_bf16(C, wuv, 'wuv', wuv_d, 1, 512)
                    Kh = [C.sb(s2, "Kh%d" % i, [96, TA], BF16) for i in range(2)]
                    qh = [C.sb(s2, "qh%d" % i, [96, 1280], BF16) for i in range(2)]
                    Vp = [C.sb(s2, "Vp%d" % i, [128, NKT, 192], BF16) for i in range(2)]
                    for i in range(2):
                        P.op('pool', 'memset', Vp[i][:, :, 64:128], 1.0, writes=[('Vp', i)])
                    tabq = C.sb(s2, "tabq", [96, 2, 1280], F32)
                    for a in range(2):
                        P.dma('sp', tabq[64:96, a, 0:npass], tabQm_d[a, 64:96, base:base + npass], writes=['tabq'])
                    PT = [C.sb(s2, "PT%d" % i, [128, 512], BF16) for i in range(3)]
                    rden = [C.sb(s2, "rden%d" % i, [128, 512], F32) for i in range(2)]
                    on = [C.sb(s2, "on%d" % i, [128, 512], F32) for i in range(2)]
                    t1 = C.sb(s2, "a_t1", [96, 512], F32)
                    t2 = C.sb(s2, "a_t2", [96, 512], F32)
                    Sring = Ring(C.psum[0:3])
                    Oring = Ring(C.psum[3:5])
                    Mring = Ring(C.psum[5:8])
                    cnt = {'pt': 0, 'nrm': 0}

                    def attention(Kap, Kkeys, qbuf, qkeys, kdim, prow, scale, Vfn, Vkeys, nrow, drow, cm):
                        for (r0, n, j) in blocks:
                            l0 = r0 - base
                            nk = 2 if j == 1 else NKT
                            ops_, opk = Oring.next()
                            sl = [None] * nk
                            sl[0] = Sring.next()
                            C.mm(sl[0][0][:, 0:n], Kap(0), qbuf(l0, n), True, True, reads=Kkeys + qkeys, writes=[sl[0][1]])
                            for kt in range(nk):
                                if kt + 1 < nk:
                                    sl[kt + 1] = Sring.next()
                                    C.mm(sl[kt + 1][0][:, 0:n], Kap(kt + 1), qbuf(l0, n), True, True, reads=Kkeys + qkeys,
                                         writes=[sl[kt + 1][1]])
                                pi_ = cnt['pt'] % 3
                                cnt['pt'] += 1
                                C.act(PT[pi_][:, 0:n], sl[kt][0][:, 0:n], AF.Exp, reads=[sl[kt][1]], writes=[('PT', pi_)],
                                      scale=float(scale))
                                C.mm(ops_[:, 0:n], Vfn(kt), PT[pi_][:, 0:n], kt == 0, kt == nk - 1,
                                     reads=Vkeys + [('PT', pi_)], writes=[opk])
                            ni = cnt['nrm'] % 2
                            cnt['nrm'] += 1
                            C.recip(rden[ni][nrow, 0:n], ops_[drow, 0:n], reads=[opk], writes=[('rden', ni)])
                            C.tt('dve', on[ni][nrow, 0:n], ops_[nrow, 0:n], rden[ni][nrow, 0:n], ALU.mult,
                                 reads=[opk, ('rden', ni)], writes=[('on', ni)])
                            C.tt('pool', mixT[nrow, cm, l0:l0 + n], on[ni][nrow, 0:n], mixT[nrow, cm, l0:l0 + n], ALU.mult,
                                 reads=[('on', ni), ('mixT', cm)], writes=[('mixT', cm)])

                    hcount = 0
                    for jp in range(4):
                        vp, vk = Vp[jp % 2], ('Vp', jp % 2)
                        for t0 in range(0, NKT, 4):
                            nt = min(4, NKT - t0)
                            ps, pk = Mring.next()
                            for tt in range(nt):
                                C.mm(ps[:, tt * 128:(tt + 1) * 128], ckvT[:, (t0 + tt) * 128:(t0 + tt + 1) * 128],
                                     wuv[:, 0, jp * 128:(jp + 1) * 128], True, True, reads=['ckvT', 'wuv'], writes=[pk])
                            for tt in range(nt):
                                C.copy('dve', vp[:, t0 + tt, 0:64], ps[:, tt * 128:tt * 128 + 64], reads=[pk], writes=[vk])
                                C.copy('dve', vp[:, t0 + tt, 128:192], ps[:, tt * 128 + 64:tt * 128 + 128], reads=[pk], writes=[vk])
                        for hh in range(2):
                            hd = 2 * jp + hh
                            kh, kk = Kh[hcount % 2], ('Kh', hcount % 2)
                            q_, qk = qh[hcount % 2], ('qh', hcount % 2)
                            hcount += 1
                            for (r0, n, j) in KV_BLOCKS:
                                ps, pk = Mring.next()
                                C.mm(ps[0:64, 0:n], wuk[:, 0, hd * 64:(hd + 1) * 64], ckvT[:, r0:r0 + n], True, True,
                                     reads=['wuk', 'ckvT'], writes=[pk])
                                C.copy('dve', kh[0:64, r0:r0 + n], ps[0:64, 0:n], reads=[pk], writes=[kk])
                            C.copy('pool', kh[64:96, :], krT[64:96, :], reads=['krT'], writes=[kk])
                            for (r0, n, j) in blocks:
                                l0 = r0 - base
                                psA, pkA = Mring.next()
                                psB, pkB = Mring.next()
                                for i in range(2):
                                    C.mm(psA[0:96, 0:n], wuqA[:, i, hd * 96:(hd + 1) * 96], cqT[:, i, l0:l0 + n], i == 0, i == 1,
                                         reads=['wuqA', 'cqT'], writes=[pkA])
                                for i in range(2):
                                    C.mm(psB[0:96, 0:n], wuqB[:, i, hd * 96:(hd + 1) * 96], cqT[:, i, l0:l0 + n], i == 0, i == 1,
                                         reads=['wuqB', 'cqT'], writes=[pkB])
                                C.copy('dve', q_[0:64, l0:l0 + n], psA[0:64, 0:n], reads=[pkA], writes=[qk])
                                C.tt('dve', t1[64:96, 0:n], psA[64:96, 0:n], tabq[64:96, 0, l0:l0 + n], ALU.mult,
                                     reads=[pkA, 'tabq'], writes=['a_t1'])
                                C.tt('dve', t2[64:96, 0:n], psB[64:96, 0:n], tabq[64:96, 1, l0:l0 + n], ALU.mult,
                                     reads=[pkB, 'tabq'], writes=['a_t2'])
                                C.tt('pool', q_[64:96, l0:l0 + n], t1[64:96, 0:n], t2[64:96, 0:n], ALU.add,
                                     reads=['a_t1', 'a_t2'], writes=[qk])
                            nrow = slice(0, 64) if hh == 0 else slice(64, 128)
                            drow = slice(64, 128) if hh == 0 else slice(0, 64)
                            attention(lambda kt, kh=kh: kh[0:96, kt * 128:(kt + 1) * 128], [kk],
                                      lambda l0, n, q_=q_: q_[0:96, l0:l0 + n], [qk], 96, None, 96.0 ** -0.5,
                                      lambda kt, vp=vp, hh=hh: vp[:, kt, hh * 64:hh * 64 + 128], [vk], nrow, drow, jp)
                    for c in range(4):
                        for g in range(2):
                            rows = slice(64 * g, 64 * g + 64)
                            drow = slice(64, 128) if g == 0 else slice(0, 64)
                            attention(lambda kt, rows=rows: kTg[rows, kt * 128:(kt + 1) * 128], ['kTg'],
                                      lambda l0, n, rows=rows, c=c: qTg[rows, c, l0:l0 + n], [('qTg', c)], 64, None, 0.125,
                                      lambda kt, g=g: Vg[:, kt, g * 64:g * 64 + 128], ['Vg'], rows, drow, 4 + c)
                    _drain(C)

                with ExitStack() as s2:
                    wout = C.sb(s2, "wout", [128, 8, 1024], BF16)
                    load_w_bf16(C, wout, 'wout', wout_d, 8, 1024)
                    E = alloc_epilogue(C, s2)
                    ti = 0
                    for (r0, n, j) in blocks:
                        for tt in range(n // 128):
                            l0 = r0 - base + tt * 128
                            rr = r0 + tt * 128
                            emit_epilogue_tile(
                                C, E, lambda c, l0=l0: (mixT[:, c, l0:l0 + 128], [('mixT', c)]), wout,
                                xown[rr:rr + 128, :], xo[rr:rr + 128, :], j, gate_bc, lnbc, allps, ti)
                            ti += 1
                    _drain(C)


_CACHE = {}


def _get(name, fn):
    if name not in _CACHE:
        _CACHE[name] = fn()
    return _CACHE[name]


def run_att_layer(inp, l, x, ctx):
    sh, (tabKg, tabKm) = prep_att_layer(inp, l)
    nc = _get('att', build_att)
    in_maps = []
    for core in range(8):
        b, hf = core // 2, core % 2
        xall = np.concatenate([ctx[b], x[b]], axis=0)
        xown = np.concatenate([ctx[b], x[b, hf * OWN:(hf + 1) * OWN]], axis=0)
        sel = np.concatenate([np.arange(TC), TC + hf * OWN + np.arange(OWN)])
        m = dict(sh)
        m.update(xall=np.ascontiguousarray(xall), xown=np.ascontiguousarray(xown),
                 cc=_cc_cols(inp['c'][b], inp['c_ctx']), tabKg=tabKg, tabKm=tabKm,
                 tabQg=np.ascontiguousarray(tabKg[:, :, sel]), tabQm=np.ascontiguousarray(tabKm[:, :, sel]))
        in_maps.append(m)
    res = run_bass_kernel_spmd(nc, in_maps, core_ids=list(range(8)))
    xn = np.empty_like(x)
    cn = np.empty_like(ctx)
    for core in range(8):
        b, hf = core // 2, core % 2
        o = res.results[core]["xo"]
        xn[b, hf * OWN:(hf + 1) * OWN] = o[TC:]
        if hf == 0:
            cn[b] = o[:TC]
    return xn, cn


def _rec_consts():
    idx = np.arange(128)
    same = (idx[:, None] // 64) == (idx[None, :] // 64)
    cm = np.zeros((10, 128, 128), np.float32)
    cm[0] = same & (idx[:, None] <= idx[None, :])
    cm[1] = same & (idx[:, None] >= idx[None, :])
    cm[2] = (idx[:, None] < 64) * np.ones((1, 128))
    cm[3] = (idx[:, None] >= 64) * np.ones((1, 128))
    BIG = 30000.0
    cm[4] = np.where(same & (idx[None, :] < idx[:, None]), 0.0, BIG)
    cm[5] = np.where(same & (idx[None, :] > idx[:, None]), 0.0, BIG)
    cm[6] = np.where(same & (idx[:, None] <= idx[None, :]), 0.0, -BIG)
    cm[7] = np.where(same & (idx[:, None] >= idx[None, :]), 0.0, -BIG)
    cm[8] = same
    return cm


def prep_rec_layer(inp, l, hf):
    li = l // 2
    w_in = np.asarray(inp['rec_w_in'][li], np.float32)
    hs = [2 * hf, 2 * hf + 1]
    cols = []
    for base in (0, 512, 1024, 1536):
        for h in hs:
            cols.append(np.arange(base + h * 128, base + (h + 1) * 128))
    for base in (2064, 2576):
        for i in range(2):
            cols.append(np.arange(base + 256 * hf + 128 * i, base + 256 * hf + 128 * (i + 1)))
    wmain = w_in[:, np.concatenate(cols)]
    bcols = [2048 + d * 4 + h for d in range(2) for h in hs]
    acols = [2056 + d * 4 + h for d in range(2) for h in hs]
    wba = w_in[:, bcols + acols]
    gcw = np.asarray(inp['gdn_conv_w'][li], np.float32)
    lcw = np.asarray(inp['lru_conv_w'][li], np.float32)
    lcb = np.asarray(inp['lru_conv_b'][li], np.float32)
    convw = np.zeros((128, 8, 4), np.float32)
    for ci in range(6):
        convw[:, ci, :] = gcw[:, cols[ci]].T
    convb = np.zeros((128, 2), np.float32)
    for i in range(2):
        ch = 256 * hf + 128 * i + np.arange(128)
        convw[:, 6 + i, :] = lcw[:, ch].T
        convb[:, i] = lcb[ch]
    dtb = np.asarray(inp['gdn_dt_bias'][li], np.float32)
    alog = np.asarray(inp['gdn_a_log'][li], np.float32)
    dtb_row = np.tile(np.array([dtb[d, h] for d in range(2) for h in hs], np.float32), NKT)[None, :]
    alog_row = np.tile(np.array([alog[d, h] for d in range(2) for h in hs], np.float32), NKT)[None, :]
    gnorm = np.asarray(inp['gdn_norm'][li], np.float32)[:, None]
    gw = np.asarray(inp['lru_gate_w'][li], np.float32)
    gb = np.asarray(inp['lru_gate_b'][li], np.float32)
    lam = np.asarray(inp['lru_lambda'][li], np.float32)
    wgate = np.zeros((128, 8, 128), np.float32)
    gbcol = np.zeros((128, 8), np.float32)
    lamcol = np.zeros((128, 4), np.float32)
    for d in range(2):
        for i in range(2):
            ch = 256 * hf + 128 * i + np.arange(128)
            lamcol[:, d * 2 + i] = lam[d, ch]
            for g in range(2):
                e = (d * 2 + g) * 2 + i
                n0 = 4 * hf + 2 * i
                wgate[0:64, e, 0:64] = gw[d, g, n0]
                wgate[64:128, e, 64:128] = gw[d, g, n0 + 1]
                gbcol[:, e] = gb[d, g, ch]
    return dict(wmain=np.ascontiguousarray(wmain), wba=np.ascontiguousarray(wba), convw=convw, convb=convb,
                dtb=dtb_row, alog=alog_row, gnorm=gnorm, wgate=wgate, gbcol=gbcol, lamcol=lamcol, cmats=_rec_consts())


_STOP = {}


def _poff(r0):
    return r0 + 1 if r0 == 0 else r0 + 4


def _csrc(r0):
    return r0 if r0 == 0 else r0 + 3


def build_recA():
    nc = bass.Bass("TRN2", target_bir_lowering=False)
    with ExitStack() as st:
        C = Ctx(nc, st)
        emit_recA(C, st)
        C.P.finish()
    return nc


def emit_recA(C, st, K=None, mix_dst=None):
    if True:
        P = C.P
        xall = C.dram_in("xall", [TA, D])
        cc_d = C.dram_in("cc", [128, 16])
        modw_d = C.dram_in("modw", [D, 3 * D])
        modbcol_d = C.dram_in("modb_col", [128, 24])
        modbgate_d = C.dram_in("modb_gate", [1, D])
        ident_d = C.dram_in("ident", [128, 128])
        wmain_d = C.dram_in("wmain", [D, 1536])
        wba_d = C.dram_in("wba", [D, 8])
        convw_d = C.dram_in("convw", [128, 8, 4])
        convb_d = C.dram_in("convb", [128, 2])
        dtb_d = C.dram_in("dtb", [1, NKT * 4])
        alog_d = C.dram_in("alog", [1, NKT * 4])
        gnorm_d = C.dram_in("gnorm", [128, 1])
        wgate_d = C.dram_in("wgate", [128, 8, 128])
        gbcol_d = C.dram_in("gbcol", [128, 8])
        lamcol_d = C.dram_in("lamcol", [128, 4])
        cm_d = C.dram_in("cmats", [10, 128, 128])
        if mix_dst is None:
            mixo = C.dram_out("mixo", [4, 128, TA], BF16)
            mix_dst = lambda idx, c0, n: mixo[idx, :, c0:c0 + n]

        K = K or emit_consts(C, st, ident_d)
        modcol, _ = emit_modulation(C, st, K, cc_d, modw_d, modbcol_d, modbgate_d, need_gate=False)
        cms = C.sb(st, "cms", [128, 10, 128], F32)
        for i in range(9):
            P.dma('sp', cms[:, i, :], cm_d[i], writes=['cms'])
        convw = C.sb(st, "convw", [128, 8, 4], F32)
        convb = C.sb(st, "convb", [128, 2], F32)
        gnorm = C.sb(st, "gnorm", [128, 1], F32)
        gbcol = C.sb(st, "gbcol", [128, 8], F32)
        lamc = C.sb(st, "lamc", [128, 4], F32)
        c8 = C.sb(st, "c8", [128, 4], F32)
        P.dma('sp', convw[:], convw_d, writes=['cw'])
        P.dma('sp', convb[:], convb_d, writes=['cw'])
        P.dma('sp', gnorm[:], gnorm_d, writes=['cw'])
        P.dma('sp', gbcol[:], gbcol_d, writes=['cw'])
        P.dma('sp', lamc[:], lamcol_d, writes=['lamc'])
        wgate = C.sb(st, "wgate", [128, 8, 128], BF16)
        P.dma('pool', wgate[:], wgate_d, writes=['wgate'])
        C.act(c8[:], lamc[:], AF.Exp, reads=['lamc'], writes=['c8'], scale=-1.0)
        C.act(c8[:], c8[:], AF.Ln, reads=['c8'], writes=['c8'], bias=1.0, scale=1.0)
        C.ts('dve', c8[:], c8[:], -8.0, None, ALU.mult, ALU.bypass, reads=['c8'], writes=['c8'])
        beta = C.sb(st, "beta", [128, NKT, 4], F32)
        nbeta = C.sb(st, "nbeta", [128, NKT, 4], F32)
        gall = C.sb(st, "gall", [128, NKT, 8], F32)
        P.op('pool', 'memset', gall[:], 0.0, writes=['gall'])
        allps = Ring(C.psum)

        def build_hT(s2):
            hT_all = C.sb(s2, "hT_all", [128, 8, TA], BF16)
            with ExitStack() as s3:
                xs = C.sb(s3, "xs", [128, 4, 1024], F32)
                for (r0, n, j) in KV_BLOCKS:
                    emit_hT(C, K, xs, 'xs', xall[r0:r0 + n, :], n, j, modcol, None, ('hT', r0), allps,
                            dst=lambda kc, r0=r0, n=n: hT_all[:, kc, r0:r0 + n])
                _drain(C)
            return hT_all

        def conv_block(pre, cvb, ci, r0, n, bias=None):
            sb_ = _csrc(r0)
            C.ts('dve', cvb[:, 0:n], pre[:, sb_:sb_ + n], convw[:, ci, 0:1], None, ALU.mult, ALU.bypass,
                 reads=['pre', 'cw'], writes=['cvb'])
            for k in range(1, 4):
                C.stt(cvb[:, 0:n], pre[:, sb_ + k:sb_ + k + n], convw[:, ci, k:k + 1], cvb[:, 0:n], ALU.mult, ALU.add,
                      reads=['pre', 'cw', 'cvb'], writes=['cvb'])
            if bias is not None:
                C.ts('dve', cvb[:, 0:n], cvb[:, 0:n], bias, None, ALU.add, ALU.bypass, reads=['cvb', 'cw'], writes=['cvb'])

        def fill_pre(pre, wch, hT_all, ci):
            for (r0, n, j) in KV_BLOCKS:
                ps, pk = allps.next()
                for kc in range(8):
                    C.mm(ps[:, 0:n], wch[:, kc, :], hT_all[:, kc, r0:r0 + n], kc == 0, kc == 7,
                         reads=[('wch', ci % 2), ('hT', r0)], writes=[pk])
                C.copy('dve', pre[:, _poff(r0):_poff(r0) + n], ps[:, 0:n], reads=[pk], writes=['pre'])

        def load_chunk_w(wchs, ci):
            w = wchs[ci % 2]
            for kc in range(8):
                P.dma('pool', w[:, kc, :], wmain_d[kc * 128:(kc + 1) * 128, ci * 128:(ci + 1) * 128], writes=[('wch', ci % 2)])
            return w

        with ExitStack() as s2:
            hT_all = build_hT(s2)
            wchs = [C.sb(s2, "wch%d" % i, [128, 8, 128], BF16) for i in range(2)]
            wba = C.sb(s2, "wba", [128, 8, 8], BF16)
            load_w_bf16(C, wba, 'wba', wba_d, 8, 8)
            ba = C.sb(s2, "ba", [128, NKT, 8], F32)
            dtb = C.sb(s2, "dtb", [128, NKT, 4], F32)
            nea = C.sb(s2, "nea", [128, NKT, 4], F32)
            t4a = C.sb(s2, "t4a", [128, NKT, 4], F32)
            t4b = C.sb(s2, "t4b", [128, NKT, 4], F32)
            P.dma('sp', dtb[:].rearrange("p a b -> p (a b)"), dtb_d[0:1, :].broadcast_to([128, NKT * 4]), writes=['dtb'])
            P.dma('sp', nea[:].rearrange("p a b -> p (a b)"), alog_d[0:1, :].broadcast_to([128, NKT * 4]), writes=['nea'])
            ps, pk = allps.next()
            for t in range(NKT):
                for kc in range(8):
                    C.mm(ps[:, 8 * t:8 * t + 8], hT_all[:, kc, t * 128:(t + 1) * 128], wba[:, kc, :], kc == 0, kc == 7,
                         reads=['wba'] + [('hT', b[0]) for b in KV_BLOCKS], writes=[pk])
            C.copy('dve', ba[:].rearrange("p a b -> p (a b)"), ps[:, 0:NKT * 8], reads=[pk], writes=['ba'])
            C.act(beta[:], ba[:, :, 0:4], AF.Sigmoid, reads=['ba'], writes=['beta'])
            C.ts('dve', nbeta[:], beta[:], -1.0, None, ALU.mult, ALU.bypass, reads=['beta'], writes=['nbeta'])
            C.tt('dve', t4a[:], ba[:, :, 4:8], dtb[:], ALU.add, reads=['ba', 'dtb'], writes=['t4a'])
            C.act(t4b[:], t4a[:], AF.Abs, reads=['t4a'], writes=['t4b'])
            C.act(t4b[:], t4b[:], AF.Exp, reads=['t4b'], writes=['t4b'], scale=-1.0)
            C.act(t4b[:], t4b[:], AF.Ln, reads=['t4b'], writes=['t4b'], bias=1.0, scale=1.0)
            C.ts('dve', t4a[:], t4a[:], 0.0, None, ALU.max, ALU.bypass, reads=['t4a'], writes=['t4a'])
            C.tt('dve', t4a[:], t4a[:], t4b[:], ALU.add, reads=['t4a', 't4b'], writes=['t4a'])
            C.act(nea[:], nea[:], AF.Exp, reads=['nea'], writes=['nea'])
            C.ts('dve', nea[:], nea[:], -1.0, None, ALU.mult, ALU.bypass, reads=['nea'], writes=['nea'])
            C.tt('dve', gall[:, :, 0:4], t4a[:], nea[:], ALU.mult, reads=['t4a', 'nea', 'gall'], writes=['gall'])
            pre = C.sb(s2, "pre", [128, TA + 6], F32)
            P.op('pool', 'memset', pre[:], 0.0, writes=['pre'])
            grs = C.sb(s2, "grs", [128, TA], BF16)
            hfw = C.sb(s2, "hfw", [128, TA], F32)
            hbw = [C.sb(s2, "hbw%d" % i, [128, 512], F32) for i in range(2)]
            hctx = C.sb(s2, "hctx", [128, 1], F32)
            L = {}
            for nm in ['xlb', 'r', 'ii', 'a', 'a2', 'bb']:
                L[nm] = C.sb(s2, "l_" + nm, [128, 512], F32)
            xlbf = C.sb(s2, "l_xlbf", [128, 512], BF16)
            ymix = [C.sb(s2, "ymix%d" % i, [128, 512], BF16) for i in range(2)]
            for i in range(2):
                wx = load_chunk_w(wchs, 8 + i)
                fill_pre(pre, wx, hT_all, 8 + i)
                wg = load_chunk_w(wchs, 10 + i)
                for (r0, n, j) in KV_BLOCKS:
                    ps, pk = allps.next()
                    for kc in range(8):
                        C.mm(ps[:, 0:n], wg[:, kc, :], hT_all[:, kc, r0:r0 + n], kc == 0, kc == 7,
                             reads=[('wch', (10 + i) % 2), ('hT', r0)], writes=[pk])
                    C.act(grs[:, r0:r0 + n], ps[:, 0:n], AF.Silu, reads=[pk], writes=['grs'])
                for d in range(2):
                    order = KV_BLOCKS if d == 0 else [KV_BLOCKS[0]] + KV_BLOCKS[:0:-1]
                    for bi, (r0, n, j) in enumerate(order):
                        conv_block(pre, L['xlb'], 6 + i, r0, n, bias=convb[:, i:i + 1])
                        C.copy('pool', xlbf[:, 0:n], L['xlb'][:, 0:n], reads=['cvb'], writes=['xlbf'])
                        er, ei = (d * 2 + 0) * 2 + i, (d * 2 + 1) * 2 + i
                        psr, pkr = allps.next()
                        C.mm(psr[:, 0:n], wgate[:, er, :], xlbf[:, 0:n], True, True, reads=['wgate', 'xlbf'], writes=[pkr])
                        psi, pki = allps.next()
                        C.mm(psi[:, 0:n], wgate[:, ei, :], xlbf[:, 0:n], True, True, reads=['wgate', 'xlbf'], writes=[pki])
                        C.act(L['r'][:, 0:n], psr[:, 0:n], AF.Sigmoid, reads=[pkr, 'cw'], writes=['l_r'], bias=gbcol[:, er:er + 1], scale=1.0)
                        C.act(L['ii'][:, 0:n], psi[:, 0:n], AF.Sigmoid, reads=[pki, 'cw'], writes=['l_ii'], bias=gbcol[:, ei:ei + 1], scale=1.0)
                        C.ts('dve', L['r'][:, 0:n], L['r'][:, 0:n], c8[:, d * 2 + i:d * 2 + i + 1], None, ALU.mult, ALU.bypass,
                             reads=['l_r', 'c8'], writes=['l_r'])
                        C.act(L['a'][:, 0:n], L['r'][:, 0:n], AF.Exp, reads=['l_r'], writes=['l_a'])
                        C.act(L['a2'][:, 0:n], L['r'][:, 0:n], AF.Exp, reads=['l_r'], writes=['l_a2'], scale=2.0)
                        C.ts('dve', L['a2'][:, 0:n], L['a2'][:, 0:n], -1.0, 1.0, ALU.mult, ALU.add, reads=['l_a2'], writes=['l_a2'])
                        C.act(L['a2'][:, 0:n], L['a2'][:, 0:n], AF.Sqrt, reads=['l_a2'], writes=['l_a2'])
                        C.tt('pool', L['ii'][:, 0:n], L['ii'][:, 0:n], L['xlb'][:, 0:n], ALU.mult, reads=['l_ii', 'cvb'], writes=['l_ii'])
                        C.tt('pool', L['bb'][:, 0:n], L['a2'][:, 0:n], L['ii'][:, 0:n], ALU.mult, reads=['l_a2', 'l_ii'], writes=['l_bb'])
                        if d == 0:
                            init = 0.0 if bi == 0 else hfw[:, r0 - 1:r0]
                            P.op('dve', 'tensor_tensor_scan', out=hfw[:, r0:r0 + n], data0=L['a'][:, 0:n], data1=L['bb'][:, 0:n],
                                 initial=init, op0=ALU.mult, op1=ALU.add, reads=['l_a', 'l_bb', 'hfw'], writes=['hfw'])
                        else:
                            hb, hk = hbw[bi % 2], ('hbw', bi % 2)
                            if bi == 0:
                                init = 0.0
                            elif bi == 1:
                                init = hctx[:, 0:1]
                            else:
                                init = hbw[(bi - 1) % 2][:, 0:1]
                            P.op('dve', 'tensor_tensor_scan', out=hb[:, 0:n][:, ::-1], data0=L['a'][:, 0:n][:, ::-1],
                                 data1=L['bb'][:, 0:n][:, ::-1], initial=init, op0=ALU.mult, op1=ALU.add,
                                 reads=['l_a', 'l_bb', ('hbw', (bi - 1) % 2), 'hctx'], writes=[hk])
                            if bi == 0:
                                C.copy('dve', hctx[:, 0:1], hb[:, 0:1], reads=[hk], writes=['hctx'])
                            ym, yk = ymix[bi % 2], ('ymix', bi % 2)
                            C.tt('dve', L['xlb'][:, 0:n], hfw[:, r0:r0 + n], hb[:, 0:n], ALU.add, reads=['hfw', hk, 'cvb'], writes=['cvb'])
                            C.tt('pool', ym[:, 0:n], L['xlb'][:, 0:n], grs[:, r0:r0 + n], ALU.mult, reads=['cvb', 'grs'], writes=[yk])
                            P.dma('sp', mix_dst(2 + i, r0, n), ym[:, 0:n], reads=[yk])
            _drain(C)

        qnT = C.sb(st, "qnT", [128, 2, TA], BF16)
        knT = C.sb(st, "knT", [128, 2, TA], BF16)
        ktm = C.sb(st, "ktm", [128, 2, NKT, 128], BF16)
        vtm = C.sb(st, "vtm", [128, 2, NKT, 128], BF16)
        zsT = C.sb(st, "zsT", [128, 2, TA], BF16)
        with ExitStack() as s2:
            hT_all = build_hT(s2)
            wchs = [C.sb(s2, "wch%d" % i, [128, 8, 128], BF16) for i in range(2)]
            pre = C.sb(s2, "pre", [128, TA + 6], F32)
            P.op('pool', 'memset', pre[:], 0.0, writes=['pre'])
            cvb = C.sb(s2, "cvb", [128, 512], F32)
            sq = C.sb(s2, "g_sq", [128, 512], BF16)
            rt = C.sb(s2, "g_rt", [128, 512], F32)
            knf = C.sb(s2, "g_knf", [128, 512], F32)
            for hh in range(2):
                wz = load_chunk_w(wchs, 6 + hh)
                for (r0, n, j) in KV_BLOCKS:
                    ps, pk = allps.next()
                    for kc in range(8):
                        C.mm(ps[:, 0:n], wz[:, kc, :], hT_all[:, kc, r0:r0 + n], kc == 0, kc == 7,
                             reads=[('wch', (6 + hh) % 2), ('hT', r0)], writes=[pk])
                    C.act(zsT[:, hh, r0:r0 + n], ps[:, 0:n], AF.Silu, reads=[pk], writes=['zsT'])
            for ci in range(6):
                kind, hh = ci // 2, ci % 2
                wc = load_chunk_w(wchs, ci)
                fill_pre(pre, wc, hT_all, ci)
                for (r0, n, j) in KV_BLOCKS:
                    conv_block(pre, cvb, ci, r0, n)
                    C.act(cvb[:, 0:n], cvb[:, 0:n], AF.Silu, reads=['cvb'], writes=['cvb'])
                    src = cvb
                    if kind < 2:
                        C.act(sq[:, 0:n], cvb[:, 0:n], AF.Square, reads=['cvb'], writes=['g_sq'])
                        ps2, pk2 = allps.next()
                        C.mm(ps2[:, 0:n], K['ones_bf'][:], sq[:, 0:n], True, True, reads=['ones_bf', 'g_sq'], writes=[pk2])
                        C.act(rt[:, 0:n], ps2[:, 0:n], AF.Sqrt, reads=[pk2], writes=['g_rt'], bias=float(EPS), scale=1.0)
                        C.recip(rt[:, 0:n], rt[:, 0:n], reads=['g_rt'], writes=['g_rt'])
                        if kind == 0:
                            C.stt(qnT[:, hh, r0:r0 + n], cvb[:, 0:n], float(128.0 ** -0.5), rt[:, 0:n], ALU.mult, ALU.mult,
                                  reads=['cvb', 'g_rt'], writes=['qnT'])
                            continue
                        C.tt('dve', knf[:, 0:n], cvb[:, 0:n], rt[:, 0:n], ALU.mult, reads=['cvb', 'g_rt'], writes=['g_knf'])
                        C.copy('pool', knT[:, hh, r0:r0 + n], knf[:, 0:n], reads=['g_knf'], writes=['knT'])
                        src = knf
                    skey = 'g_knf' if kind == 1 else 'cvb'
                    dstm, dkey = (ktm, 'ktm') if kind == 1 else (vtm, 'vtm')
                    ps, pk = allps.next()
                    nt = n // 128
                    for tt in range(nt):
                        P.op('pe', 'transpose', ps[:, tt * 128:(tt + 1) * 128], src[:, tt * 128:(tt + 1) * 128], K['ident'][:],
                             reads=[skey, 'ident'], writes=[pk])
                    for tt in range(nt):
                        C.copy('dve' if tt % 2 == 0 else 'act', dstm[:, hh, r0 // 128 + tt, :], ps[:, tt * 128:(tt + 1) * 128],
                               reads=[pk], writes=[dkey]) if tt % 2 == 0 else \
                            C.act(dstm[:, hh, r0 // 128 + tt, :], ps[:, tt * 128:(tt + 1) * 128], AF.Copy, reads=[pk], writes=[dkey])
            _drain(C)

        with ExitStack() as s2:
            oacc = C.sb(s2, "oacc", [128, 2, NKT, 128], F32)
            regs = [(C.psum[bnk][0][:, 0:128], C.psum[bnk][1]) for bnk in range(8)]
            RR = Ring(regs)
            NS = 3
            S = {}
            for m in range(4):
                S[m] = dict(
                    wT=[C.sb(s2, "wT%d_%d" % (m, i), [128, 128], BF16) for i in range(NS)],
                    aT=[C.sb(s2, "aT%d_%d" % (m, i), [128, 128], BF16) for i in range(NS)],
                    u=[C.sb(s2, "u%d_%d" % (m, i), [128, 128], F32) for i in range(NS)],
                    kd=[C.sb(s2, "kd%d_%d" % (m, i), [128, 2, 128], BF16) for i in range(NS)],
                    gsm=[C.sb(s2, "gsm%d_%d" % (m, i), [128, 10], F32) for i in range(NS)],
                    Z=[C.sb(s2, "Z%d_%d" % (m, i), [128, 128], F32) for i in range(2)],
                    ZT=[C.sb(s2, "ZT%d_%d" % (m, i), [128, 128], F32) for i in range(2)],
                    Y=[C.sb(s2, "Y%d_%d" % (m, i), [128, 128], F32) for i in range(2)],
                    grep=C.sb(s2, "grep%d" % m, [128, 128], F32),
                    DS=C.sb(s2, "DS%d" % m, [128, 128], F32),
                    DT=C.sb(s2, "DT%d" % m, [128, 128], F32),
                    bv=C.sb(s2, "bv%d" % m, [128, 128], F32),
                    kbg=C.sb(s2, "kbg%d" % m, [128, 128], F32),
                    Sf=[C.sb(s2, "Sf%d_%d" % (m, i), [128, 128], F32) for i in range(2)],
                    Sb=[C.sb(s2, "Sb%d_%d" % (m, i), [128, 128], BF16) for i in range(2)],
                    vn=[C.sb(s2, "vn%d_%d" % (m, i), [128, 128], BF16) for i in range(2)],
                    tB=C.sb(s2, "tB%d" % m, [128, 128], F32),
                    oo=C.sb(s2, "oo%d" % m, [128, 128], F32),
                    cur=0, nblk=0,
                )
                P.op('dve', 'memset', S[m]['Sf'][0][:], 0.0, writes=[('Sf', m, 0)])
                P.op('dve', 'memset', S[m]['Sb'][0][:], 0.0, writes=[('Sb', m, 0)])
                for i_ in range(2):
                    P.op('dve', 'memset', S[m]['vn'][i_][:], 0.0, writes=[('vn', m, i_)])
            identf = K['ident']

            def prep(m, t, sl):
                d, hh = m // 2, m % 2
                X = S[m]
                kn = knT[:, hh, t * 128:(t + 1) * 128]
                qn = qnT[:, hh, t * 128:(t + 1) * 128]
                kt_ = ktm[:, hh, t, :]
                vt_ = vtm[:, hh, t, :]
                gcol, bcol, nbcol = gall[:, t, m:m + 1], beta[:, t, m:m + 1], nbeta[:, t, m:m + 1]
                tri, penS, penT = cms[:, d, :], cms[:, 4 + d, :], cms[:, 6 + d, :]
                gsm, gk = X['gsm'][sl], ('gsm', m, sl)
                pKK, kKK = RR.next()
                C.mm(pKK, kn, kn, True, True, reads=['knT'], writes=[kKK])
                pQK, kQK = RR.next()
                C.mm(pQK, kn, qn, True, True, reads=['knT', 'qnT'], writes=[kQK])
                pg, kg = RR.next()
                for ci_, lh in enumerate([tri, cms[:, 8, :], cms[:, 2, :], cms[:, 3, :]]):
                    C.mm(pg[:, 2 * ci_:2 * ci_ + 2], lh, gall[:, t, m:m + 2], True, True, reads=['cms', 'gall'], writes=[kg])
                C.ts('dve', X['grep'][:], K['ones_f'][:], gcol, None, ALU.mult, ALU.bypass, reads=['ones_f', 'gall'], writes=[('grep', m)])
                pD, kD = RR.next()
                C.mm(pD, X['grep'][:], tri, True, False, reads=[('grep', m), 'cms'], writes=[kD])
                C.mm(pD, identf[:], penS, False, True, reads=['ident', 'cms'], writes=[kD])
                pDT, kDT = RR.next()
                C.mm(pDT, X['grep'][:], tri, True, False, reads=[('grep', m), 'cms'], writes=[kDT])
                C.mm(pDT, identf[:], penT, False, True, reads=['ident', 'cms'], writes=[kDT])
                C.copy('dve', gsm[:, 0:4], pg[:, 0:8:2], reads=[kg], writes=[gk])
                C.act(gsm[:, 4:5], gsm[:, 0:1], AF.Exp, reads=[gk], writes=[gk])
                C.act(gsm[:, 5:6], gsm[:, 0:1], AF.Exp, reads=[gk], writes=[gk], scale=-1.0, bias=gsm[:, 1:2])
                C.act(gsm[:, 7:9], gsm[:, 2:4], AF.Exp, reads=[gk], writes=[gk])
                C.tt('dve', gsm[:, 6:7], bcol, gsm[:, 4:5], ALU.mult, reads=[gk, 'beta'], writes=[gk])
                C.ts('dve', X['DS'][:], pD, gsm[:, 0:1], 0.0, ALU.subtract, ALU.max, reads=[kD, gk], writes=[('DS', m)])
                C.act(X['DS'][:], X['DS'][:], AF.Exp, reads=[('DS', m)], writes=[('DS', m)], scale=-1.0)
                C.ts('dve', X['DT'][:], pDT, gsm[:, 0:1], 0.0, ALU.subtract, ALU.min, reads=[kDT, gk], writes=[('DT', m)])
                C.act(X['DT'][:], X['DT'][:], AF.Exp, reads=[('DT', m)], writes=[('DT', m)])
                Z, ZT, Y = X['Z'], X['ZT'], X['Y']
                zk = lambda i: ('Z', m, i)
                ztk = lambda i: ('ZT', m, i)
                yk = lambda i: ('Y', m, i)
                C.stt(ZT[0][:], pKK, nbcol, X['DS'][:], ALU.mult, ALU.mult, reads=[kKK, 'nbeta', ('DS', m)], writes=[ztk(0)])
                C.tt('dve', X['aT'][sl][:], pQK, X['DT'][:], ALU.mult, reads=[kQK, ('DT', m)], writes=[('aT', m, sl)])
                pZ, kZ = RR.next()
                P.op('pe', 'transpose', pZ, ZT[0][:], identf[:], reads=[ztk(0), 'ident'], writes=[kZ])
                C.act(Z[0][:], pZ, AF.Copy, reads=[kZ], writes=[zk(0)])
                C.tt('dve', Y[0][:], pZ, identf[:], ALU.add, reads=[kZ, 'ident'], writes=[yk(0)])
                for k in range(5):
                    cur, nxt = k % 2, 1 - k % 2
                    p1, k1 = RR.next()
                    C.mm(p1, Z[cur][:], ZT[cur][:], True, True, reads=[zk(cur), ztk(cur)], writes=[k1])
                    if k < 4:
                        p2, k2 = RR.next()
                        C.mm(p2, ZT[cur][:], Z[cur][:], True, True, reads=[zk(cur), ztk(cur)], writes=[k2])
                    C.act(ZT[nxt][:], p1, AF.Copy, reads=[k1], writes=[ztk(nxt)])
                    if k < 4:
                        C.copy('dve', Z[nxt][:], p2, reads=[k2], writes=[zk(nxt)])
                    p3, k3 = RR.next()
                    C.mm(p3, ZT[nxt][:], Y[cur][:], True, True, reads=[ztk(nxt), yk(cur)], writes=[k3])
                    C.tt('dve', Y[nxt][:], Y[cur][:], p3, ALU.add, reads=[yk(cur), k3], writes=[yk(nxt)])
                Yf, Yk = Y[1], yk(1)
                C.ts('dve', X['bv'][:], vt_, bcol, None, ALU.mult, ALU.bypass, reads=['vtm', 'beta'], writes=[('bv', m)])
                C.ts('dve', X['kbg'][:], kt_, gsm[:, 6:7], None, ALU.mult, ALU.bypass, reads=['ktm', gk], writes=[('kbg', m)])
                pU, kU = RR.next()
                C.mm(pU, Yf[:], X['bv'][:], True, True, reads=[Yk, ('bv', m)], writes=[kU])
                pW, kW = RR.next()
                C.mm(pW, X['kbg'][:], Yf[:], True, True, reads=[Yk, ('kbg', m)], writes=[kW])
                C.copy('dve', X['u'][sl][:], pU, reads=[kU], writes=[('u', m, sl)])
                C.act(X['wT'][sl][:], pW, AF.Copy, reads=[kW], writes=[('wT', m, sl)])
                for blk_ in range(2):
                    C.tt('dve', gsm[:, 9:10], gsm[:, 5:6], cms[:, 2 + blk_, 0:1], ALU.mult, reads=[gk, 'cms'], writes=[gk])
                    C.ts('dve', X['kd'][sl][:, blk_, :], kt_, gsm[:, 9:10], None, ALU.mult, ALU.bypass, reads=['ktm', gk],
                         writes=[('kd', m, sl)])

            def seq(m, t, sl, first):
                d, hh = m // 2, m % 2
                X = S[m]
                gsm, gk = X['gsm'][sl], ('gsm', m, sl)
                for blk in ([0, 1] if d == 0 else [1, 0]):
                    R = slice(64 * blk, 64 * blk + 64)
                    cur = X['cur']
                    nxt = 1 - cur
                    vi = X['nblk'] % 2
                    X['nblk'] += 1
                    vn, vk = X['vn'][vi], ('vn', m, vi)
                    Sb, Sbk = X['Sb'][cur], ('Sb', m, cur)
                    p1, k1 = RR.next()
                    C.mm(p1, X['wT'][sl][:], Sb[:], True, True, reads=[('wT', m, sl), Sbk], writes=[k1])
                    C.tt('dve', vn[R, :], X['u'][sl][R, :], p1[R, :], ALU.subtract, reads=[('u', m, sl), k1], writes=[vk])
                    pA, kA = RR.next()
                    C.mm(pA, qnT[:, hh, t * 128:(t + 1) * 128], Sb[:], True, True, reads=['qnT', Sbk], writes=[kA])
                    pB, kB = RR.next()
                    C.mm(pB, X['aT'][sl][:], vn[:], True, True, reads=[('aT', m, sl), vk], writes=[kB])
                    pS, kS = RR.next()
                    C.mm(pS, X['kd'][sl][:, blk, :], vn[:], True, True, reads=[('kd', m, sl), vk], writes=[kS])
                    C.stt(X['Sf'][nxt][:], X['Sf'][cur][:], gsm[:, 7 + blk:8 + blk], pS, ALU.mult, ALU.add,
                          reads=[('Sf', m, cur), gk, kS], writes=[('Sf', m, nxt)])
                    C.act(X['Sb'][nxt][:], X['Sf'][nxt][:], AF.Copy, reads=[('Sf', m, nxt)], writes=[('Sb', m, nxt)])
                    X['cur'] = nxt
                    C.act(X['tB'][R, :], pB[R, :], AF.Copy, reads=[kB], writes=[('tB', m)])
                    ok = ('oacc', hh, t)
                    if first:
                        C.stt(oacc[R, hh, t, :], pA[R, :], gsm[R, 4:5], X['tB'][R, :], ALU.mult, ALU.add,
                              reads=[kA, gk, ('tB', m)], writes=[ok])
                    else:
                        C.stt(X['oo'][R, :], pA[R, :], gsm[R, 4:5], X['tB'][R, :], ALU.mult, ALU.add,
                              reads=[kA, gk, ('tB', m)], writes=[('oo', m)])
                        C.tt('pool', oacc[R, hh, t, :], oacc[R, hh, t, :], X['oo'][R, :], ALU.add, reads=[ok, ('oo', m)], writes=[ok])

            ORD = [list(range(NKT)), [1, 0] + list(range(NKT - 1, 1, -1))]
            STEP = [{t: i for i, t in enumerate(o)} for o in ORD]

            def tile_of(m, s):
                return ORD[m // 2][s]
            for m in range(4):
                prep(m, tile_of(m, 0), 0)
            for s in range(NKT):
                if s + 1 < NKT:
                    for m in range(4):
                        prep(m, tile_of(m, s + 1), (s + 1) % NS)
                for m in range(4):
                    t = tile_of(m, s)
                    d_ = m // 2
                    first = STEP[d_][t] < STEP[1 - d_][t] or (STEP[d_][t] == STEP[1 - d_][t] and d_ == 0)
                    seq(m, t, s % NS, first)
            ssq = C.sb(s2, "f_ssq", [128, 2], F32)
            junk = C.sb(s2, "f_junk", [128, 128], F32)
            on = C.sb(s2, "f_on", [128, 128], F32)
            stg = [C.sb(s2, "f_stg%d" % i, [128, 512], BF16) for i in range(2)]
            gi = 0
            for hh in range(2):
                for t0 in range(0, NKT, 4):
                    nt = min(4, NKT - t0)
                    sg, sk = stg[gi % 2], ('stg', gi % 2)
                    gi += 1
                    for tt in range(nt):
                        t = t0 + tt
                        ok = ('oacc', hh, t)
                        C.act(junk[:], oacc[:, hh, t, :], AF.Square, reads=[ok], writes=['f_junk', 'f_ssq'], accum_out=ssq[:, 0:1])
                        C.act(ssq[:, 1:2], ssq[:, 0:1], AF.Sqrt, reads=['f_ssq'], writes=['f_ssq'], bias=float(EPS), scale=1.0 / 128)
                        C.recip(ssq[:, 1:2], ssq[:, 1:2], reads=['f_ssq'], writes=['f_ssq'])
                        C.ts('dve', on[:], oacc[:, hh, t, :], ssq[:, 1:2], None, ALU.mult, ALU.bypass, reads=[ok, 'f_ssq'], writes=['f_on'])
                        pT, kT = RR.next()
                        P.op('pe', 'transpose', pT, on[:], identf[:], reads=['f_on', 'ident'], writes=[kT])
                        C.stt(sg[:, tt * 128:(tt + 1) * 128], pT, gnorm[:, 0:1], zsT[:, hh, t * 128:(t + 1) * 128], ALU.mult, ALU.mult,
                              reads=[kT, 'cw', 'zsT'], writes=[sk])
                    P.dma('sp', mix_dst(hh, t0 * 128, nt * 128), sg[:, 0:nt * 128], reads=[sk])
            _drain(C)


def build_recB():
    nc = bass.Bass("TRN2", target_bir_lowering=False)
    with ExitStack() as st:
        C = Ctx(nc, st)
        emit_recB(C, st)
        C.P.finish()
    return nc


def emit_recB(C, st, K=None, ntok=NOWN):
    if True:
        P = C.P
        xown = C.dram_in("xown", [ntok, D])
        mixT_d = C.dram_in("mixT", [8, 128, ntok], BF16)
        cc_d = C.dram_in("cc", [128, 16])
        modw_d = C.dram_in("modw", [D, 3 * D])
        modbcol_d = C.dram_in("modb_col", [128, 24])
        modbgate_d = C.dram_in("modb_gate", [1, D])
        lng_d = C.dram_in("lng", [1, D])
        lnb_d = C.dram_in("lnb", [1, D])
        wout_d = C.dram_in("wout", [D, D])
        ident_d = C.dram_in("ident", [128, 128])
        xo = C.dram_out("xo", [ntok, D])
        K = K or emit_consts(C, st, ident_d)
        lnbc = C.sb(st, "lnbc", [128, 2, 1024], F32)
        P.dma('sp', lnbc[:, 0, :], lng_d[0:1, :].broadcast_to([128, D]), writes=['lnbc'])
        P.dma('sp', lnbc[:, 1, :], lnb_d[0:1, :].broadcast_to([128, D]), writes=['lnbc'])
        _, gate_bc = emit_modulation(C, st, K, cc_d, modw_d, modbcol_d, modbgate_d, need_cols=False)
        wout = C.sb(st, "wout", [128, 8, 1024], BF16)
        load_w_bf16(C, wout, 'wout', wout_d, 8, 1024)
        mixT = C.sb(st, "mixT", [128, 8, ntok], BF16)
        for c in range(8):
            P.dma('sp', mixT[:, c, :], mixT_d[c], writes=[('mixT', c)])
        E = alloc_epilogue(C, st)
        allps = Ring(C.psum)
        for ti in range(ntok // 128):
            r0 = ti * 128
            j = 1 if r0 < TC else 0
            emit_epilogue_tile(C, E, lambda c, r0=r0: (mixT[:, c, r0:r0 + 128], [('mixT', c)]), wout,
                               xown[r0:r0 + 128, :], xo[r0:r0 + 128, :], j, gate_bc, lnbc, allps, ti)
        _drain(C)


def run_rec_layer(inp, l, x, ctx):
    li = l // 2
    com = _common_layer(inp, l)
    ncA = _get('recA', build_recA)
    in_maps = []
    for core in range(8):
        b, hf = core // 2, core % 2
        m = dict(xall=np.ascontiguousarray(np.concatenate([ctx[b], x[b]], axis=0)), cc=_cc_cols(inp['c'][b], inp['c_ctx']),
                 modw=com['modw'], modb_col=com['modb_col'], modb_gate=com['modb_gate'], ident=com['ident'])
        m.update(prep_rec_layer(inp, l, hf))
        in_maps.append(m)
    resA = run_bass_kernel_spmd(ncA, in_maps, core_ids=list(range(8)))
    ncB = _get('recB', build_recB)
    wout = np.asarray(inp['rec_w_out'][li], np.float32)
    in_maps = []
    for core in range(8):
        b, hf = core // 2, core % 2
        mo = [resA.results[2 * b + k]["mixo"] for k in range(2)]
        full = np.concatenate([mo[0][0:2], mo[1][0:2], mo[0][2:4], mo[1][2:4]], axis=0)
        sel = np.concatenate([np.arange(TC), TC + hf * OWN + np.arange(OWN)])
        m = dict(xown=np.ascontiguousarray(np.concatenate([ctx[b], x[b, hf * OWN:(hf + 1) * OWN]], axis=0)),
                 mixT=np.ascontiguousarray(full[:, :, sel]), cc=_cc_cols(inp['c'][b], inp['c_ctx']), wout=wout)
        m.update(com)
        in_maps.append(m)
    resB = run_bass_kernel_spmd(ncB, in_maps, core_ids=list(range(8)))
    xn = np.empty_like(x)
    cn = np.empty_like(ctx)
    for core in range(8):
        b, hf = core // 2, core % 2
        o = resB.results[core]["xo"]
        xn[b, hf * OWN:(hf + 1) * OWN] = o[TC:]
        if hf == 0:
            cn[b] = o[:TC]
    return xn, cn


def build_fused():
    nc = bass.Bass("TRN2", target_bir_lowering=False)
    with ExitStack() as st:
        C = Ctx(nc, st)
        ident_d = C.dram_in("ident", [128, 128])
        K = emit_consts(C, st, ident_d)
        x_in = C.dram_in("x_in", [TA, D])
        out = C.dram_out("xfin", [TA, D])
        xb = [nc.dram_tensor("xb%d" % i, [TA, D], F32, kind="Internal").ap() for i in range(2)]
        mixb = nc.dram_tensor("mixb", [8, 128, TA], BF16, kind="Internal").ap()
        for l in range(4):
            src = x_in if l == 0 else xb[(l - 1) % 2]
            dst = out if l == 3 else xb[l % 2]
            C.lsfx = "_L%d" % l
            if l % 2 == 0:
                C.sfx = "_L%d" % l
                C.override = dict(xall=src, xown=src, xo=dst)
                with ExitStack() as ph:
                    emit_att(C, ph, K, full=True)
            else:
                for hfr in range(2):
                    C.sfx = "_L%d_%d" % (l, hfr)
                    C.override = dict(xall=src)
                    with ExitStack() as ph:
                        emit_recA(C, ph, K, mix_dst=lambda idx, c0, n, hfr=hfr: mixb[
                            (2 * hfr + idx) if idx < 2 else (4 + 2 * hfr + idx - 2), :, c0:c0 + n])
                C.sfx = "_L%d" % l
                C.override = dict(xown=src, mixT=mixb, xo=dst)
                with ExitStack() as ph:
                    emit_recB(C, ph, K, ntok=TA)
        C.P.finish()
    return nc


def fused_inputs(inp, b):
    m = dict(ident=np.eye(128, dtype=np.float32), cc=_cc_cols(inp['c'][b], inp['c_ctx']),
             x_in=np.ascontiguousarray(np.concatenate([inp['ctx'][b], inp['x'][b]], axis=0), dtype=np.float32))
    LAYER = Ctx.LAYER
    for l in range(4):
        com = _common_layer(inp, l)
        for k in LAYER:
            m[k + "_L%d" % l] = com[k]
        if l % 2 == 0:
            sh, (tabKg, tabKm) = prep_att_layer(inp, l)
            for k, v in sh.items():
                if k not in LAYER and k != 'ident':
                    m[k + "_L%d" % l] = v
            m["tabKg_L%d" % l] = tabKg
            m["tabKm_L%d" % l] = tabKm
        else:
            for hfr in range(2):
                for k, v in prep_rec_layer(inp, l, hfr).items():
                    m[k + "_L%d_%d" % (l, hfr)] = v
            m["wout_L%d" % l] = np.asarray(inp['rec_w_out'][l // 2], np.float32)
    return m


def kernel(**inputs):
    inp = {k: np.asarray(v) for k, v in inputs.items()}
    nc = _get('fused', build_fused)
    per_b = [fused_inputs(inp, b) for b in range(NB)]
    in_maps = [per_b[core // 2] for core in range(8)]
    res = run_bass_kernel_spmd(nc, in_maps, core_ids=list(range(8)))
    out = np.empty((NB, TL, D), np.float32)
    for b in range(NB):
        out[b] = res.results[2 * b]["xfin"][TC:]
    return out


def kernel_unfused(**inputs):
    inp = {k: np.asarray(v) for k, v in inputs.items()}
    x = np.ascontiguousarray(inp['x'], dtype=np.float32)
    ctx = np.ascontiguousarray(inp['ctx'], dtype=np.float32)
    for l in range(4):
        if l % 2 == 0:
            x, ctx = run_att_layer(inp, l, x, ctx)
        else:
            x, ctx = run_rec_layer(inp, l, x, ctx)
    return x
```

```python
import numpy as np
from contextlib import ExitStack
import concourse.bass as bass
import concourse.mybir as mybir
from concourse.bass_utils import run_bass_kernel_spmd

F32 = mybir.dt.float32
BF16 = mybir.dt.bfloat16
AF = mybir.ActivationFunctionType
ALU = mybir.AluOpType

D = 1024
NB = 4
TL = 4096
TC = 256
TA = TL + TC
OWN = 2048
NOWN = OWN + TC
EPS = 1e-6
ALPHA = 8.0 ** 0.25
THETA = 10000.0


class Prog:
    def __init__(self, nc, stack, n_dma=12, same_engine_sync=True):
        self.nc = nc
        self.stack = stack
        self.eng = {'pe': nc.tensor, 'act': nc.scalar, 'dve': nc.vector, 'pool': nc.gpsimd, 'sp': nc.sync}
        self.sem = {}
        for e in ['pe', 'act', 'dve', 'pool']:
            self.sem[e] = stack.enter_context(nc.semaphore("s_" + e))
        self.cnt = {e: 0 for e in ['pe', 'act', 'dve', 'pool']}
        self.n_dma = n_dma
        self.dq = {}
        for q in ['sp', 'pool']:
            sems = [stack.enter_context(nc.semaphore("d_%s_%d" % (q, i))) for i in range(n_dma)]
            for i, s in enumerate(sems):
                self.sem[('d', q, i)] = s
            self.dq[q] = dict(targets=[0] * n_dma, next=0)
        self.waited = {e: {} for e in self.eng}
        self.lastw = {}
        self.readers = {}
        self.same = same_engine_sync
        self.n_inst = 0

    def _wait(self, E, tok):
        if tok is None:
            return
        key, val = tok
        if key == E and (E == 'pe' or not self.same):
            return
        if self.waited[E].get(key, 0) >= val:
            return
        self.eng[E].wait_ge(self.sem[key], val)
        self.waited[E][key] = val
        self.n_inst += 1

    def _deps(self, E, reads, writes):
        for b in reads:
            self._wait(E, self.lastw.get(b))
            if isinstance(b, str) and b.startswith('ps'):
                for t in self.readers.get(b, ()):
                    if t[0] != E:
                        self._wait(E, t)
        for b in writes:
            self._wait(E, self.lastw.get(b))
            for t in self.readers.get(b, ()):
                self._wait(E, t)

    def _record(self, tok, reads, writes):
        for b in reads:
            lst = self.readers.setdefault(b, [])
            lst[:] = [t for t in lst if t[0] != tok[0]]
            lst.append(tok)
        for b in writes:
            self.lastw[b] = tok
            self.readers[b] = []

    def op(self, E, name, *args, reads=(), writes=(), **kw):
        self._deps(E, reads, writes)
        inst = getattr(self.eng[E], name)(*args, **kw)
        self.cnt[E] += 1
        inst.then_inc(self.sem[E], 1)
        self.n_inst += 1
        self._record((E, self.cnt[E]), reads, writes)
        return inst

    def dma(self, q, out, in_, reads=(), writes=(), **kw):
        self._deps(q, reads, writes)
        d = self.dq[q]
        idx = d['next'] % self.n_dma
        d['next'] += 1
        key = ('d', q, idx)
        if d['targets'][idx] > 0:
            self._wait(q, (key, d['targets'][idx]))
        inst = self.eng[q].dma_start(out=out, in_=in_, **kw)
        d['targets'][idx] += 16
        inst.then_inc(self.sem[key], 16)
        self.n_inst += 1
        self._record((key, d['targets'][idx]), reads, writes)
        return inst

    def finish(self):
        for q in self.dq:
            d = self.dq[q]
            for idx in range(self.n_dma):
                if d['targets'][idx] > 0:
                    self._wait('sp', (('d', q, idx), d['targets'][idx]))
        for e in ['pe', 'act', 'dve', 'pool']:
            if self.cnt[e] > 0:
                self._wait('sp', (e, self.cnt[e]))


class Ring:
    def __init__(self, items):
        self.items = items
        self.i = 0

    def next(self):
        it = self.items[self.i % len(self.items)]
        self.i += 1
        return it


class Ctx:
    def __init__(self, nc, st):
        self.nc = nc
        self.st = st
        self.P = Prog(nc, st)
        self.psum = []
        for i in range(8):
            t = st.enter_context(nc.psum_tensor("ps%d" % i, [128, 512], F32))
            self.psum.append((t, "ps%d" % i))

    def sb(self, stack, name, shape, dt):
        self.uid = getattr(self, 'uid', 0) + 1
        return stack.enter_context(self.nc.sbuf_tensor("%s_%d" % (name, self.uid), shape, dt))

    SHARED = ('ident', 'cc')
    LAYER = ('modw', 'modb_col', 'modb_gate', 'lng', 'lnb')

    def _dram(self, name, shape, dt, kind):
        ov = getattr(self, 'override', {})
        if name in ov:
            return ov[name]
        if name in self.SHARED:
            full = name
        elif name in self.LAYER:
            full = name + getattr(self, 'lsfx', getattr(self, 'sfx', ''))
        else:
            full = name + getattr(self, 'sfx', '')
        cache = self.__dict__.setdefault('decl', {})
        if full not in cache:
            cache[full] = self.nc.dram_tensor(full, list(shape), dt, kind=kind).ap()
        return cache[full]

    def dram_in(self, name, shape, dt=F32):
        return self._dram(name, shape, dt, "ExternalInput")

    def dram_out(self, name, shape, dt=F32):
        return self._dram(name, shape, dt, "ExternalOutput")

    def mm(self, out, lhsT, rhs, start, stop, reads, writes):
        return self.P.op('pe', 'matmul', out, lhsT, rhs, start=start, stop=stop, reads=reads, writes=writes)

    def act(self, out, in_, func, reads, writes, **kw):
        return self.P.op('act', 'activation', out=out, in_=in_, func=func, reads=reads, writes=writes, **kw)

    def tt(self, E, out, in0, in1, op, reads, writes):
        return self.P.op(E, 'tensor_tensor', out=out, in0=in0, in1=in1, op=op, reads=reads, writes=writes)

    def stt(self, out, in0, scalar, in1, op0, op1, reads, writes):
        return self.P.op('dve', 'scalar_tensor_tensor', out=out, in0=in0, scalar=scalar, in1=in1,
                         op0=op0, op1=op1, reads=reads, writes=writes)

    def ts(self, E, out, in0, s1, s2, op0, op1, reads, writes):
        return self.P.op(E, 'tensor_scalar', out=out, in0=in0, scalar1=s1, scalar2=s2, op0=op0, op1=op1,
                         reads=reads, writes=writes)

    def copy(self, E, out, in_, reads, writes):
        return self.P.op(E, 'tensor_copy', out=out, in_=in_, reads=reads, writes=writes)

    def recip(self, out, in_, reads, writes):
        return self.P.op('dve', 'reciprocal', out=out, in_=in_, reads=reads, writes=writes)


def _rope_tabs(pos_row, pos_col, dim):
    half = dim // 2
    q = half // 2
    inv = THETA ** (-np.arange(0, half, 2, dtype=np.float32) / np.float32(half))
    inv = inv.astype(np.float32)
    ang_r = pos_row.astype(np.float32)[:, None] * inv
    ang_c = pos_col.astype(np.float32)[:, None] * inv
    cr, sr, cc_, sc_ = np.cos(ang_r), np.sin(ang_r), np.cos(ang_c), np.sin(ang_c)
    T = pos_row.shape[0]
    cos = np.zeros((dim, T), np.float32)
    sin = np.zeros((dim, T), np.float32)
    partner = np.zeros(dim, np.int64)
    for d in range(dim):
        grp, i = d // q, d % q
        c, s = (cr, sr) if grp < 2 else (cc_, sc_)
        cos[d] = c[:, i]
        if grp % 2 == 0:
            sin[d] = -s[:, i]
            partner[d] = d + q
        else:
            sin[d] = s[:, i]
            partner[d] = d - q
    return cos, sin, partner


def _att_tables():
    t = np.arange(TL)
    row, col = t // 64, t % 64
    cg, sg, pg = _rope_tabs(row, col, 64)
    cm, sm, pm = _rope_tabs(row, col, 32)
    return cg, sg, pg, cm, sm, pm


def prep_att_layer(inp, l):
    li = l // 2
    cg, sg, pg, cm, sm, pm = _att_tables()
    w_in = np.asarray(inp['att_w_in'][li], np.float32)
    o = np.cumsum([0, 256, 128, 32, 512, 512, 128, 128, 512])
    cq, ckv, kr, ga, qb, kb, vb, gb = [w_in[:, o[i]:o[i + 1]] for i in range(8)]
    hperm = np.concatenate([np.concatenate([np.arange(64) + 64 * c, np.arange(64) + 64 * (4 + c)]) for c in range(4)])
    sw64 = np.concatenate([pg + 64 * h for h in range(8)])
    qb_sw = qb[:, sw64]
    wq = np.concatenate([cq, ga, gb[:, hperm], qb[:, hperm], qb_sw[:, hperm]], axis=1)
    kb_sw = kb[:, np.concatenate([pg, pg + 64])]
    z64 = np.zeros((D, 64), np.float32)
    wkv = np.concatenate([ckv, kb, kb_sw, z64, kr, z64, kr[:, pm], vb], axis=1)
    w_uq = np.asarray(inp['mla_w_uq'][li], np.float32).reshape(256, 8, 96)
    wuqA = w_uq.reshape(256, 768)
    wuqB = np.concatenate([w_uq[:, :, :64], w_uq[:, :, 64:][:, :, pm]], axis=2).reshape(256, 768)
    w_ukv = np.asarray(inp['mla_w_ukv'][li], np.float32).reshape(128, 8, 128)
    wuk = w_ukv[:, :, :64].reshape(128, 512)
    wuv = w_ukv[:, :, 64:].reshape(128, 512)
    w_out = np.asarray(inp['att_w_out'][li], np.float32)
    wout = np.concatenate([w_out[:512], w_out[512:][hperm]], axis=0)
    gq = np.asarray(inp['gqa_q_norm'][li], np.float32)
    gk = np.asarray(inp['gqa_k_norm'][li], np.float32)
    gcols = np.zeros((128, 8), np.float32)
    gcols[:, 0:2] = np.asarray(inp['mla_q_norm'][li], np.float32).reshape(2, 128).T
    gcols[:, 2] = np.asarray(inp['mla_kv_norm'][li], np.float32)
    gcols[:, 3] = np.tile(gq, 2)
    gcols[:, 4] = np.tile(gq[pg], 2)
    gcols[:, 5] = np.tile(gk, 2)
    gcols[:, 6] = np.tile(gk[pg], 2)
    tabKg = np.zeros((2, 128, TA), np.float32)
    tabKg[0, :, :TC] = 1.0
    tabKg[0, :, TC:] = np.tile(cg, (2, 1))
    tabKg[1, :, TC:] = np.tile(sg, (2, 1))
    tabKm = np.zeros((2, 96, TA), np.float32)
    tabKm[0, 64:, :TC] = 1.0
    tabKm[0, 64:, TC:] = cm
    tabKm[1, 64:, TC:] = sm
    sh = dict(wq=wq, wkv=wkv, wuqA=wuqA, wuqB=wuqB, wuk=wuk, wuv=wuv, wout=wout, gcols=gcols)
    sh.update(_common_layer(inp, l))
    return sh, (tabKg, tabKm)


def _common_layer(inp, l):
    mod_b = np.asarray(inp['mod_b'][l], np.float32)
    return dict(
        modw=np.asarray(inp['mod_w'][l], np.float32),
        modb_col=np.ascontiguousarray(mod_b.reshape(24, 128).T),
        modb_gate=np.ascontiguousarray(mod_b[None, 2048:]),
        lng=np.asarray(inp['ln_g'][l], np.float32)[None, :],
        lnb=np.asarray(inp['ln_b'][l], np.float32)[None, :],
        ident=np.eye(128, dtype=np.float32),
    )


def _cc_cols(c_b, c_ctx):
    cc = np.zeros((128, 16), np.float32)
    cc[:, 0::2] = np.asarray(c_b, np.float32).reshape(8, 128).T
    cc[:, 1::2] = np.asarray(c_ctx, np.float32).reshape(8, 128).T
    return cc


def emit_consts(C, st, ident_d):
    P = C.P
    k = {}
    k['ident'] = C.sb(st, "ident", [128, 128], F32)
    P.dma('sp', k['ident'][:], ident_d, writes=['ident'])
    k['ones_f'] = C.sb(st, "ones_f", [128, 128], F32)
    P.op('dve', 'memset', k['ones_f'][:], 1.0, writes=['ones_f'])
    k['ones_bf'] = C.sb(st, "ones_bf", [128, 128], BF16)
    P.op('dve', 'memset', k['ones_bf'][:], 1.0, writes=['ones_bf'])
    k['onesbd'] = C.sb(st, "onesbd", [128, 128], BF16)
    P.op('dve', 'memset', k['onesbd'][:], 0.0, writes=['onesbd'])
    P.op('dve', 'memset', k['onesbd'][0:64, 0:64], 1.0, reads=['onesbd'], writes=['onesbd'])
    P.op('dve', 'memset', k['onesbd'][64:128, 64:128], 1.0, reads=['onesbd'], writes=['onesbd'])
    return k


def emit_modulation(C, st, K, cc_d, modw_d, modbcol_d, modbgate_d, need_cols=True, need_gate=True):
    P = C.P
    modcol = C.sb(st, "modcol", [128, 16, 2], F32)
    gate_bc = C.sb(st, "gate_bc", [128, 2, 1024], F32) if need_gate else None
    with ExitStack() as s2:
        cc = C.sb(s2, "cc", [128, 16], F32)
        sc = C.sb(s2, "sc", [128, 16], F32)
        mbc = C.sb(s2, "mbc", [128, 24], F32)
        mbg = C.sb(s2, "mbg", [128, 1024], F32)
        rep = C.sb(s2, "rep", [128, 2, 8, 128], F32)
        mw = C.sb(s2, "mw", [128, 8, 1024], F32)
        P.dma('sp', cc[:], cc_d, writes=['cc'])
        P.dma('sp', mbc[:], modbcol_d, writes=['mbc'])
        P.dma('sp', mbg[:], modbgate_d[0:1, :].broadcast_to([128, 1024]), writes=['mbg'])
        C.act(sc[:], cc[:], AF.Silu, reads=['cc'], writes=['sc'])
        for j in range(2):
            for kc in range(8):
                C.ts('dve', rep[:, j, kc, :], K['ones_f'][:], sc[:, 2 * kc + j:2 * kc + j + 1], None, ALU.mult,
                     ALU.bypass, reads=['ones_f', 'sc'], writes=['rep'])
        for t in range(3):
            if (t < 2 and not need_cols) or (t == 2 and not need_gate):
                continue
            for kc in range(8):
                P.dma('sp', mw[:, kc, :], modw_d[kc * 128:(kc + 1) * 128, t * 1024:(t + 1) * 1024], writes=['mw'])
            if t < 2:
                ps, pk = C.psum[t]
                for oc in range(8):
                    for kc in range(8):
                        C.mm(ps[:, 2 * oc:2 * oc + 2], mw[:, kc, oc * 128:(oc + 1) * 128], sc[:, 2 * kc:2 * kc + 2],
                             kc == 0, kc == 7, reads=['mw', 'sc'], writes=[pk])
                    C.ts('dve', modcol[:, t * 8 + oc, :], ps[:, 2 * oc:2 * oc + 2],
                         mbc[:, t * 8 + oc:t * 8 + oc + 1], float(t), ALU.add, ALU.add,
                         reads=[pk, 'mbc'], writes=['modcol'])
            else:
                for j in range(2):
                    for half in range(2):
                        ps, pk = C.psum[2 + 2 * j + half]
                        for kc in range(8):
                            C.mm(ps[:, :], rep[:, j, kc, :], mw[:, kc, half * 512:(half + 1) * 512], kc == 0, kc == 7,
                                 reads=['mw', 'rep'], writes=[pk])
                        C.tt('dve', gate_bc[:, j, half * 512:(half + 1) * 512], ps[:, :],
                             mbg[:, half * 512:(half + 1) * 512], ALU.add, reads=[pk, 'mbg'], writes=['gate_bc'])
        _drain(C)
    return modcol, gate_bc


def _drain(C):
    P = C.P
    toks = [(e, P.cnt[e]) for e in ['pe', 'act', 'dve', 'pool'] if P.cnt[e] > 0]
    dtoks = []
    for q in P.dq:
        d = P.dq[q]
        for idx in range(P.n_dma):
            if d['targets'][idx] > 0:
                dtoks.append((('d', q, idx), d['targets'][idx]))
    for E in ['pe', 'act', 'dve', 'pool', 'sp']:
        for t in toks + dtoks:
            P._wait(E, t)


def emit_hT(C, K, xs, xs_key, x_rows, ntok, j, modcol, hT, hT_key, ring, dst=None):
    P = C.P
    nt = ntok // 128
    for tt in range(nt):
        P.dma('sp', xs[:, tt, :], x_rows[tt * 128:(tt + 1) * 128, :], writes=[(xs_key, tt)])
    for kc in range(8):
        ps, pk = ring.next()
        for tt in range(nt):
            P.op('pe', 'transpose', ps[:, tt * 128:(tt + 1) * 128], xs[:, tt, kc * 128:(kc + 1) * 128], K['ident'][:],
                 reads=[(xs_key, tt), 'ident'], writes=[pk])
        C.act(hT[:, kc, 0:ntok] if dst is None else dst(kc), ps[:, 0:ntok], AF.Identity, reads=[pk, 'modcol'], writes=[hT_key],
              scale=modcol[:, 8 + kc, j:j + 1], bias=modcol[:, kc, j:j + 1])


def emit_epilogue_tile(C, E, mixT_tile_fn, wout, x_rows, out_rows, j, gate_bc, lnbc, ring, tagi):
    P = C.P
    xt, tb, st6, mv, sm = E['xt'][tagi % 2], E['tb'][tagi % 2], E['st6'][tagi % 2], E['mv'][tagi % 2], E['sm'][tagi % 2]
    kx, kt, ks = ('xt', tagi % 2), ('tb', tagi % 2), ('sm', tagi % 2)
    P.dma('sp', xt[:], x_rows, writes=[kx])
    for half in range(2):
        ps, pk = ring.next()
        for c in range(8):
            lhsT, rk = mixT_tile_fn(c)
            C.mm(ps[:, :], lhsT, wout[:, c, half * 512:(half + 1) * 512], c == 0, c == 7, reads=rk + ['wout'], writes=[pk])
        C.tt('dve', tb[:, half * 512:(half + 1) * 512], ps[:, :], gate_bc[:, j, half * 512:(half + 1) * 512], ALU.mult,
             reads=[pk, 'gate_bc'], writes=[kt])
    C.stt(tb[:], xt[:], float(ALPHA), tb[:], ALU.mult, ALU.add, reads=[kx, kt], writes=[kt])
    for half in range(2):
        P.op('dve', 'bn_stats', out=st6[:, half, :], in_=tb[:, half * 512:(half + 1) * 512], reads=[kt], writes=[ks])
    P.op('dve', 'bn_aggr', out=mv[:], in_=st6[:].rearrange("p a b -> p (a b)"), reads=[ks], writes=[ks])
    C.act(sm[:, 0:1], mv[:, 1:2], AF.Sqrt, reads=[ks], writes=[ks], bias=float(EPS), scale=1.0)
    C.recip(sm[:, 1:2], sm[:, 0:1], reads=[ks], writes=[ks])
    C.ts('dve', sm[:, 2:3], mv[:, 0:1], sm[:, 1:2], -1.0, ALU.mult, ALU.mult, reads=[ks], writes=[ks])
    C.act(tb[:], tb[:], AF.Identity, reads=[kt, ks], writes=[kt], scale=sm[:, 1:2], bias=sm[:, 2:3])
    C.tt('pool', tb[:], tb[:], lnbc[:, 0, :], ALU.mult, reads=[kt, 'lnbc'], writes=[kt])
    C.tt('pool', tb[:], tb[:], lnbc[:, 1, :], ALU.add, reads=[kt, 'lnbc'], writes=[kt])
    P.dma('sp', out_rows, tb[:], reads=[kt])


def alloc_epilogue(C, st):
    E = dict(xt=[], tb=[], st6=[], mv=[], sm=[])
    for i in range(2):
        E['xt'].append(C.sb(st, "e_xt%d" % i, [128, 1024], F32))
        E['tb'].append(C.sb(st, "e_tb%d" % i, [128, 1024], F32))
        E['st6'].append(C.sb(st, "e_st%d" % i, [128, 2, 6], F32))
        E['mv'].append(C.sb(st, "e_mv%d" % i, [128, 2], F32))
        E['sm'].append(C.sb(st, "e_sm%d" % i, [128, 4], F32))
    return E


def load_w_bf16(C, dst, dst_key, src, kcs, ncols):
    for kc in range(kcs):
        for c0 in range(0, ncols, 1024):
            c1 = min(ncols, c0 + 1024)
            C.P.dma('pool', dst[:, kc, c0:c1], src[kc * 128:(kc + 1) * 128, c0:c1], writes=[dst_key])


KV_BLOCKS = [(0, 256, 1)] + [(256 + 512 * i, 512, 0) for i in range(8)]
Q_PASSES = [[(0, 256, 1), (256, 512, 0), (768, 512, 0)], [(1280, 512, 0), (1792, 512, 0)]]
FULL_PASSES = [[(0, 256, 1), (256, 512, 0), (768, 512, 0)]] + [[(1280 + 1024 * i, 512, 0), (1792 + 1024 * i, 512, 0)] for i in range(3)]
NKT = TA // 128


def build_att():
    nc = bass.Bass("TRN2", target_bir_lowering=False)
    with ExitStack() as st:
        C = Ctx(nc, st)
        emit_att(C, st)
        C.P.finish()
    return nc


def emit_att(C, st, K=None, full=False):
    if True:
        P = C.P
        Q_PASSES_ = FULL_PASSES if full else Q_PASSES
        xall = C.dram_in("xall", [TA, D])
        xown = C.dram_in("xown", [NOWN, D])
        cc_d = C.dram_in("cc", [128, 16])
        modw_d = C.dram_in("modw", [D, 3 * D])
        modbcol_d = C.dram_in("modb_col", [128, 24])
        modbgate_d = C.dram_in("modb_gate", [1, D])
        lng_d = C.dram_in("lng", [1, D])
        lnb_d = C.dram_in("lnb", [1, D])
        wkv_d = C.dram_in("wkv", [D, 704])
        wq_d = C.dram_in("wq", [D, 2304])
        wuqA_d = C.dram_in("wuqA", [256, 768])
        wuqB_d = C.dram_in("wuqB", [256, 768])
        wuk_d = C.dram_in("wuk", [128, 512])
        wuv_d = C.dram_in("wuv", [128, 512])
        wout_d = C.dram_in("wout", [D, D])
        gcols_d = C.dram_in("gcols", [128, 8])
        tabKg_d = C.dram_in("tabKg", [2, 128, TA])
        tabKm_d = C.dram_in("tabKm", [2, 96, TA])
        tabQg_d = None if full else C.dram_in("tabQg", [2, 128, NOWN])
        tabQm_d = None if full else C.dram_in("tabQm", [2, 96, NOWN])
        ident_d = C.dram_in("ident", [128, 128])
        xo = C.dram_out("xo", [NOWN, D])
        if full:
            tabQg_d, tabQm_d = tabKg_d, tabKm_d

        K = K or emit_consts(C, st, ident_d)
        gcols = C.sb(st, "gcols", [128, 8], F32)
        P.dma('sp', gcols[:], gcols_d, writes=['gcols'])
        lnbc = C.sb(st, "lnbc", [128, 2, 1024], F32)
        P.dma('sp', lnbc[:, 0, :], lng_d[0:1, :].broadcast_to([128, D]), writes=['lnbc'])
        P.dma('sp', lnbc[:, 1, :], lnb_d[0:1, :].broadcast_to([128, D]), writes=['lnbc'])
        modcol, gate_bc = emit_modulation(C, st, K, cc_d, modw_d, modbcol_d, modbgate_d)

        ckvT = C.sb(st, "ckvT", [128, TA], BF16)
        krT = C.sb(st, "krT", [96, TA], BF16)
        kTg = C.sb(st, "kTg", [128, TA], BF16)
        Vg = C.sb(st, "Vg", [128, NKT, 192], BF16)
        P.op('pool', 'memset', Vg[:, :, 64:128], 1.0, writes=['Vg'])
        allps = Ring(C.psum)

        def rms_rope_chunk(s2, psA, pkA, psB, pkB, n, gA, gB, tab, tabk, dst, dst_key, tmp):
            sq, rt, t1, t2 = tmp
            C.act(sq[:, 0:n], psA[:, 0:n], AF.Square, reads=[pkA], writes=['sq'])
            ps2, pk2 = allps.next()
            C.mm(ps2[:, 0:n], K['onesbd'][:], sq[:, 0:n], True, True, reads=['onesbd', 'sq'], writes=[pk2])
            C.act(rt[:, 0:n], ps2[:, 0:n], AF.Sqrt, reads=[pk2], writes=['rt'], bias=float(EPS), scale=1.0 / 64)
            C.recip(rt[:, 0:n], rt[:, 0:n], reads=['rt'], writes=['rt'])
            C.stt(t1[:, 0:n], psA[:, 0:n], gA, tab[:, 0, 0:n], ALU.mult, ALU.mult, reads=[pkA, tabk, 'gcols'], writes=['t1'])
            C.stt(t2[:, 0:n], psB[:, 0:n], gB, tab[:, 1, 0:n], ALU.mult, ALU.mult, reads=[pkB, tabk, 'gcols'], writes=['t2'])
            C.tt('pool', t1[:, 0:n], t1[:, 0:n], t2[:, 0:n], ALU.add, reads=['t1', 't2'], writes=['t1'])
            C.tt('pool', dst, t1[:, 0:n], rt[:, 0:n], ALU.mult, reads=['t1', 'rt'], writes=[dst_key])

        with ExitStack() as s2:
            wkv = C.sb(s2, "wkv", [128, 8, 704], BF16)
            load_w_bf16(C, wkv, 'wkv', wkv_d, 8, 704)
            xs = C.sb(s2, "xs", [128, 4, 1024], F32)
            hT = [C.sb(s2, "hT%d" % i, [128, 8, 512], BF16) for i in range(2)]
            tabg = [C.sb(s2, "tabg%d" % i, [128, 2, 512], F32) for i in range(2)]
            tabm = [C.sb(s2, "tabm%d" % i, [96, 2, 512], F32) for i in range(2)]
            tmp = (C.sb(s2, "sq", [128, 512], BF16), C.sb(s2, "rt", [128, 512], F32),
                   C.sb(s2, "t1", [128, 512], F32), C.sb(s2, "t2", [128, 512], F32))
            for bi, (r0, n, j) in enumerate(KV_BLOCKS):
                h, hk = hT[bi % 2], ('hT', bi % 2)
                tg, tgk = tabg[bi % 2], ('tabg', bi % 2)
                tm, tmk = tabm[bi % 2], ('tabm', bi % 2)
                for a in range(2):
                    P.dma('sp', tg[:, a, 0:n], tabKg_d[a, :, r0:r0 + n], writes=[tgk])
                    P.dma('sp', tm[64:96, a, 0:n], tabKm_d[a, 64:96, r0:r0 + n], writes=[tmk])
                emit_hT(C, K, xs, 'xs', xall[r0:r0 + n, :], n, j, modcol, h, hk, allps)

                def proj(off, M):
                    ps, pk = allps.next()
                    for kc in range(8):
                        C.mm(ps[0:M, 0:n], wkv[:, kc, off:off + M], h[:, kc, 0:n], kc == 0, kc == 7,
                             reads=['wkv', hk], writes=[pk])
                    return ps, pk
                ps, pk = proj(0, 128)
                sq, rt, t1, t2 = tmp
                C.act(sq[:, 0:n], ps[:, 0:n], AF.Square, reads=[pk], writes=['sq'])
                ps2, pk2 = allps.next()
                C.mm(ps2[:, 0:n], K['ones_bf'][:], sq[:, 0:n], True, True, reads=['ones_bf', 'sq'], writes=[pk2])
                C.act(rt[:, 0:n], ps2[:, 0:n], AF.Sqrt, reads=[pk2], writes=['rt'], bias=float(EPS), scale=1.0 / 128)
                C.recip(rt[:, 0:n], rt[:, 0:n], reads=['rt'], writes=['rt'])
                C.stt(ckvT[:, r0:r0 + n], ps[:, 0:n], gcols[:, 2:3], rt[:, 0:n], ALU.mult, ALU.mult,
                      reads=[pk, 'rt', 'gcols'], writes=['ckvT'])
                psA, pkA = proj(128, 128)
                psB, pkB = proj(256, 128)
                rms_rope_chunk(s2, psA, pkA, psB, pkB, n, gcols[:, 5:6], gcols[:, 6:7], tg, tgk, kTg[:, r0:r0 + n], 'kTg', tmp)
                psA, pkA = proj(384, 96)
                psB, pkB = proj(480, 96)
                C.tt('dve', t1[64:96, 0:n], psA[64:96, 0:n], tm[64:96, 0, 0:n], ALU.mult, reads=[pkA, tmk], writes=['t1'])
                C.tt('dve', t2[64:96, 0:n], psB[64:96, 0:n], tm[64:96, 1, 0:n], ALU.mult, reads=[pkB, tmk], writes=['t2'])
                C.tt('pool', krT[64:96, r0:r0 + n], t1[64:96, 0:n], t2[64:96, 0:n], ALU.add, reads=['t1', 't2'], writes=['krT'])
                nt = n // 128
                ps, pk = allps.next()
                for tt in range(nt):
                    for kc in range(8):
                        C.mm(ps[:, tt * 128:(tt + 1) * 128], h[:, kc, tt * 128:(tt + 1) * 128], wkv[:, kc, 576:704],
                             kc == 0, kc == 7, reads=['wkv', hk], writes=[pk])
                t0 = r0 // 128
                for tt in range(nt):
                    C.copy('dve', Vg[:, t0 + tt, 0:64], ps[:, tt * 128:tt * 128 + 64], reads=[pk], writes=['Vg'])
                    C.copy('dve', Vg[:, t0 + tt, 128:192], ps[:, tt * 128 + 64:tt * 128 + 128], reads=[pk], writes=['Vg'])
            _drain(C)

        for pi, blocks in enumerate(Q_PASSES_):
            base = blocks[0][0]
            npass = sum(b[1] for b in blocks)
            with ExitStack() as sp:
                cqT = C.sb(sp, "cqT", [128, 2, 1280], BF16)
                qTg = C.sb(sp, "qTg", [128, 4, 1280], BF16)
                mixT = C.sb(sp, "mixT", [128, 8, 1280], BF16)
                with ExitStack() as s2:
                    wq = C.sb(s2, "wq", [128, 8, 2304], BF16)
                    load_w_bf16(C, wq, 'wq', wq_d, 8, 2304)
                    xs = C.sb(s2, "xs", [128, 4, 1024], F32)
                    hT = [C.sb(s2, "hT%d" % i, [128, 8, 512], BF16) for i in range(2)]
                    tabg = [C.sb(s2, "tabg%d" % i, [128, 2, 512], F32) for i in range(2)]
                    tmp = (C.sb(s2, "sq", [128, 512], BF16), C.sb(s2, "rt", [128, 512], F32),
                           C.sb(s2, "t1", [128, 512], F32), C.sb(s2, "t2", [128, 512], F32))
                    sq2 = C.sb(s2, "sq2", [128, 512], BF16)
                    for bi, (r0, n, j) in enumerate(blocks):
                        h, hk = hT[bi % 2], ('hT', bi % 2)
                        tg, tgk = tabg[bi % 2], ('tabg', bi % 2)
                        l0 = r0 - base
                        for a in range(2):
                            P.dma('sp', tg[:, a, 0:n], tabQg_d[a, :, r0:r0 + n], writes=[tgk])
                        emit_hT(C, K, xs, 'xs', xown[r0:r0 + n, :], n, j, modcol, h, hk, allps)

                        def proj(ci):
                            ps, pk = allps.next()
                            for kc in range(8):
                                C.mm(ps[:, 0:n], wq[:, kc, ci * 128:(ci + 1) * 128], h[:, kc, 0:n], kc == 0, kc == 7,
                                     reads=['wq', hk], writes=[pk])
                            return ps, pk
                        sq, rt, t1, t2 = tmp
                        ps0, pk0 = proj(0)
                        ps1, pk1 = proj(1)
                        C.act(sq[:, 0:n], ps0[:, 0:n], AF.Square, reads=[pk0], writes=['sq'])
                        C.act(sq2[:, 0:n], ps1[:, 0:n], AF.Square, reads=[pk1], writes=['sq2'])
                        ps2, pk2 = allps.next()
                        C.mm(ps2[:, 0:n], K['ones_bf'][:], sq[:, 0:n], True, False, reads=['ones_bf', 'sq'], writes=[pk2])
                        C.mm(ps2[:, 0:n], K['ones_bf'][:], sq2[:, 0:n], False, True, reads=['ones_bf', 'sq2'], writes=[pk2])
                        C.act(rt[:, 0:n], ps2[:, 0:n], AF.Sqrt, reads=[pk2], writes=['rt'], bias=float(EPS), scale=1.0 / 256)
                        C.recip(rt[:, 0:n], rt[:, 0:n], reads=['rt'], writes=['rt'])
                        C.stt(cqT[:, 0, l0:l0 + n], ps0[:, 0:n], gcols[:, 0:1], rt[:, 0:n], ALU.mult, ALU.mult,
                              reads=[pk0, 'rt', 'gcols'], writes=['cqT'])
                        C.stt(cqT[:, 1, l0:l0 + n], ps1[:, 0:n], gcols[:, 1:2], rt[:, 0:n], ALU.mult, ALU.mult,
                              reads=[pk1, 'rt', 'gcols'], writes=['cqT'])
                        for c in range(8):
                            ps, pk = proj(2 + c)
                            C.act(mixT[:, c, l0:l0 + n], ps[:, 0:n], AF.Silu, reads=[pk], writes=[('mixT', c)])
                        for c in range(4):
                            psA, pkA = proj(10 + c)
                            psB, pkB = proj(14 + c)
                            rms_rope_chunk(s2, psA, pkA, psB, pkB, n, gcols[:, 3:4], gcols[:, 4:5], tg, tgk,
                                           qTg[:, c, l0:l0 + n], ('qTg', c), tmp)
                    _drain(C)

                with ExitStack() as s2:
                    wuqA = C.sb(s2, "wuqA", [128, 2, 768], BF16)
                    wuqB = C.sb(s2, "wuqB", [128, 2, 768], BF16)
                    wuk = C.sb(s2, "wuk", [128, 1, 512], BF16)
                    wuv = C.sb(s2, "wuv", [128, 1, 512], BF16)
                    load_w_bf16(C, wuqA, 'wuqA', wuqA_d, 2, 768)
                    load_w_bf16(C, wuqB, 'wuqB', wuqB_d, 2, 768)
                    load_w_bf16(C, wuk, 'wuk', wuk_d, 1, 512)
                    load_w_bf16(C, wuv, 'wuv', wuv_d, 1, 512)
                    Kh = [C.sb(s2, "Kh%d" % i, [96, TA], BF16) for i in range(2)]
                    qh = [C.sb(s2, "qh%d" % i, [96, 1280], BF16) for i in range(2)]
                    Vp = [C.sb(s2, "Vp%d" % i, [128, NKT, 192], BF16) for i in range(2)]
                    for i in range(2):
                        P.op('pool', 'memset', Vp[i][:, :, 64:128], 1.0, writes=[('Vp', i)])
                    tabq = C.sb(s2, "tabq", [96, 2, 1280], F32)
                    for a in range(2):
                        P.dma('sp', tabq[64:96, a, 0:npass], tabQm_d[a, 64:96, base:base + npass], writes=['tabq'])
                    PT = [C.sb(s2, "PT%d" % i, [128, 512], BF16) for i in range(3)]
                    rden = [C.sb(s2, "rden%d" % i, [128, 512], F32) for i in range(2)]
                    on = [C.sb(s2, "on%d" % i, [128, 512], F32) for i in range(2)]
                    t1 = C.sb(s2, "a_t1", [96, 512], F32)
                    t2 = C.sb(s2, "a_t2", [96, 512], F32)
                    Sring = Ring(C.psum[0:3])
                    Oring = Ring(C.psum[3:5])
                    Mring = Ring(C.psum[5:8])
                    cnt = {'pt': 0, 'nrm': 0}

                    def attention(Kap, Kkeys, qbuf, qkeys, kdim, prow, scale, Vfn, Vkeys, nrow, drow, cm):
                        for (r0, n, j) in blocks:
                            l0 = r0 - base
                            nk = 2 if j == 1 else NKT
                            ops_, opk = Oring.next()
                            sl = [None] * nk
                            sl[0] = Sring.next()
                            C.mm(sl[0][0][:, 0:n], Kap(0), qbuf(l0, n), True, True, reads=Kkeys + qkeys, writes=[sl[0][1]])
                            for kt in range(nk):
                                if kt + 1 < nk:
                                    sl[kt + 1] = Sring.next()
                                    C.mm(sl[kt + 1][0][:, 0:n], Kap(kt + 1), qbuf(l0, n), True, True, reads=Kkeys + qkeys,
                                         writes=[sl[kt + 1][1]])
                                pi_ = cnt['pt'] % 3
                                cnt['pt'] += 1
                                C.act(PT[pi_][:, 0:n], sl[kt][0][:, 0:n], AF.Exp, reads=[sl[kt][1]], writes=[('PT', pi_)],
                                      scale=float(scale))
                                C.mm(ops_[:, 0:n], Vfn(kt), PT[pi_][:, 0:n], kt == 0, kt == nk - 1,
                                     reads=Vkeys + [('PT', pi_)], writes=[opk])
                            ni = cnt['nrm'] % 2
                            cnt['nrm'] += 1
                            C.recip(rden[ni][nrow, 0:n], ops_[drow, 0:n], reads=[opk], writes=[('rden', ni)])
                            C.tt('dve', on[ni][nrow, 0:n], ops_[nrow, 0:n], rden[ni][nrow, 0:n], ALU.mult,
                                 reads=[opk, ('rden', ni)], writes=[('on', ni)])
                            C.tt('pool', mixT[nrow, cm, l0:l0 + n], on[ni][nrow, 0:n], mixT[nrow, cm, l0:l0 + n], ALU.mult,
                                 reads=[('on', ni), ('mixT', cm)], writes=[('mixT', cm)])

                    hcount = 0
                    for jp in range(4):
                        vp, vk = Vp[jp % 2], ('Vp', jp % 2)
                        for t0 in range(0, NKT, 4):
                            nt = min(4, NKT - t0)
                            ps, pk = Mring.next()
                            for tt in range(nt):
                                C.mm(ps[:, tt * 128:(tt + 1) * 128], ckvT[:, (t0 + tt) * 128:(t0 + tt + 1) * 128],
                                     wuv[:, 0, jp * 128:(jp + 1) * 128], True, True, reads=['ckvT', 'wuv'], writes=[pk])
                            for tt in range(nt):
                                C.copy('dve', vp[:, t0 + tt, 0:64], ps[:, tt * 128:tt * 128 + 64], reads=[pk], writes=[vk])
                                C.copy('dve', vp[:, t0 + tt, 128:192], ps[:, tt * 128 + 64:tt * 128 + 128], reads=[pk], writes=[vk])
                        for hh in range(2):
                            hd = 2 * jp + hh
                            kh, kk = Kh[hcount % 2], ('Kh', hcount % 2)
                            q_, qk = qh[hcount % 2], ('qh', hcount % 2)
                            hcount += 1
                            for (r0, n, j) in KV_BLOCKS:
                                ps, pk = Mring.next()
                                C.mm(ps[0:64, 0:n], wuk[:, 0, hd * 64:(hd + 1) * 64], ckvT[:, r0:r0 + n], True, True,
                                     reads=['wuk', 'ckvT'], writes=[pk])
                                C.copy('dve', kh[0:64, r0:r0 + n], ps[0:64, 0:n], reads=[pk], writes=[kk])
                            C.copy('pool', kh[64:96, :], krT[64:96, :], reads=['krT'], writes=[kk])
                            for (r0, n, j) in blocks:
                                l0 = r0 - base
                                psA, pkA = Mring.next()
                                psB, pkB = Mring.next()
                                for i in range(2):
                                    C.mm(psA[0:96, 0:n], wuqA[:, i, hd * 96:(hd + 1) * 96], cqT[:, i, l0:l0 + n], i == 0, i == 1,
                                         reads=['wuqA', 'cqT'], writes=[pkA])
                                for i in range(2):
                                    C.mm(psB[0:96, 0:n], wuqB[:, i, hd * 96:(hd + 1) * 96], cqT[:, i, l0:l0 + n], i == 0, i == 1,
                                         reads=['wuqB', 'cqT'], writes=[pkB])
                                C.copy('dve', q_[0:64, l0:l0 + n], psA[0:64, 0:n], reads=[pkA], writes=[qk])
                                C.tt('dve', t1[64:96, 0:n], psA[64:96, 0:n], tabq[64:96, 0, l0:l0 + n], ALU.mult,
                                     reads=[pkA, 'tabq'], writes=['a_t1'])
                                C.tt('dve', t2[64:96, 0:n], psB[64:96, 0:n], tabq[64:96, 1, l0:l0 + n], ALU.mult,
                                     reads=[pkB, 'tabq'], writes=['a_t2'])
                                C.tt('pool', q_[64:96, l0:l0 + n], t1[64:96, 0:n], t2[64:96, 0:n], ALU.add,
                                     reads=['a_t1', 'a_t2'], writes=[qk])
                            nrow = slice(0, 64) if hh == 0 else slice(64, 128)
                            drow = slice(64, 128) if hh == 0 else slice(0, 64)
                            attention(lambda kt, kh=kh: kh[0:96, kt * 128:(kt + 1) * 128], [kk],
                                      lambda l0, n, q_=q_: q_[0:96, l0:l0 + n], [qk], 96, None, 96.0 ** -0.5,
                                      lambda kt, vp=vp, hh=hh: vp[:, kt, hh * 64:hh * 64 + 128], [vk], nrow, drow, jp)
                    for c in range(4):
                        for g in range(2):
                            rows = slice(64 * g, 64 * g + 64)
                            drow = slice(64, 128) if g == 0 else slice(0, 64)
                            attention(lambda kt, rows=rows: kTg[rows, kt * 128:(kt + 1) * 128], ['kTg'],
                                      lambda l0, n, rows=rows, c=c: qTg[rows, c, l0:l0 + n], [('qTg', c)], 64, None, 0.125,
                                      lambda kt, g=g: Vg[:, kt, g * 64:g * 64 + 128], ['Vg'], rows, drow, 4 + c)
                    _drain(C)

                with ExitStack() as s2:
                    wout = C.sb(s2, "wout", [128, 8, 1024], BF16)
                    load_w_bf16(C, wout, 'wout', wout_d, 8, 1024)
                    E = alloc_epilogue(C, s2)
                    ti = 0
                    for (r0, n, j) in blocks:
                        for tt in range(n // 128):
                            l0 = r0 - base + tt * 128
                            rr = r0 + tt * 128
                            emit_epilogue_tile(
                                C, E, lambda c, l0=l0: (mixT[:, c, l0:l0 + 128], [('mixT', c)]), wout,
                                xown[rr:rr + 128, :], xo[rr:rr + 128, :], j, gate_bc, lnbc, allps, ti)
                            ti += 1
                    _drain(C)


_CACHE = {}


def _get(name, fn):
    if name not in _CACHE:
        _CACHE[name] = fn()
    return _CACHE[name]


def run_att_layer(inp, l, x, ctx):
    sh, (tabKg, tabKm) = prep_att_layer(inp, l)
    nc = _get('att', build_att)
    in_maps = []
    for core in range(8):
        b, hf = core // 2, core % 2
        xall = np.concatenate([ctx[b], x[b]], axis=0)
        xown = np.concatenate([ctx[b], x[b, hf * OWN:(hf + 1) * OWN]], axis=0)
        sel = np.concatenate([np.arange(TC), TC + hf * OWN + np.arange(OWN)])
        m = dict(sh)
        m.update(xall=np.ascontiguousarray(xall), xown=np.ascontiguousarray(xown),
                 cc=_cc_cols(inp['c'][b], inp['c_ctx']), tabKg=tabKg, tabKm=tabKm,
                 tabQg=np.ascontiguousarray(tabKg[:, :, sel]), tabQm=np.ascontiguousarray(tabKm[:, :, sel]))
        in_maps.append(m)
    res = run_bass_kernel_spmd(nc, in_maps, core_ids=list(range(8)))
    xn = np.empty_like(x)
    cn = np.empty_like(ctx)
    for core in range(8):
        b, hf = core // 2, core % 2
        o = res.results[core]["xo"]
        xn[b, hf * OWN:(hf + 1) * OWN] = o[TC:]
        if hf == 0:
            cn[b] = o[:TC]
    return xn, cn


def _rec_consts():
    idx = np.arange(128)
    same = (idx[:, None] // 64) == (idx[None, :] // 64)
    cm = np.zeros((10, 128, 128), np.float32)
    cm[0] = same & (idx[:, None] <= idx[None, :])
    cm[1] = same & (idx[:, None] >= idx[None, :])
    cm[2] = (idx[:, None] < 64) * np.ones((1, 128))
    cm[3] = (idx[:, None] >= 64) * np.ones((1, 128))
    BIG = 30000.0
    cm[4] = np.where(same & (idx[None, :] < idx[:, None]), 0.0, BIG)
    cm[5] = np.where(same & (idx[None, :] > idx[:, None]), 0.0, BIG)
    cm[6] = np.where(same & (idx[:, None] <= idx[None, :]), 0.0, -BIG)
    cm[7] = np.where(same & (idx[:, None] >= idx[None, :]), 0.0, -BIG)
    cm[8] = same
    return cm


def prep_rec_layer(inp, l, hf):
    li = l // 2
    w_in = np.asarray(inp['rec_w_in'][li], np.float32)
    hs = [2 * hf, 2 * hf + 1]
    cols = []
    for base in (0, 512, 1024, 1536):
        for h in hs:
            cols.append(np.arange(base + h * 128, base + (h + 1) * 128))
    for base in (2064, 2576):
        for i in range(2):
            cols.append(np.arange(base + 256 * hf + 128 * i, base + 256 * hf + 128 * (i + 1)))
    wmain = w_in[:, np.concatenate(cols)]
    bcols = [2048 + d * 4 + h for d in range(2) for h in hs]
    acols = [2056 + d * 4 + h for d in range(2) for h in hs]
    wba = w_in[:, bcols + acols]
    gcw = np.asarray(inp['gdn_conv_w'][li], np.float32)
    lcw = np.asarray(inp['lru_conv_w'][li], np.float32)
    lcb = np.asarray(inp['lru_conv_b'][li], np.float32)
    convw = np.zeros((128, 8, 4), np.float32)
    for ci in range(6):
        convw[:, ci, :] = gcw[:, cols[ci]].T
    convb = np.zeros((128, 2), np.float32)
    for i in range(2):
        ch = 256 * hf + 128 * i + np.arange(128)
        convw[:, 6 + i, :] = lcw[:, ch].T
        convb[:, i] = lcb[ch]
    dtb = np.asarray(inp['gdn_dt_bias'][li], np.float32)
    alog = np.asarray(inp['gdn_a_log'][li], np.float32)
    dtb_row = np.tile(np.array([dtb[d, h] for d in range(2) for h in hs], np.float32), NKT)[None, :]
    alog_row = np.tile(np.array([alog[d, h] for d in range(2) for h in hs], np.float32), NKT)[None, :]
    gnorm = np.asarray(inp['gdn_norm'][li], np.float32)[:, None]
    gw = np.asarray(inp['lru_gate_w'][li], np.float32)
    gb = np.asarray(inp['lru_gate_b'][li], np.float32)
    lam = np.asarray(inp['lru_lambda'][li], np.float32)
    wgate = np.zeros((128, 8, 128), np.float32)
    gbcol = np.zeros((128, 8), np.float32)
    lamcol = np.zeros((128, 4), np.float32)
    for d in range(2):
        for i in range(2):
            ch = 256 * hf + 128 * i + np.arange(128)
            lamcol[:, d * 2 + i] = lam[d, ch]
            for g in range(2):
                e = (d * 2 + g) * 2 + i
                n0 = 4 * hf + 2 * i
                wgate[0:64, e, 0:64] = gw[d, g, n0]
                wgate[64:128, e, 64:128] = gw[d, g, n0 + 1]
                gbcol[:, e] = gb[d, g, ch]
    return dict(wmain=np.ascontiguousarray(wmain), wba=np.ascontiguousarray(wba), convw=convw, convb=convb,
                dtb=dtb_row, alog=alog_row, gnorm=gnorm, wgate=wgate, gbcol=gbcol, lamcol=lamcol, cmats=_rec_consts())


_STOP = {}


def _poff(r0):
    return r0 + 1 if r0 == 0 else r0 + 4


def _csrc(r0):
    return r0 if r0 == 0 else r0 + 3


def build_recA():
    nc = bass.Bass("TRN2", target_bir_lowering=False)
    with ExitStack() as st:
        C = Ctx(nc, st)
        emit_recA(C, st)
        C.P.finish()
    return nc


def emit_recA(C, st, K=None, mix_dst=None):
    if True:
        P = C.P
        xall = C.dram_in("xall", [TA, D])
        cc_d = C.dram_in("cc", [128, 16])
        modw_d = C.dram_in("modw", [D, 3 * D])
        modbcol_d = C.dram_in("modb_col", [128, 24])
        modbgate_d = C.dram_in("modb_gate", [1, D])
        ident_d = C.dram_in("ident", [128, 128])
        wmain_d = C.dram_in("wmain", [D, 1536])
        wba_d = C.dram_in("wba", [D, 8])
        convw_d = C.dram_in("convw", [128, 8, 4])
        convb_d = C.dram_in("convb", [128, 2])
        dtb_d = C.dram_in("dtb", [1, NKT * 4])
        alog_d = C.dram_in("alog", [1, NKT * 4])
        gnorm_d = C.dram_in("gnorm", [128, 1])
        wgate_d = C.dram_in("wgate", [128, 8, 128])
        gbcol_d = C.dram_in("gbcol", [128, 8])
        lamcol_d = C.dram_in("lamcol", [128, 4])
        cm_d = C.dram_in("cmats", [10, 128, 128])
        if mix_dst is None:
            mixo = C.dram_out("mixo", [4, 128, TA], BF16)
            mix_dst = lambda idx, c0, n: mixo[idx, :, c0:c0 + n]

        K = K or emit_consts(C, st, ident_d)
        modcol, _ = emit_modulation(C, st, K, cc_d, modw_d, modbcol_d, modbgate_d, need_gate=False)
        cms = C.sb(st, "cms", [128, 10, 128], F32)
        for i in range(9):
            P.dma('sp', cms[:, i, :], cm_d[i], writes=['cms'])
        convw = C.sb(st, "convw", [128, 8, 4], F32)
        convb = C.sb(st, "convb", [128, 2], F32)
        gnorm = C.sb(st, "gnorm", [128, 1], F32)
        gbcol = C.sb(st, "gbcol", [128, 8], F32)
        lamc = C.sb(st, "lamc", [128, 4], F32)
        c8 = C.sb(st, "c8", [128, 4], F32)
        P.dma('sp', convw[:], convw_d, writes=['cw'])
        P.dma('sp', convb[:], convb_d, writes=['cw'])
        P.dma('sp', gnorm[:], gnorm_d, writes=['cw'])
        P.dma('sp', gbcol[:], gbcol_d, writes=['cw'])
        P.dma('sp', lamc[:], lamcol_d, writes=['lamc'])
        wgate = C.sb(st, "wgate", [128, 8, 128], BF16)
        P.dma('pool', wgate[:], wgate_d, writes=['wgate'])
        C.act(c8[:], lamc[:], AF.Exp, reads=['lamc'], writes=['c8'], scale=-1.0)
        C.act(c8[:], c8[:], AF.Ln, reads=['c8'], writes=['c8'], bias=1.0, scale=1.0)
        C.ts('dve', c8[:], c8[:], -8.0, None, ALU.mult, ALU.bypass, reads=['c8'], writes=['c8'])
        beta = C.sb(st, "beta", [128, NKT, 4], F32)
        nbeta = C.sb(st, "nbeta", [128, NKT, 4], F32)
        gall = C.sb(st, "gall", [128, NKT, 8], F32)
        P.op('pool', 'memset', gall[:], 0.0, writes=['gall'])
        allps = Ring(C.psum)

        def build_hT(s2):
            hT_all = C.sb(s2, "hT_all", [128, 8, TA], BF16)
            with ExitStack() as s3:
                xs = C.sb(s3, "xs", [128, 4, 1024], F32)
                for (r0, n, j) in KV_BLOCKS:
                    emit_hT(C, K, xs, 'xs', xall[r0:r0 + n, :], n, j, modcol, None, ('hT', r0), allps,
                            dst=lambda kc, r0=r0, n=n: hT_all[:, kc, r0:r0 + n])
                _drain(C)
            return hT_all

        def conv_block(pre, cvb, ci, r0, n, bias=None):
            sb_ = _csrc(r0)
            C.ts('dve', cvb[:, 0:n], pre[:, sb_:sb_ + n], convw[:, ci, 0:1], None, ALU.mult, ALU.bypass,
                 reads=['pre', 'cw'], writes=['cvb'])
            for k in range(1, 4):
                C.stt(cvb[:, 0:n], pre[:, sb_ + k:sb_ + k + n], convw[:, ci, k:k + 1], cvb[:, 0:n], ALU.mult, ALU.add,
                      reads=['pre', 'cw', 'cvb'], writes=['cvb'])
            if bias is not None:
                C.ts('dve', cvb[:, 0:n], cvb[:, 0:n], bias, None, ALU.add, ALU.bypass, reads=['cvb', 'cw'], writes=['cvb'])

        def fill_pre(pre, wch, hT_all, ci):
            for (r0, n, j) in KV_BLOCKS:
                ps, pk = allps.next()
                for kc in range(8):
                    C.mm(ps[:, 0:n], wch[:, kc, :], hT_all[:, kc, r0:r0 + n], kc == 0, kc == 7,
                         reads=[('wch', ci % 2), ('hT', r0)], writes=[pk])
                C.copy('dve', pre[:, _poff(r0):_poff(r0) + n], ps[:, 0:n], reads=[pk], writes=['pre'])

        def load_chunk_w(wchs, ci):
            w = wchs[ci % 2]
            for kc in range(8):
                P.dma('pool', w[:, kc, :], wmain_d[kc * 128:(kc + 1) * 128, ci * 128:(ci + 1) * 128], writes=[('wch', ci % 2)])
            return w

        with ExitStack() as s2:
            hT_all = build_hT(s2)
            wchs = [C.sb(s2, "wch%d" % i, [128, 8, 128], BF16) for i in range(2)]
            wba = C.sb(s2, "wba", [128, 8, 8], BF16)
            load_w_bf16(C, wba, 'wba', wba_d, 8, 8)
            ba = C.sb(s2, "ba", [128, NKT, 8], F32)
            dtb = C.sb(s2, "dtb", [128, NKT, 4], F32)
            nea = C.sb(s2, "nea", [128, NKT, 4], F32)
            t4a = C.sb(s2, "t4a", [128, NKT, 4], F32)
            t4b = C.sb(s2, "t4b", [128, NKT, 4], F32)
            P.dma('sp', dtb[:].rearrange("p a b -> p (a b)"), dtb_d[0:1, :].broadcast_to([128, NKT * 4]), writes=['dtb'])
            P.dma('sp', nea[:].rearrange("p a b -> p (a b)"), alog_d[0:1, :].broadcast_to([128, NKT * 4]), writes=['nea'])
            ps, pk = allps.next()
            for t in range(NKT):
                for kc in range(8):
                    C.mm(ps[:, 8 * t:8 * t + 8], hT_all[:, kc, t * 128:(t + 1) * 128], wba[:, kc, :], kc == 0, kc == 7,
                         reads=['wba'] + [('hT', b[0]) for b in KV_BLOCKS], writes=[pk])
            C.copy('dve', ba[:].rearrange("p a b -> p (a b)"), ps[:, 0:NKT * 8], reads=[pk], writes=['ba'])
            C.act(beta[:], ba[:, :, 0:4], AF.Sigmoid, reads=['ba'], writes=['beta'])
            C.ts('dve', nbeta[:], beta[:], -1.0, None, ALU.mult, ALU.bypass, reads=['beta'], writes=['nbeta'])
            C.tt('dve', t4a[:], ba[:, :, 4:8], dtb[:], ALU.add, reads=['ba', 'dtb'], writes=['t4a'])
            C.act(t4b[:], t4a[:], AF.Abs, reads=['t4a'], writes=['t4b'])
            C.act(t4b[:], t4b[:], AF.Exp, reads=['t4b'], writes=['t4b'], scale=-1.0)
            C.act(t4b[:], t4b[:], AF.Ln, reads=['t4b'], writes=['t4b'], bias=1.0, scale=1.0)
            C.ts('dve', t4a[:], t4a[:], 0.0, None, ALU.max, ALU.bypass, reads=['t4a'], writes=['t4a'])
            C.tt('dve', t4a[:], t4a[:], t4b[:], ALU.add, reads=['t4a', 't4b'], writes=['t4a'])
            C.act(nea[:], nea[:], AF.Exp, reads=['nea'], writes=['nea'])
            C.ts('dve', nea[:], nea[:], -1.0, None, ALU.mult, ALU.bypass, reads=['nea'], writes=['nea'])
            C.tt('dve', gall[:, :, 0:4], t4a[:], nea[:], ALU.mult, reads=['t4a', 'nea', 'gall'], writes=['gall'])
            pre = C.sb(s2, "pre", [128, TA + 6], F32)
            P.op('pool', 'memset', pre[:], 0.0, writes=['pre'])
            grs = C.sb(s2, "grs", [128, TA], BF16)
            hfw = C.sb(s2, "hfw", [128, TA], F32)
            hbw = [C.sb(s2, "hbw%d" % i, [128, 512], F32) for i in range(2)]
            hctx = C.sb(s2, "hctx", [128, 1], F32)
            L = {}
            for nm in ['xlb', 'r', 'ii', 'a', 'a2', 'bb']:
                L[nm] = C.sb(s2, "l_" + nm, [128, 512], F32)
            xlbf = C.sb(s2, "l_xlbf", [128, 512], BF16)
            ymix = [C.sb(s2, "ymix%d" % i, [128, 512], BF16) for i in range(2)]
            for i in range(2):
                wx = load_chunk_w(wchs, 8 + i)
                fill_pre(pre, wx, hT_all, 8 + i)
                wg = load_chunk_w(wchs, 10 + i)
                for (r0, n, j) in KV_BLOCKS:
                    ps, pk = allps.next()
                    for kc in range(8):
                        C.mm(ps[:, 0:n], wg[:, kc, :], hT_all[:, kc, r0:r0 + n], kc == 0, kc == 7,
                             reads=[('wch', (10 + i) % 2), ('hT', r0)], writes=[pk])
                    C.act(grs[:, r0:r0 + n], ps[:, 0:n], AF.Silu, reads=[pk], writes=['grs'])
                for d in range(2):
                    order = KV_BLOCKS if d == 0 else [KV_BLOCKS[0]] + KV_BLOCKS[:0:-1]
                    for bi, (r0, n, j) in enumerate(order):
                        conv_block(pre, L['xlb'], 6 + i, r0, n, bias=convb[:, i:i + 1])
                        C.copy('pool', xlbf[:, 0:n], L['xlb'][:, 0:n], reads=['cvb'], writes=['xlbf'])
                        er, ei = (d * 2 + 0) * 2 + i, (d * 2 + 1) * 2 + i
                        psr, pkr = allps.next()
                        C.mm(psr[:, 0:n], wgate[:, er, :], xlbf[:, 0:n], True, True, reads=['wgate', 'xlbf'], writes=[pkr])
                        psi, pki = allps.next()
                        C.mm(psi[:, 0:n], wgate[:, ei, :], xlbf[:, 0:n], True, True, reads=['wgate', 'xlbf'], writes=[pki])
                        C.act(L['r'][:, 0:n], psr[:, 0:n], AF.Sigmoid, reads=[pkr, 'cw'], writes=['l_r'], bias=gbcol[:, er:er + 1], scale=1.0)
                        C.act(L['ii'][:, 0:n], psi[:, 0:n], AF.Sigmoid, reads=[pki, 'cw'], writes=['l_ii'], bias=gbcol[:, ei:ei + 1], scale=1.0)
                        C.ts('dve', L['r'][:, 0:n], L['r'][:, 0:n], c8[:, d * 2 + i:d * 2 + i + 1], None, ALU.mult, ALU.bypass,
                             reads=['l_r', 'c8'], writes=['l_r'])
                        C.act(L['a'][:, 0:n], L['r'][:, 0:n], AF.Exp, reads=['l_r'], writes=['l_a'])
                        C.act(L['a2'][:, 0:n], L['r'][:, 0:n], AF.Exp, reads=['l_r'], writes=['l_a2'], scale=2.0)
                        C.ts('dve', L['a2'][:, 0:n], L['a2'][:, 0:n], -1.0, 1.0, ALU.mult, ALU.add, reads=['l_a2'], writes=['l_a2'])
                        C.act(L['a2'][:, 0:n], L['a2'][:, 0:n], AF.Sqrt, reads=['l_a2'], writes=['l_a2'])
                        C.tt('pool', L['ii'][:, 0:n], L['ii'][:, 0:n], L['xlb'][:, 0:n], ALU.mult, reads=['l_ii', 'cvb'], writes=['l_ii'])
                        C.tt('pool', L['bb'][:, 0:n], L['a2'][:, 0:n], L['ii'][:, 0:n], ALU.mult, reads=['l_a2', 'l_ii'], writes=['l_bb'])
                        if d == 0:
                            init = 0.0 if bi == 0 else hfw[:, r0 - 1:r0]
                            P.op('dve', 'tensor_tensor_scan', out=hfw[:, r0:r0 + n], data0=L['a'][:, 0:n], data1=L['bb'][:, 0:n],
                                 initial=init, op0=ALU.mult, op1=ALU.add, reads=['l_a', 'l_bb', 'hfw'], writes=['hfw'])
                        else:
                            hb, hk = hbw[bi % 2], ('hbw', bi % 2)
                            if bi == 0:
                                init = 0.0
                            elif bi == 1:
                                init = hctx[:, 0:1]
                            else:
                                init = hbw[(bi - 1) % 2][:, 0:1]
                            P.op('dve', 'tensor_tensor_scan', out=hb[:, 0:n][:, ::-1], data0=L['a'][:, 0:n][:, ::-1],
                                 data1=L['bb'][:, 0:n][:, ::-1], initial=init, op0=ALU.mult, op1=ALU.add,
                                 reads=['l_a', 'l_bb', ('hbw', (bi - 1) % 2), 'hctx'], writes=[hk])
                            if bi == 0:
                                C.copy('dve', hctx[:, 0:1], hb[:, 0:1], reads=[hk], writes=['hctx'])
                            ym, yk = ymix[bi % 2], ('ymix', bi % 2)
                            C.tt('dve', L['xlb'][:, 0:n], hfw[:, r0:r0 + n], hb[:, 0:n], ALU.add, reads=['hfw', hk, 'cvb'], writes=['cvb'])
                            C.tt('pool', ym[:, 0:n], L['xlb'][:, 0:n], grs[:, r0:r0 + n], ALU.mult, reads=['cvb', 'grs'], writes=[yk])
                            P.dma('sp', mix_dst(2 + i, r0, n), ym[:, 0:n], reads=[yk])
            _drain(C)

        qnT = C.sb(st, "qnT", [128, 2, TA], BF16)
        knT = C.sb(st, "knT", [128, 2, TA], BF16)
        ktm = C.sb(st, "ktm", [128, 2, NKT, 128], BF16)
        vtm = C.sb(st, "vtm", [128, 2, NKT, 128], BF16)
        zsT = C.sb(st, "zsT", [128, 2, TA], BF16)
        with ExitStack() as s2:
            hT_all = build_hT(s2)
            wchs = [C.sb(s2, "wch%d" % i, [128, 8, 128], BF16) for i in range(2)]
            pre = C.sb(s2, "pre", [128, TA + 6], F32)
            P.op('pool', 'memset', pre[:], 0.0, writes=['pre'])
            cvb = C.sb(s2, "cvb", [128, 512], F32)
            sq = C.sb(s2, "g_sq", [128, 512], BF16)
            rt = C.sb(s2, "g_rt", [128, 512], F32)
            knf = C.sb(s2, "g_knf", [128, 512], F32)
            for hh in range(2):
                wz = load_chunk_w(wchs, 6 + hh)
                for (r0, n, j) in KV_BLOCKS:
                    ps, pk = allps.next()
                    for kc in range(8):
                        C.mm(ps[:, 0:n], wz[:, kc, :], hT_all[:, kc, r0:r0 + n], kc == 0, kc == 7,
                             reads=[('wch', (6 + hh) % 2), ('hT', r0)], writes=[pk])
                    C.act(zsT[:, hh, r0:r0 + n], ps[:, 0:n], AF.Silu, reads=[pk], writes=['zsT'])
            for ci in range(6):
                kind, hh = ci // 2, ci % 2
                wc = load_chunk_w(wchs, ci)
                fill_pre(pre, wc, hT_all, ci)
                for (r0, n, j) in KV_BLOCKS:
                    conv_block(pre, cvb, ci, r0, n)
                    C.act(cvb[:, 0:n], cvb[:, 0:n], AF.Silu, reads=['cvb'], writes=['cvb'])
                    src = cvb
                    if kind < 2:
                        C.act(sq[:, 0:n], cvb[:, 0:n], AF.Square, reads=['cvb'], writes=['g_sq'])
                        ps2, pk2 = allps.next()
                        C.mm(ps2[:, 0:n], K['ones_bf'][:], sq[:, 0:n], True, True, reads=['ones_bf', 'g_sq'], writes=[pk2])
                        C.act(rt[:, 0:n], ps2[:, 0:n], AF.Sqrt, reads=[pk2], writes=['g_rt'], bias=float(EPS), scale=1.0)
                        C.recip(rt[:, 0:n], rt[:, 0:n], reads=['g_rt'], writes=['g_rt'])
                        if kind == 0:
                            C.stt(qnT[:, hh, r0:r0 + n], cvb[:, 0:n], float(128.0 ** -0.5), rt[:, 0:n], ALU.mult, ALU.mult,
                                  reads=['cvb', 'g_rt'], writes=['qnT'])
                            continue
                        C.tt('dve', knf[:, 0:n], cvb[:, 0:n], rt[:, 0:n], ALU.mult, reads=['cvb', 'g_rt'], writes=['g_knf'])
                        C.copy('pool', knT[:, hh, r0:r0 + n], knf[:, 0:n], reads=['g_knf'], writes=['knT'])
                        src = knf
                    skey = 'g_knf' if kind == 1 else 'cvb'
                    dstm, dkey = (ktm, 'ktm') if kind == 1 else (vtm, 'vtm')
                    ps, pk = allps.next()
                    nt = n // 128
                    for tt in range(nt):
                        P.op('pe', 'transpose', ps[:, tt * 128:(tt + 1) * 128], src[:, tt * 128:(tt + 1) * 128], K['ident'][:],
                             reads=[skey, 'ident'], writes=[pk])
                    for tt in range(nt):
                        C.copy('dve' if tt % 2 == 0 else 'act', dstm[:, hh, r0 // 128 + tt, :], ps[:, tt * 128:(tt + 1) * 128],
                               reads=[pk], writes=[dkey]) if tt % 2 == 0 else \
                            C.act(dstm[:, hh, r0 // 128 + tt, :], ps[:, tt * 128:(tt + 1) * 128], AF.Copy, reads=[pk], writes=[dkey])
            _drain(C)

        with ExitStack() as s2:
            oacc = C.sb(s2, "oacc", [128, 2, NKT, 128], F32)
            regs = [(C.psum[bnk][0][:, 0:128], C.psum[bnk][1]) for bnk in range(8)]
            RR = Ring(regs)
            NS = 3
            S = {}
            for m in range(4):
                S[m] = dict(
                    wT=[C.sb(s2, "wT%d_%d" % (m, i), [128, 128], BF16) for i in range(NS)],
                    aT=[C.sb(s2, "aT%d_%d" % (m, i), [128, 128], BF16) for i in range(NS)],
                    u=[C.sb(s2, "u%d_%d" % (m, i), [128, 128], F32) for i in range(NS)],
                    kd=[C.sb(s2, "kd%d_%d" % (m, i), [128, 2, 128], BF16) for i in range(NS)],
                    gsm=[C.sb(s2, "gsm%d_%d" % (m, i), [128, 10], F32) for i in range(NS)],
                    Z=[C.sb(s2, "Z%d_%d" % (m, i), [128, 128], F32) for i in range(2)],
                    ZT=[C.sb(s2, "ZT%d_%d" % (m, i), [128, 128], F32) for i in range(2)],
                    Y=[C.sb(s2, "Y%d_%d" % (m, i), [128, 128], F32) for i in range(2)],
                    grep=C.sb(s2, "grep%d" % m, [128, 128], F32),
                    DS=C.sb(s2, "DS%d" % m, [128, 128], F32),
                    DT=C.sb(s2, "DT%d" % m, [128, 128], F32),
                    bv=C.sb(s2, "bv%d" % m, [128, 128], F32),
                    kbg=C.sb(s2, "kbg%d" % m, [128, 128], F32),
                    Sf=[C.sb(s2, "Sf%d_%d" % (m, i), [128, 128], F32) for i in range(2)],
                    Sb=[C.sb(s2, "Sb%d_%d" % (m, i), [128, 128], BF16) for i in range(2)],
                    vn=[C.sb(s2, "vn%d_%d" % (m, i), [128, 128], BF16) for i in range(2)],
                    tB=C.sb(s2, "tB%d" % m, [128, 128], F32),
                    oo=C.sb(s2, "oo%d" % m, [128, 128], F32),
                    cur=0, nblk=0,
                )
                P.op('dve', 'memset', S[m]['Sf'][0][:], 0.0, writes=[('Sf', m, 0)])
                P.op('dve', 'memset', S[m]['Sb'][0][:], 0.0, writes=[('Sb', m, 0)])
                for i_ in range(2):
                    P.op('dve', 'memset', S[m]['vn'][i_][:], 0.0, writes=[('vn', m, i_)])
            identf = K['ident']

            def prep(m, t, sl):
                d, hh = m // 2, m % 2
                X = S[m]
                kn = knT[:, hh, t * 128:(t + 1) * 128]
                qn = qnT[:, hh, t * 128:(t + 1) * 128]
                kt_ = ktm[:, hh, t, :]
                vt_ = vtm[:, hh, t, :]
                gcol, bcol, nbcol = gall[:, t, m:m + 1], beta[:, t, m:m + 1], nbeta[:, t, m:m + 1]
                tri, penS, penT = cms[:, d, :], cms[:, 4 + d, :], cms[:, 6 + d, :]
                gsm, gk = X['gsm'][sl], ('gsm', m, sl)
                bank, bkey = C.psum[m]
                Q = lambda i: bank[:, i * 128:(i + 1) * 128]
                pKK, kKK = Q(0), bkey
                C.mm(pKK, kn, kn, True, True, reads=['knT'], writes=[kKK])
                pQK, kQK = Q(1), bkey
                C.mm(pQK, kn, qn, True, True, reads=['knT', 'qnT'], writes=[kQK])
                pg, kg = Q(2), bkey
                for ci_, lh in enumerate([tri, cms[:, 8, :], cms[:, 2, :], cms[:, 3, :]]):
                    C.mm(pg[:, 2 * ci_:2 * ci_ + 2], lh, gall[:, t, m:m + 2], True, True, reads=['cms', 'gall'], writes=[kg])
                C.ts('dve', X['grep'][:], K['ones_f'][:], gcol, None, ALU.mult, ALU.bypass, reads=['ones_f', 'gall'], writes=[('grep', m)])
                yield
                C.copy('dve', gsm[:, 0:4], pg[:, 0:8:2], reads=[kg], writes=[gk])
                yield
                pD, kD = Q(2), bkey
                C.mm(pD, X['grep'][:], tri, True, False, reads=[('grep', m), 'cms'], writes=[kD])
                C.mm(pD, identf[:], penS, False, True, reads=['ident', 'cms'], writes=[kD])
                pDT, kDT = Q(3), bkey
                C.mm(pDT, X['grep'][:], tri, True, False, reads=[('grep', m), 'cms'], writes=[kDT])
                C.mm(pDT, identf[:], penT, False, True, reads=['ident', 'cms'], writes=[kDT])
                yield
                C.act(gsm[:, 4:5], gsm[:, 0:1], AF.Exp, reads=[gk], writes=[gk])
                C.act(gsm[:, 5:6], gsm[:, 0:1], AF.Exp, reads=[gk], writes=[gk], scale=-1.0, bias=gsm[:, 1:2])
                C.act(gsm[:, 7:9], gsm[:, 2:4], AF.Exp, reads=[gk], writes=[gk])
                C.tt('dve', gsm[:, 6:7], bcol, gsm[:, 4:5], ALU.mult, reads=[gk, 'beta'], writes=[gk])
                yield
                C.ts('dve', X['DS'][:], pD, gsm[:, 0:1], 0.0, ALU.subtract, ALU.max, reads=[kD, gk], writes=[('DS', m)])
                C.act(X['DS'][:], X['DS'][:], AF.Exp, reads=[('DS', m)], writes=[('DS', m)], scale=-1.0)
                C.ts('dve', X['DT'][:], pDT, gsm[:, 0:1], 0.0, ALU.subtract, ALU.min, reads=[kDT, gk], writes=[('DT', m)])
                yield
                C.act(X['DT'][:], X['DT'][:], AF.Exp, reads=[('DT', m)], writes=[('DT', m)])
                yield
                Z, ZT, Y = X['Z'], X['ZT'], X['Y']
                zk = lambda i: ('Z', m, i)
                ztk = lambda i: ('ZT', m, i)
                yk = lambda i: ('Y', m, i)
                C.stt(ZT[0][:], pKK, nbcol, X['DS'][:], ALU.mult, ALU.mult, reads=[kKK, 'nbeta', ('DS', m)], writes=[ztk(0)])
                C.tt('dve', X['aT'][sl][:], pQK, X['DT'][:], ALU.mult, reads=[kQK, ('DT', m)], writes=[('aT', m, sl)])
                yield
                pZ, kZ = Q(0), bkey
                P.op('pe', 'transpose', pZ, ZT[0][:], identf[:], reads=[ztk(0), 'ident'], writes=[kZ])
                yield
                C.act(Z[0][:], pZ, AF.Copy, reads=[kZ], writes=[zk(0)])
                C.tt('dve', Y[0][:], pZ, identf[:], ALU.add, reads=[kZ, 'ident'], writes=[yk(0)])
                yield
                for k in range(5):
                    cur, nxt = k % 2, 1 - k % 2
                    p1, k1 = Q(0), bkey
                    C.mm(p1, Z[cur][:], ZT[cur][:], True, True, reads=[zk(cur), ztk(cur)], writes=[k1])
                    if k < 4:
                        p2, k2 = Q(1), bkey
                        C.mm(p2, ZT[cur][:], Z[cur][:], True, True, reads=[zk(cur), ztk(cur)], writes=[k2])
                        yield
                    if k == 4:
                        yield
                    C.act(ZT[nxt][:], p1, AF.Copy, reads=[k1], writes=[ztk(nxt)])
                    if k == 4:
                        yield
                    if k < 4:
                        C.copy('dve', Z[nxt][:], p2, reads=[k2], writes=[zk(nxt)])
                        yield
                    p3, k3 = Q(2), bkey
                    C.mm(p3, ZT[nxt][:], Y[cur][:], True, True, reads=[ztk(nxt), yk(cur)], writes=[k3])
                    yield
                    C.tt('dve', Y[nxt][:], Y[cur][:], p3, ALU.add, reads=[yk(cur), k3], writes=[yk(nxt)])
                    yield
                Yf, Yk = Y[1], yk(1)
                C.ts('dve', X['bv'][:], vt_, bcol, None, ALU.mult, ALU.bypass, reads=['vtm', 'beta'], writes=[('bv', m)])
                C.ts('dve', X['kbg'][:], kt_, gsm[:, 6:7], None, ALU.mult, ALU.bypass, reads=['ktm', gk], writes=[('kbg', m)])
                pU, kU = Q(1), bkey
                C.mm(pU, Yf[:], X['bv'][:], True, True, reads=[Yk, ('bv', m)], writes=[kU])
                pW, kW = Q(2), bkey
                C.mm(pW, X['kbg'][:], Yf[:], True, True, reads=[Yk, ('kbg', m)], writes=[kW])
                yield
                C.copy('dve', X['u'][sl][:], pU, reads=[kU], writes=[('u', m, sl)])
                C.act(X['wT'][sl][:], pW, AF.Copy, reads=[kW], writes=[('wT', m, sl)])
                for blk_ in range(2):
                    C.tt('dve', gsm[:, 9:10], gsm[:, 5:6], cms[:, 2 + blk_, 0:1], ALU.mult, reads=[gk, 'cms'], writes=[gk])
                    C.ts('dve', X['kd'][sl][:, blk_, :], kt_, gsm[:, 9:10], None, ALU.mult, ALU.bypass, reads=['ktm', gk],
                         writes=[('kd', m, sl)])

            def seq(m, t, sl, first):
                d, hh = m // 2, m % 2
                X = S[m]
                gsm, gk = X['gsm'][sl], ('gsm', m, sl)
                bank, bkey = C.psum[4 + m]
                Q = lambda i: bank[:, i * 128:(i + 1) * 128]
                for blk in ([0, 1] if d == 0 else [1, 0]):
                    R = slice(64 * blk, 64 * blk + 64)
                    cur = X['cur']
                    nxt = 1 - cur
                    vi = X['nblk'] % 2
                    X['nblk'] += 1
                    vn, vk = X['vn'][vi], ('vn', m, vi)
                    Sb, Sbk = X['Sb'][cur], ('Sb', m, cur)
                    p1, k1 = Q(0), bkey
                    C.mm(p1, X['wT'][sl][:], Sb[:], True, True, reads=[('wT', m, sl), Sbk], writes=[k1])
                    yield
                    C.tt('dve', vn[R, :], X['u'][sl][R, :], p1[R, :], ALU.subtract, reads=[('u', m, sl), k1], writes=[vk])
                    yield
                    pA, kA = Q(1), bkey
                    C.mm(pA, qnT[:, hh, t * 128:(t + 1) * 128], Sb[:], True, True, reads=['qnT', Sbk], writes=[kA])
                    pB, kB = Q(2), bkey
                    C.mm(pB, X['aT'][sl][:], vn[:], True, True, reads=[('aT', m, sl), vk], writes=[kB])
                    pS, kS = Q(3), bkey
                    C.mm(pS, X['kd'][sl][:, blk, :], vn[:], True, True, reads=[('kd', m, sl), vk], writes=[kS])
                    yield
                    C.stt(X['Sf'][nxt][:], X['Sf'][cur][:], gsm[:, 7 + blk:8 + blk], pS, ALU.mult, ALU.add,
                          reads=[('Sf', m, cur), gk, kS], writes=[('Sf', m, nxt)])
                    C.act(X['Sb'][nxt][:], X['Sf'][nxt][:], AF.Copy, reads=[('Sf', m, nxt)], writes=[('Sb', m, nxt)])
                    X['cur'] = nxt
                    C.act(X['tB'][R, :], pB[R, :], AF.Copy, reads=[kB], writes=[('tB', m)])
                    yield
                    ok = ('oacc', hh, t)
                    if first:
                        C.stt(oacc[R, hh, t, :], pA[R, :], gsm[R, 4:5], X['tB'][R, :], ALU.mult, ALU.add,
                              reads=[kA, gk, ('tB', m)], writes=[ok])
                    else:
                        C.stt(X['oo'][R, :], pA[R, :], gsm[R, 4:5], X['tB'][R, :], ALU.mult, ALU.add,
                              reads=[kA, gk, ('tB', m)], writes=[('oo', m)])
                        C.tt('pool', oacc[R, hh, t, :], oacc[R, hh, t, :], X['oo'][R, :], ALU.add, reads=[ok, ('oo', m)], writes=[ok])

            ORD = [list(range(NKT)), [1, 0] + list(range(NKT - 1, 1, -1))]
            STEP = [{t: i for i, t in enumerate(o)} for o in ORD]

            def tile_of(m, s):
                return ORD[m // 2][s]
            def interleave(gens):
                gens = list(gens)
                while gens:
                    for g in list(gens):
                        try:
                            next(g)
                        except StopIteration:
                            gens.remove(g)
            interleave([prep(m, tile_of(m, 0), 0) for m in range(4)])
            for s in range(NKT):
                gens = []
                for m in range(4):
                    t = tile_of(m, s)
                    d_ = m // 2
                    first = STEP[d_][t] < STEP[1 - d_][t] or (STEP[d_][t] == STEP[1 - d_][t] and d_ == 0)
                    gens.append(seq(m, t, s % NS, first))
                    if s + 1 < NKT:
                        gens.append(prep(m, tile_of(m, s + 1), (s + 1) % NS))
                interleave(gens)
            ssq = C.sb(s2, "f_ssq", [128, 2], F32)
            junk = C.sb(s2, "f_junk", [128, 128], F32)
            on = C.sb(s2, "f_on", [128, 128], F32)
            stg = [C.sb(s2, "f_stg%d" % i, [128, 512], BF16) for i in range(2)]
            gi = 0
            for hh in range(2):
                for t0 in range(0, NKT, 4):
                    nt = min(4, NKT - t0)
                    sg, sk = stg[gi % 2], ('stg', gi % 2)
                    gi += 1
                    for tt in range(nt):
                        t = t0 + tt
                        ok = ('oacc', hh, t)
                        C.act(junk[:], oacc[:, hh, t, :], AF.Square, reads=[ok], writes=['f_junk', 'f_ssq'], accum_out=ssq[:, 0:1])
                        C.act(ssq[:, 1:2], ssq[:, 0:1], AF.Sqrt, reads=['f_ssq'], writes=['f_ssq'], bias=float(EPS), scale=1.0 / 128)
                        C.recip(ssq[:, 1:2], ssq[:, 1:2], reads=['f_ssq'], writes=['f_ssq'])
                        C.ts('dve', on[:], oacc[:, hh, t, :], ssq[:, 1:2], None, ALU.mult, ALU.bypass, reads=[ok, 'f_ssq'], writes=['f_on'])
                        pT, kT = RR.next()
                        P.op('pe', 'transpose', pT, on[:], identf[:], reads=['f_on', 'ident'], writes=[kT])
                        C.stt(sg[:, tt * 128:(tt + 1) * 128], pT, gnorm[:, 0:1], zsT[:, hh, t * 128:(t + 1) * 128], ALU.mult, ALU.mult,
                              reads=[kT, 'cw', 'zsT'], writes=[sk])
                    P.dma('sp', mix_dst(hh, t0 * 128, nt * 128), sg[:, 0:nt * 128], reads=[sk])
            _drain(C)


def build_recB():
    nc = bass.Bass("TRN2", target_bir_lowering=False)
    with ExitStack() as st:
        C = Ctx(nc, st)
        emit_recB(C, st)
        C.P.finish()
    return nc


def emit_recB(C, st, K=None, ntok=NOWN):
    if True:
        P = C.P
        xown = C.dram_in("xown", [ntok, D])
        mixT_d = C.dram_in("mixT", [8, 128, ntok], BF16)
        cc_d = C.dram_in("cc", [128, 16])
        modw_d = C.dram_in("modw", [D, 3 * D])
        modbcol_d = C.dram_in("modb_col", [128, 24])
        modbgate_d = C.dram_in("modb_gate", [1, D])
        lng_d = C.dram_in("lng", [1, D])
        lnb_d = C.dram_in("lnb", [1, D])
        wout_d = C.dram_in("wout", [D, D])
        ident_d = C.dram_in("ident", [128, 128])
        xo = C.dram_out("xo", [ntok, D])
        K = K or emit_consts(C, st, ident_d)
        lnbc = C.sb(st, "lnbc", [128, 2, 1024], F32)
        P.dma('sp', lnbc[:, 0, :], lng_d[0:1, :].broadcast_to([128, D]), writes=['lnbc'])
        P.dma('sp', lnbc[:, 1, :], lnb_d[0:1, :].broadcast_to([128, D]), writes=['lnbc'])
        _, gate_bc = emit_modulation(C, st, K, cc_d, modw_d, modbcol_d, modbgate_d, need_cols=False)
        wout = C.sb(st, "wout", [128, 8, 1024], BF16)
        load_w_bf16(C, wout, 'wout', wout_d, 8, 1024)
        mixT = C.sb(st, "mixT", [128, 8, ntok], BF16)
        for c in range(8):
            P.dma('sp', mixT[:, c, :], mixT_d[c], writes=[('mixT', c)])
        E = alloc_epilogue(C, st)
        allps = Ring(C.psum)
        for ti in range(ntok // 128):
            r0 = ti * 128
            j = 1 if r0 < TC else 0
            emit_epilogue_tile(C, E, lambda c, r0=r0: (mixT[:, c, r0:r0 + 128], [('mixT', c)]), wout,
                               xown[r0:r0 + 128, :], xo[r0:r0 + 128, :], j, gate_bc, lnbc, allps, ti)
        _drain(C)


def run_rec_layer(inp, l, x, ctx):
    li = l // 2
    com = _common_layer(inp, l)
    ncA = _get('recA', build_recA)
    in_maps = []
    for core in range(8):
        b, hf = core // 2, core % 2
        m = dict(xall=np.ascontiguousarray(np.concatenate([ctx[b], x[b]], axis=0)), cc=_cc_cols(inp['c'][b], inp['c_ctx']),
                 modw=com['modw'], modb_col=com['modb_col'], modb_gate=com['modb_gate'], ident=com['ident'])
        m.update(prep_rec_layer(inp, l, hf))
        in_maps.append(m)
    resA = run_bass_kernel_spmd(ncA, in_maps, core_ids=list(range(8)))
    ncB = _get('recB', build_recB)
    wout = np.asarray(inp['rec_w_out'][li], np.float32)
    in_maps = []
    for core in range(8):
        b, hf = core // 2, core % 2
        mo = [resA.results[2 * b + k]["mixo"] for k in range(2)]
        full = np.concatenate([mo[0][0:2], mo[1][0:2], mo[0][2:4], mo[1][2:4]], axis=0)
        sel = np.concatenate([np.arange(TC), TC + hf * OWN + np.arange(OWN)])
        m = dict(xown=np.ascontiguousarray(np.concatenate([ctx[b], x[b, hf * OWN:(hf + 1) * OWN]], axis=0)),
                 mixT=np.ascontiguousarray(full[:, :, sel]), cc=_cc_cols(inp['c'][b], inp['c_ctx']), wout=wout)
        m.update(com)
        in_maps.append(m)
    resB = run_bass_kernel_spmd(ncB, in_maps, core_ids=list(range(8)))
    xn = np.empty_like(x)
    cn = np.empty_like(ctx)
    for core in range(8):
        b, hf = core // 2, core % 2
        o = resB.results[core]["xo"]
        xn[b, hf * OWN:(hf + 1) * OWN] = o[TC:]
        if hf == 0:
            cn[b] = o[:TC]
    return xn, cn


def build_fused():
    nc = bass.Bass("TRN2", target_bir_lowering=False)
    with ExitStack() as st:
        C = Ctx(nc, st)
        ident_d = C.dram_in("ident", [128, 128])
        K = emit_consts(C, st, ident_d)
        x_in = C.dram_in("x_in", [TA, D])
        out = C.dram_out("xfin", [TA, D])
        xb = [nc.dram_tensor("xb%d" % i, [TA, D], F32, kind="Internal").ap() for i in range(2)]
        mixb = nc.dram_tensor("mixb", [8, 128, TA], BF16, kind="Internal").ap()
        for l in range(4):
            src = x_in if l == 0 else xb[(l - 1) % 2]
            dst = out if l == 3 else xb[l % 2]
            C.lsfx = "_L%d" % l
            if l % 2 == 0:
                C.sfx = "_L%d" % l
                C.override = dict(xall=src, xown=src, xo=dst)
                with ExitStack() as ph:
                    emit_att(C, ph, K, full=True)
            else:
                for hfr in range(2):
                    C.sfx = "_L%d_%d" % (l, hfr)
                    C.override = dict(xall=src)
                    with ExitStack() as ph:
                        emit_recA(C, ph, K, mix_dst=lambda idx, c0, n, hfr=hfr: mixb[
                            (2 * hfr + idx) if idx < 2 else (4 + 2 * hfr + idx - 2), :, c0:c0 + n])
                C.sfx = "_L%d" % l
                C.override = dict(xown=src, mixT=mixb, xo=dst)
                with ExitStack() as ph:
                    emit_recB(C, ph, K, ntok=TA)
        C.P.finish()
    return nc


def fused_inputs(inp, b):
    m = dict(ident=np.eye(128, dtype=np.float32), cc=_cc_cols(inp['c'][b], inp['c_ctx']),
             x_in=np.ascontiguousarray(np.concatenate([inp['ctx'][b], inp['x'][b]], axis=0), dtype=np.float32))
    LAYER = Ctx.LAYER
    for l in range(4):
        com = _common_layer(inp, l)
        for k in LAYER:
            m[k + "_L%d" % l] = com[k]
        if l % 2 == 0:
            sh, (tabKg, tabKm) = prep_att_layer(inp, l)
            for k, v in sh.items():
                if k not in LAYER and k != 'ident':
                    m[k + "_L%d" % l] = v
            m["tabKg_L%d" % l] = tabKg
            m["tabKm_L%d" % l] = tabKm
        else:
            for hfr in range(2):
                for k, v in prep_rec_layer(inp, l, hfr).items():
                    m[k + "_L%d_%d" % (l, hfr)] = v
            m["wout_L%d" % l] = np.asarray(inp['rec_w_out'][l // 2], np.float32)
    return m


def kernel(**inputs):
    inp = {k: np.asarray(v) for k, v in inputs.items()}
    nc = _get('fused', build_fused)
    per_b = [fused_inputs(inp, b) for b in range(NB)]
    in_maps = [per_b[core // 2] for core in range(8)]
    res = run_bass_kernel_spmd(nc, in_maps, core_ids=list(range(8)))
    out = np.empty((NB, TL, D), np.float32)
    for b in range(NB):
        out[b] = res.results[2 * b]["xfin"][TC:]
    return out


def kernel_unfused(**inputs):
    inp = {k: np.asarray(v) for k, v in inputs.items()}
    x = np.ascontiguousarray(inp['x'], dtype=np.float32)
    ctx = np.ascontiguousarray(inp['ctx'], dtype=np.float32)
    for l in range(4):
        if l % 2 == 0:
            x, ctx = run_att_layer(inp, l, x, ctx)
        else:
            x, ctx = run_rec_layer(inp, l, x, ctx)
    return x
```

```python
import numpy as np
from contextlib import ExitStack
import concourse.bass as bass
import concourse.mybir as mybir
from concourse.bass_utils import run_bass_kernel_spmd

F32 = mybir.dt.float32
BF16 = mybir.dt.bfloat16
AF = mybir.ActivationFunctionType
ALU = mybir.AluOpType

D = 1024
NB = 4
TL = 4096
TC = 256
TA = TL + TC
OWN = 2048
NOWN = OWN + TC
EPS = 1e-6
ALPHA = 8.0 ** 0.25
THETA = 10000.0


class Prog:
    def __init__(self, nc, stack, n_dma=12, same_engine_sync=True):
        self.nc = nc
        self.stack = stack
        self.eng = {'pe': nc.tensor, 'act': nc.scalar, 'dve': nc.vector, 'pool': nc.gpsimd, 'sp': nc.sync}
        self.sem = {}
        for e in ['pe', 'act', 'dve', 'pool']:
            self.sem[e] = stack.enter_context(nc.semaphore("s_" + e))
        self.cnt = {e: 0 for e in ['pe', 'act', 'dve', 'pool']}
        self.n_dma = n_dma
        self.dq = {}
        for q in ['sp', 'pool']:
            sems = [stack.enter_context(nc.semaphore("d_%s_%d" % (q, i))) for i in range(n_dma)]
            for i, s in enumerate(sems):
                self.sem[('d', q, i)] = s
            self.dq[q] = dict(targets=[0] * n_dma, next=0)
        self.waited = {e: {} for e in self.eng}
        self.lastw = {}
        self.readers = {}
        self.same = same_engine_sync
        self.n_inst = 0

    def _wait(self, E, tok):
        if tok is None:
            return
        key, val = tok
        if key == E and (E == 'pe' or not self.same):
            return
        if self.waited[E].get(key, 0) >= val:
            return
        self.eng[E].wait_ge(self.sem[key], val)
        self.waited[E][key] = val
        self.n_inst += 1

    def _deps(self, E, reads, writes):
        for b in reads:
            self._wait(E, self.lastw.get(b))
            if isinstance(b, str) and b.startswith('ps'):
                for t in self.readers.get(b, ()):
                    if t[0] != E:
                        self._wait(E, t)
        for b in writes:
            self._wait(E, self.lastw.get(b))
            for t in self.readers.get(b, ()):
                self._wait(E, t)

    def _record(self, tok, reads, writes):
        for b in reads:
            lst = self.readers.setdefault(b, [])
            lst[:] = [t for t in lst if t[0] != tok[0]]
            lst.append(tok)
        for b in writes:
            self.lastw[b] = tok
            self.readers[b] = []

    def op(self, E, name, *args, reads=(), writes=(), **kw):
        self._deps(E, reads, writes)
        inst = getattr(self.eng[E], name)(*args, **kw)
        self.cnt[E] += 1
        inst.then_inc(self.sem[E], 1)
        self.n_inst += 1
        self._record((E, self.cnt[E]), reads, writes)
        return inst

    def dma(self, q, out, in_, reads=(), writes=(), **kw):
        self._deps(q, reads, writes)
        d = self.dq[q]
        idx = d['next'] % self.n_dma
        d['next'] += 1
        key = ('d', q, idx)
        if d['targets'][idx] > 0:
            self._wait(q, (key, d['targets'][idx]))
        inst = self.eng[q].dma_start(out=out, in_=in_, **kw)
        d['targets'][idx] += 16
        inst.then_inc(self.sem[key], 16)
        self.n_inst += 1
        self._record((key, d['targets'][idx]), reads, writes)
        return inst

    def finish(self):
        for q in self.dq:
            d = self.dq[q]
            for idx in range(self.n_dma):
                if d['targets'][idx] > 0:
                    self._wait('sp', (('d', q, idx), d['targets'][idx]))
        for e in ['pe', 'act', 'dve', 'pool']:
            if self.cnt[e] > 0:
                self._wait('sp', (e, self.cnt[e]))


def interleave(gens):
    gens = list(gens)
    while gens:
        for g in list(gens):
            try:
                next(g)
            except StopIteration:
                gens.remove(g)


class Ring:
    def __init__(self, items):
        self.items = items
        self.i = 0

    def next(self):
        it = self.items[self.i % len(self.items)]
        self.i += 1
        return it


class Ctx:
    def __init__(self, nc, st):
        self.nc = nc
        self.st = st
        self.P = Prog(nc, st)
        self.psum = []
        for i in range(8):
            t = st.enter_context(nc.psum_tensor("ps%d" % i, [128, 512], F32))
            self.psum.append((t, "ps%d" % i))

    def sb(self, stack, name, shape, dt):
        self.uid = getattr(self, 'uid', 0) + 1
        return stack.enter_context(self.nc.sbuf_tensor("%s_%d" % (name, self.uid), shape, dt))

    SHARED = ('ident', 'cc')
    LAYER = ('modw', 'modb_col', 'modb_gate', 'lng', 'lnb')

    def _dram(self, name, shape, dt, kind):
        ov = getattr(self, 'override', {})
        if name in ov:
            return ov[name]
        if name in self.SHARED:
            full = name
        elif name in self.LAYER:
            full = name + getattr(self, 'lsfx', getattr(self, 'sfx', ''))
        else:
            full = name + getattr(self, 'sfx', '')
        cache = self.__dict__.setdefault('decl', {})
        if full not in cache:
            cache[full] = self.nc.dram_tensor(full, list(shape), dt, kind=kind).ap()
        return cache[full]

    def dram_in(self, name, shape, dt=F32):
        return self._dram(name, shape, dt, "ExternalInput")

    def dram_out(self, name, shape, dt=F32):
        return self._dram(name, shape, dt, "ExternalOutput")

    def mm(self, out, lhsT, rhs, start, stop, reads, writes):
        return self.P.op('pe', 'matmul', out, lhsT, rhs, start=start, stop=stop, reads=reads, writes=writes)

    def act(self, out, in_, func, reads, writes, **kw):
        return self.P.op('act', 'activation', out=out, in_=in_, func=func, reads=reads, writes=writes, **kw)

    def tt(self, E, out, in0, in1, op, reads, writes):
        return self.P.op(E, 'tensor_tensor', out=out, in0=in0, in1=in1, op=op, reads=reads, writes=writes)

    def stt(self, out, in0, scalar, in1, op0, op1, reads, writes):
        return self.P.op('dve', 'scalar_tensor_tensor', out=out, in0=in0, scalar=scalar, in1=in1,
                         op0=op0, op1=op1, reads=reads, writes=writes)

    def ts(self, E, out, in0, s1, s2, op0, op1, reads, writes):
        return self.P.op(E, 'tensor_scalar', out=out, in0=in0, scalar1=s1, scalar2=s2, op0=op0, op1=op1,
                         reads=reads, writes=writes)

    def copy(self, E, out, in_, reads, writes):
        return self.P.op(E, 'tensor_copy', out=out, in_=in_, reads=reads, writes=writes)

    def recip(self, out, in_, reads, writes):
        return self.P.op('dve', 'reciprocal', out=out, in_=in_, reads=reads, writes=writes)


def _rope_tabs(pos_row, pos_col, dim):
    half = dim // 2
    q = half // 2
    inv = THETA ** (-np.arange(0, half, 2, dtype=np.float32) / np.float32(half))
    inv = inv.astype(np.float32)
    ang_r = pos_row.astype(np.float32)[:, None] * inv
    ang_c = pos_col.astype(np.float32)[:, None] * inv
    cr, sr, cc_, sc_ = np.cos(ang_r), np.sin(ang_r), np.cos(ang_c), np.sin(ang_c)
    T = pos_row.shape[0]
    cos = np.zeros((dim, T), np.float32)
    sin = np.zeros((dim, T), np.float32)
    partner = np.zeros(dim, np.int64)
    for d in range(dim):
        grp, i = d // q, d % q
        c, s = (cr, sr) if grp < 2 else (cc_, sc_)
        cos[d] = c[:, i]
        if grp % 2 == 0:
            sin[d] = -s[:, i]
            partner[d] = d + q
        else:
            sin[d] = s[:, i]
            partner[d] = d - q
    return cos, sin, partner


def _att_tables():
    t = np.arange(TL)
    row, col = t // 64, t % 64
    cg, sg, pg = _rope_tabs(row, col, 64)
    cm, sm, pm = _rope_tabs(row, col, 32)
    return cg, sg, pg, cm, sm, pm


def prep_att_layer(inp, l):
    li = l // 2
    cg, sg, pg, cm, sm, pm = _att_tables()
    w_in = np.asarray(inp['att_w_in'][li], np.float32)
    o = np.cumsum([0, 256, 128, 32, 512, 512, 128, 128, 512])
    cq, ckv, kr, ga, qb, kb, vb, gb = [w_in[:, o[i]:o[i + 1]] for i in range(8)]
    hperm = np.concatenate([np.concatenate([np.arange(64) + 64 * c, np.arange(64) + 64 * (4 + c)]) for c in range(4)])
    sw64 = np.concatenate([pg + 64 * h for h in range(8)])
    qb_sw = qb[:, sw64]
    wq = np.concatenate([cq, ga, gb[:, hperm], qb[:, hperm], qb_sw[:, hperm]], axis=1)
    kb_sw = kb[:, np.concatenate([pg, pg + 64])]
    z64 = np.zeros((D, 64), np.float32)
    wkv = np.concatenate([ckv, kb, kb_sw, z64, kr, z64, kr[:, pm], vb], axis=1)
    w_uq = np.asarray(inp['mla_w_uq'][li], np.float32).reshape(256, 8, 96)
    wuqA = w_uq.reshape(256, 768)
    wuqB = np.concatenate([w_uq[:, :, :64], w_uq[:, :, 64:][:, :, pm]], axis=2).reshape(256, 768)
    w_ukv = np.asarray(inp['mla_w_ukv'][li], np.float32).reshape(128, 8, 128)
    wuk = w_ukv[:, :, :64].reshape(128, 512)
    wuv = w_ukv[:, :, 64:].reshape(128, 512)
    w_out = np.asarray(inp['att_w_out'][li], np.float32)
    wout = np.concatenate([w_out[:512], w_out[512:][hperm]], axis=0)
    gq = np.asarray(inp['gqa_q_norm'][li], np.float32)
    gk = np.asarray(inp['gqa_k_norm'][li], np.float32)
    gcols = np.zeros((128, 8), np.float32)
    gcols[:, 0:2] = np.asarray(inp['mla_q_norm'][li], np.float32).reshape(2, 128).T
    gcols[:, 2] = np.asarray(inp['mla_kv_norm'][li], np.float32)
    gcols[:, 3] = np.tile(gq, 2)
    gcols[:, 4] = np.tile(gq[pg], 2)
    gcols[:, 5] = np.tile(gk, 2)
    gcols[:, 6] = np.tile(gk[pg], 2)
    tabKg = np.zeros((2, 128, TA), np.float32)
    tabKg[0, :, :TC] = 1.0
    tabKg[0, :, TC:] = np.tile(cg, (2, 1))
    tabKg[1, :, TC:] = np.tile(sg, (2, 1))
    tabKm = np.zeros((2, 96, TA), np.float32)
    tabKm[0, 64:, :TC] = 1.0
    tabKm[0, 64:, TC:] = cm
    tabKm[1, 64:, TC:] = sm
    sh = dict(wq=wq, wkv=wkv, wuqA=wuqA, wuqB=wuqB, wuk=wuk, wuv=wuv, wout=wout, gcols=gcols)
    sh.update(_common_layer(inp, l))
    return sh, (tabKg, tabKm)


def _common_layer(inp, l):
    mod_b = np.asarray(inp['mod_b'][l], np.float32)
    return dict(
        modw=np.asarray(inp['mod_w'][l], np.float32),
        modb_col=np.ascontiguousarray(mod_b.reshape(24, 128).T),
        modb_gate=np.ascontiguousarray(mod_b[None, 2048:]),
        lng=np.asarray(inp['ln_g'][l], np.float32)[None, :],
        lnb=np.asarray(inp['ln_b'][l], np.float32)[None, :],
        ident=np.eye(128, dtype=np.float32),
    )


def _cc_cols(c_b, c_ctx):
    cc = np.zeros((128, 16), np.float32)
    cc[:, 0::2] = np.asarray(c_b, np.float32).reshape(8, 128).T
    cc[:, 1::2] = np.asarray(c_ctx, np.float32).reshape(8, 128).T
    return cc


def emit_consts(C, st, ident_d):
    P = C.P
    k = {}
    k['ident'] = C.sb(st, "ident", [128, 128], F32)
    P.dma('sp', k['ident'][:], ident_d, writes=['ident'])
    k['ones_f'] = C.sb(st, "ones_f", [128, 128], F32)
    P.op('dve', 'memset', k['ones_f'][:], 1.0, writes=['ones_f'])
    k['ones_bf'] = C.sb(st, "ones_bf", [128, 128], BF16)
    P.op('dve', 'memset', k['ones_bf'][:], 1.0, writes=['ones_bf'])
    k['onesbd'] = C.sb(st, "onesbd", [128, 128], BF16)
    P.op('dve', 'memset', k['onesbd'][:], 0.0, writes=['onesbd'])
    P.op('dve', 'memset', k['onesbd'][0:64, 0:64], 1.0, reads=['onesbd'], writes=['onesbd'])
    P.op('dve', 'memset', k['onesbd'][64:128, 64:128], 1.0, reads=['onesbd'], writes=['onesbd'])
    return k


def emit_modulation(C, st, K, cc_d, modw_d, modbcol_d, modbgate_d, need_cols=True, need_gate=True):
    P = C.P
    modcol = C.sb(st, "modcol", [128, 16, 2], F32)
    gate_bc = C.sb(st, "gate_bc", [128, 2, 1024], F32) if need_gate else None
    with ExitStack() as s2:
        cc = C.sb(s2, "cc", [128, 16], F32)
        sc = C.sb(s2, "sc", [128, 16], F32)
        mbc = C.sb(s2, "mbc", [128, 24], F32)
        mbg = C.sb(s2, "mbg", [128, 1024], F32)
        rep = C.sb(s2, "rep", [128, 2, 8, 128], F32)
        mw = C.sb(s2, "mw", [128, 8, 1024], F32)
        P.dma('sp', cc[:], cc_d, writes=['cc'])
        P.dma('sp', mbc[:], modbcol_d, writes=['mbc'])
        P.dma('sp', mbg[:], modbgate_d[0:1, :].broadcast_to([128, 1024]), writes=['mbg'])
        C.act(sc[:], cc[:], AF.Silu, reads=['cc'], writes=['sc'])
        for j in range(2):
            for kc in range(8):
                C.ts('dve', rep[:, j, kc, :], K['ones_f'][:], sc[:, 2 * kc + j:2 * kc + j + 1], None, ALU.mult,
                     ALU.bypass, reads=['ones_f', 'sc'], writes=['rep'])
        for t in range(3):
            if (t < 2 and not need_cols) or (t == 2 and not need_gate):
                continue
            for kc in range(8):
                P.dma('sp', mw[:, kc, :], modw_d[kc * 128:(kc + 1) * 128, t * 1024:(t + 1) * 1024], writes=['mw'])
            if t < 2:
                ps, pk = C.psum[t]
                for oc in range(8):
                    for kc in range(8):
                        C.mm(ps[:, 2 * oc:2 * oc + 2], mw[:, kc, oc * 128:(oc + 1) * 128], sc[:, 2 * kc:2 * kc + 2],
                             kc == 0, kc == 7, reads=['mw', 'sc'], writes=[pk])
                    C.ts('dve', modcol[:, t * 8 + oc, :], ps[:, 2 * oc:2 * oc + 2],
                         mbc[:, t * 8 + oc:t * 8 + oc + 1], float(t), ALU.add, ALU.add,
                         reads=[pk, 'mbc'], writes=['modcol'])
            else:
                for j in range(2):
                    for half in range(2):
                        ps, pk = C.psum[2 + 2 * j + half]
                        for kc in range(8):
                            C.mm(ps[:, :], rep[:, j, kc, :], mw[:, kc, half * 512:(half + 1) * 512], kc == 0, kc == 7,
                                 reads=['mw', 'rep'], writes=[pk])
                        C.tt('dve', gate_bc[:, j, half * 512:(half + 1) * 512], ps[:, :],
                             mbg[:, half * 512:(half + 1) * 512], ALU.add, reads=[pk, 'mbg'], writes=['gate_bc'])
        _drain(C)
    return modcol, gate_bc


def _drain(C):
    P = C.P
    toks = [(e, P.cnt[e]) for e in ['pe', 'act', 'dve', 'pool'] if P.cnt[e] > 0]
    dtoks = []
    for q in P.dq:
        d = P.dq[q]
        for idx in range(P.n_dma):
            if d['targets'][idx] > 0:
                dtoks.append((('d', q, idx), d['targets'][idx]))
    for E in ['pe', 'act', 'dve', 'pool', 'sp']:
        for t in toks + dtoks:
            P._wait(E, t)


def emit_hT(C, K, xs, xs_key, x_rows, ntok, j, modcol, hT, hT_key, ring, dst=None):
    P = C.P
    nt = ntok // 128
    for tt in range(nt):
        P.dma('sp', xs[:, tt, :], x_rows[tt * 128:(tt + 1) * 128, :], writes=[(xs_key, tt)])
    for kc in range(8):
        ps, pk = ring.next()
        for tt in range(nt):
            P.op('pe', 'transpose', ps[:, tt * 128:(tt + 1) * 128], xs[:, tt, kc * 128:(kc + 1) * 128], K['ident'][:],
                 reads=[(xs_key, tt), 'ident'], writes=[pk])
        C.act(hT[:, kc, 0:ntok] if dst is None else dst(kc), ps[:, 0:ntok], AF.Identity, reads=[pk, 'modcol'], writes=[hT_key],
              scale=modcol[:, 8 + kc, j:j + 1], bias=modcol[:, kc, j:j + 1])


def emit_epilogue_tile(C, E, mixT_tile_fn, wout, x_rows, out_rows, j, gate_bc, lnbc, ring, tagi):
    P = C.P
    xt, tb, st6, mv, sm = E['xt'][tagi % 2], E['tb'][tagi % 2], E['st6'][tagi % 2], E['mv'][tagi % 2], E['sm'][tagi % 2]
    kx, kt, ks = ('xt', tagi % 2), ('tb', tagi % 2), ('sm', tagi % 2)
    P.dma('sp', xt[:], x_rows, writes=[kx])
    for half in range(2):
        ps, pk = ring.next()
        for c in range(8):
            lhsT, rk = mixT_tile_fn(c)
            C.mm(ps[:, :], lhsT, wout[:, c, half * 512:(half + 1) * 512], c == 0, c == 7, reads=rk + ['wout'], writes=[pk])
        C.tt('dve', tb[:, half * 512:(half + 1) * 512], ps[:, :], gate_bc[:, j, half * 512:(half + 1) * 512], ALU.mult,
             reads=[pk, 'gate_bc'], writes=[kt])
    C.stt(tb[:], xt[:], float(ALPHA), tb[:], ALU.mult, ALU.add, reads=[kx, kt], writes=[kt])
    for half in range(2):
        P.op('dve', 'bn_stats', out=st6[:, half, :], in_=tb[:, half * 512:(half + 1) * 512], reads=[kt], writes=[ks])
    P.op('dve', 'bn_aggr', out=mv[:], in_=st6[:].rearrange("p a b -> p (a b)"), reads=[ks], writes=[ks])
    C.act(sm[:, 0:1], mv[:, 1:2], AF.Sqrt, reads=[ks], writes=[ks], bias=float(EPS), scale=1.0)
    C.recip(sm[:, 1:2], sm[:, 0:1], reads=[ks], writes=[ks])
    C.ts('dve', sm[:, 2:3], mv[:, 0:1], sm[:, 1:2], -1.0, ALU.mult, ALU.mult, reads=[ks], writes=[ks])
    C.act(tb[:], tb[:], AF.Identity, reads=[kt, ks], writes=[kt], scale=sm[:, 1:2], bias=sm[:, 2:3])
    C.tt('pool', tb[:], tb[:], lnbc[:, 0, :], ALU.mult, reads=[kt, 'lnbc'], writes=[kt])
    C.tt('pool', tb[:], tb[:], lnbc[:, 1, :], ALU.add, reads=[kt, 'lnbc'], writes=[kt])
    P.dma('sp', out_rows, tb[:], reads=[kt])


def alloc_epilogue(C, st):
    E = dict(xt=[], tb=[], st6=[], mv=[], sm=[])
    for i in range(2):
        E['xt'].append(C.sb(st, "e_xt%d" % i, [128, 1024], F32))
        E['tb'].append(C.sb(st, "e_tb%d" % i, [128, 1024], F32))
        E['st6'].append(C.sb(st, "e_st%d" % i, [128, 2, 6], F32))
        E['mv'].append(C.sb(st, "e_mv%d" % i, [128, 2], F32))
        E['sm'].append(C.sb(st, "e_sm%d" % i, [128, 4], F32))
    return E


def load_w_bf16(C, dst, dst_key, src, kcs, ncols):
    for kc in range(kcs):
        for c0 in range(0, ncols, 1024):
            c1 = min(ncols, c0 + 1024)
            C.P.dma('pool', dst[:, kc, c0:c1], src[kc * 128:(kc + 1) * 128, c0:c1], writes=[dst_key])


KV_BLOCKS = [(0, 256, 1)] + [(256 + 512 * i, 512, 0) for i in range(8)]
Q_PASSES = [[(0, 256, 1), (256, 512, 0), (768, 512, 0)], [(1280, 512, 0), (1792, 512, 0)]]
FULL_PASSES = [[(0, 256, 1), (256, 512, 0), (768, 512, 0)]] + [[(1280 + 1024 * i, 512, 0), (1792 + 1024 * i, 512, 0)] for i in range(3)]
NKT = TA // 128


def build_att():
    nc = bass.Bass("TRN2", target_bir_lowering=False)
    with ExitStack() as st:
        C = Ctx(nc, st)
        emit_att(C, st)
        C.P.finish()
    return nc


def emit_att(C, st, K=None, full=False):
    if True:
        P = C.P
        Q_PASSES_ = FULL_PASSES if full else Q_PASSES
        xall = C.dram_in("xall", [TA, D])
        xown = C.dram_in("xown", [NOWN, D])
        cc_d = C.dram_in("cc", [128, 16])
        modw_d = C.dram_in("modw", [D, 3 * D])
        modbcol_d = C.dram_in("modb_col", [128, 24])
        modbgate_d = C.dram_in("modb_gate", [1, D])
        lng_d = C.dram_in("lng", [1, D])
        lnb_d = C.dram_in("lnb", [1, D])
        wkv_d = C.dram_in("wkv", [D, 704])
        wq_d = C.dram_in("wq", [D, 2304])
        wuqA_d = C.dram_in("wuqA", [256, 768])
        wuqB_d = C.dram_in("wuqB", [256, 768])
        wuk_d = C.dram_in("wuk", [128, 512])
        wuv_d = C.dram_in("wuv", [128, 512])
        wout_d = C.dram_in("wout", [D, D])
        gcols_d = C.dram_in("gcols", [128, 8])
        tabKg_d = C.dram_in("tabKg", [2, 128, TA])
        tabKm_d = C.dram_in("tabKm", [2, 96, TA])
        tabQg_d = None if full else C.dram_in("tabQg", [2, 128, NOWN])
        tabQm_d = None if full else C.dram_in("tabQm", [2, 96, NOWN])
        ident_d = C.dram_in("ident", [128, 128])
        xo = C.dram_out("xo", [NOWN, D])
        if full:
            tabQg_d, tabQm_d = tabKg_d, tabKm_d

        K = K or emit_consts(C, st, ident_d)
        gcols = C.sb(st, "gcols", [128, 8], F32)
        P.dma('sp', gcols[:], gcols_d, writes=['gcols'])
        lnbc = C.sb(st, "lnbc", [128, 2, 1024], F32)
        P.dma('sp', lnbc[:, 0, :], lng_d[0:1, :].broadcast_to([128, D]), writes=['lnbc'])
        P.dma('sp', lnbc[:, 1, :], lnb_d[0:1, :].broadcast_to([128, D]), writes=['lnbc'])
        modcol, gate_bc = emit_modulation(C, st, K, cc_d, modw_d, modbcol_d, modbgate_d)

        ckvT = C.sb(st, "ckvT", [128, TA], BF16)
        krT = C.sb(st, "krT", [96, TA], BF16)
        kTg = C.sb(st, "kTg", [128, TA], BF16)
        Vg = C.sb(st, "Vg", [128, NKT, 192], BF16)
        P.op('pool', 'memset', Vg[:, :, 64:128], 1.0, writes=['Vg'])
        allps = Ring(C.psum)

        def rms_rope_chunk(s2, psA, pkA, psB, pkB, n, gA, gB, tab, tabk, dst, dst_key, tmp):
            sq, rt, t1, t2 = tmp
            C.act(sq[:, 0:n], psA[:, 0:n], AF.Square, reads=[pkA], writes=['sq'])
            ps2, pk2 = allps.next()
            C.mm(ps2[:, 0:n], K['onesbd'][:], sq[:, 0:n], True, True, reads=['onesbd', 'sq'], writes=[pk2])
            C.act(rt[:, 0:n], ps2[:, 0:n], AF.Sqrt, reads=[pk2], writes=['rt'], bias=float(EPS), scale=1.0 / 64)
            C.recip(rt[:, 0:n], rt[:, 0:n], reads=['rt'], writes=['rt'])
            C.stt(t1[:, 0:n], psA[:, 0:n], gA, tab[:, 0, 0:n], ALU.mult, ALU.mult, reads=[pkA, tabk, 'gcols'], writes=['t1'])
            C.stt(t2[:, 0:n], psB[:, 0:n], gB, tab[:, 1, 0:n], ALU.mult, ALU.mult, reads=[pkB, tabk, 'gcols'], writes=['t2'])
            C.tt('pool', t1[:, 0:n], t1[:, 0:n], t2[:, 0:n], ALU.add, reads=['t1', 't2'], writes=['t1'])
            C.tt('pool', dst, t1[:, 0:n], rt[:, 0:n], ALU.mult, reads=['t1', 'rt'], writes=[dst_key])

        with ExitStack() as s2:
            wkv = C.sb(s2, "wkv", [128, 8, 704], BF16)
            load_w_bf16(C, wkv, 'wkv', wkv_d, 8, 704)
            xs = C.sb(s2, "xs", [128, 4, 1024], F32)
            hT = [C.sb(s2, "hT%d" % i, [128, 8, 512], BF16) for i in range(2)]
            tabg = [C.sb(s2, "tabg%d" % i, [128, 2, 512], F32) for i in range(2)]
            tabm = [C.sb(s2, "tabm%d" % i, [96, 2, 512], F32) for i in range(2)]
            tmp = (C.sb(s2, "sq", [128, 512], BF16), C.sb(s2, "rt", [128, 512], F32),
                   C.sb(s2, "t1", [128, 512], F32), C.sb(s2, "t2", [128, 512], F32))
            for bi, (r0, n, j) in enumerate(KV_BLOCKS):
                h, hk = hT[bi % 2], ('hT', bi % 2)
                tg, tgk = tabg[bi % 2], ('tabg', bi % 2)
                tm, tmk = tabm[bi % 2], ('tabm', bi % 2)
                for a in range(2):
                    P.dma('sp', tg[:, a, 0:n], tabKg_d[a, :, r0:r0 + n], writes=[tgk])
                    P.dma('sp', tm[64:96, a, 0:n], tabKm_d[a, 64:96, r0:r0 + n], writes=[tmk])
                emit_hT(C, K, xs, 'xs', xall[r0:r0 + n, :], n, j, modcol, h, hk, allps)

                def proj(off, M):
                    ps, pk = allps.next()
                    for kc in range(8):
                        C.mm(ps[0:M, 0:n], wkv[:, kc, off:off + M], h[:, kc, 0:n], kc == 0, kc == 7,
                             reads=['wkv', hk], writes=[pk])
                    return ps, pk
                ps, pk = proj(0, 128)
                sq, rt, t1, t2 = tmp
                C.act(sq[:, 0:n], ps[:, 0:n], AF.Square, reads=[pk], writes=['sq'])
                ps2, pk2 = allps.next()
                C.mm(ps2[:, 0:n], K['ones_bf'][:], sq[:, 0:n], True, True, reads=['ones_bf', 'sq'], writes=[pk2])
                C.act(rt[:, 0:n], ps2[:, 0:n], AF.Sqrt, reads=[pk2], writes=['rt'], bias=float(EPS), scale=1.0 / 128)
                C.recip(rt[:, 0:n], rt[:, 0:n], reads=['rt'], writes=['rt'])
                C.stt(ckvT[:, r0:r0 + n], ps[:, 0:n], gcols[:, 2:3], rt[:, 0:n], ALU.mult, ALU.mult,
                      reads=[pk, 'rt', 'gcols'], writes=['ckvT'])
                psA, pkA = proj(128, 128)
                psB, pkB = proj(256, 128)
                rms_rope_chunk(s2, psA, pkA, psB, pkB, n, gcols[:, 5:6], gcols[:, 6:7], tg, tgk, kTg[:, r0:r0 + n], 'kTg', tmp)
                psA, pkA = proj(384, 96)
                psB, pkB = proj(480, 96)
                C.tt('dve', t1[64:96, 0:n], psA[64:96, 0:n], tm[64:96, 0, 0:n], ALU.mult, reads=[pkA, tmk], writes=['t1'])
                C.tt('dve', t2[64:96, 0:n], psB[64:96, 0:n], tm[64:96, 1, 0:n], ALU.mult, reads=[pkB, tmk], writes=['t2'])
                C.tt('pool', krT[64:96, r0:r0 + n], t1[64:96, 0:n], t2[64:96, 0:n], ALU.add, reads=['t1', 't2'], writes=['krT'])
                nt = n // 128
                ps, pk = allps.next()
                for tt in range(nt):
                    for kc in range(8):
                        C.mm(ps[:, tt * 128:(tt + 1) * 128], h[:, kc, tt * 128:(tt + 1) * 128], wkv[:, kc, 576:704],
                             kc == 0, kc == 7, reads=['wkv', hk], writes=[pk])
                t0 = r0 // 128
                for tt in range(nt):
                    C.copy('dve', Vg[:, t0 + tt, 0:64], ps[:, tt * 128:tt * 128 + 64], reads=[pk], writes=['Vg'])
                    C.copy('dve', Vg[:, t0 + tt, 128:192], ps[:, tt * 128 + 64:tt * 128 + 128], reads=[pk], writes=['Vg'])
            _drain(C)

        for pi, blocks in enumerate(Q_PASSES_):
            base = blocks[0][0]
            npass = sum(b[1] for b in blocks)
            with ExitStack() as sp:
                cqT = C.sb(sp, "cqT", [128, 2, 1280], BF16)
                qTg = C.sb(sp, "qTg", [128, 4, 1280], BF16)
                mixT = C.sb(sp, "mixT", [128, 8, 1280], BF16)
                with ExitStack() as s2:
                    wq = C.sb(s2, "wq", [128, 8, 2304], BF16)
                    load_w_bf16(C, wq, 'wq', wq_d, 8, 2304)
                    xs = C.sb(s2, "xs", [128, 4, 1024], F32)
                    hT = [C.sb(s2, "hT%d" % i, [128, 8, 512], BF16) for i in range(2)]
                    tabg = [C.sb(s2, "tabg%d" % i, [128, 2, 512], F32) for i in range(2)]
                    tmp = (C.sb(s2, "sq", [128, 512], BF16), C.sb(s2, "rt", [128, 512], F32),
                           C.sb(s2, "t1", [128, 512], F32), C.sb(s2, "t2", [128, 512], F32))
                    sq2 = C.sb(s2, "sq2", [128, 512], BF16)
                    for bi, (r0, n, j) in enumerate(blocks):
                        h, hk = hT[bi % 2], ('hT', bi % 2)
                        tg, tgk = tabg[bi % 2], ('tabg', bi % 2)
                        l0 = r0 - base
                        for a in range(2):
                            P.dma('sp', tg[:, a, 0:n], tabQg_d[a, :, r0:r0 + n], writes=[tgk])
                        emit_hT(C, K, xs, 'xs', xown[r0:r0 + n, :], n, j, modcol, h, hk, allps)

                        def proj(ci):
                            ps, pk = allps.next()
                            for kc in range(8):
                                C.mm(ps[:, 0:n], wq[:, kc, ci * 128:(ci + 1) * 128], h[:, kc, 0:n], kc == 0, kc == 7,
                                     reads=['wq', hk], writes=[pk])
                            return ps, pk
                        sq, rt, t1, t2 = tmp
                        ps0, pk0 = proj(0)
                        ps1, pk1 = proj(1)
                        C.act(sq[:, 0:n], ps0[:, 0:n], AF.Square, reads=[pk0], writes=['sq'])
                        C.act(sq2[:, 0:n], ps1[:, 0:n], AF.Square, reads=[pk1], writes=['sq2'])
                        ps2, pk2 = allps.next()
                        C.mm(ps2[:, 0:n], K['ones_bf'][:], sq[:, 0:n], True, False, reads=['ones_bf', 'sq'], writes=[pk2])
                        C.mm(ps2[:, 0:n], K['ones_bf'][:], sq2[:, 0:n], False, True, reads=['ones_bf', 'sq2'], writes=[pk2])
                        C.act(rt[:, 0:n], ps2[:, 0:n], AF.Sqrt, reads=[pk2], writes=['rt'], bias=float(EPS), scale=1.0 / 256)
                        C.recip(rt[:, 0:n], rt[:, 0:n], reads=['rt'], writes=['rt'])
                        C.stt(cqT[:, 0, l0:l0 + n], ps0[:, 0:n], gcols[:, 0:1], rt[:, 0:n], ALU.mult, ALU.mult,
                              reads=[pk0, 'rt', 'gcols'], writes=['cqT'])
                        C.stt(cqT[:, 1, l0:l0 + n], ps1[:, 0:n], gcols[:, 1:2], rt[:, 0:n], ALU.mult, ALU.mult,
                              reads=[pk1, 'rt', 'gcols'], writes=['cqT'])
                        for c in range(8):
                            ps, pk = proj(2 + c)
                            C.act(mixT[:, c, l0:l0 + n], ps[:, 0:n], AF.Silu, reads=[pk], writes=[('mixT', c)])
                        for c in range(4):
                            psA, pkA = proj(10 + c)
                            psB, pkB = proj(14 + c)
                            rms_rope_chunk(s2, psA, pkA, psB, pkB, n, gcols[:, 3:4], gcols[:, 4:5], tg, tgk,
                                           qTg[:, c, l0:l0 + n], ('qTg', c), tmp)
                    _drain(C)

                with ExitStack() as s2:
                    wuqA = C.sb(s2, "wuqA", [128, 2, 768], BF16)
                    wuqB = C.sb(s2, "wuqB", [128, 2, 768], BF16)
                    wuk = C.sb(s2, "wuk", [128, 1, 512], BF16)
                    wuv = C.sb(s2, "wuv", [128, 1, 512], BF16)
                    load_w_bf16(C, wuqA, 'wuqA', wuqA_d, 2, 768)
                    load_w_bf16(C, wuqB, 'wuqB', wuqB_d, 2, 768)
                    load_w_bf16(C, wuk, 'wuk', wuk_d, 1, 512)
                    load_w_bf16(C, wuv, 'wuv', wuv_d, 1, 512)
                    Kh = [C.sb(s2, "Kh%d" % i, [96, TA], BF16) for i in range(2)]
                    qh = [C.sb(s2, "qh%d" % i, [96, 1280], BF16) for i in range(2)]
                    Vp = [C.sb(s2, "Vp%d" % i, [128, NKT, 192], BF16) for i in range(2)]
                    for i in range(2):
                        P.op('pool', 'memset', Vp[i][:, :, 64:128], 1.0, writes=[('Vp', i)])
                    tabq = C.sb(s2, "tabq", [96, 2, 1280], F32)
                    for a in range(2):
                        P.dma('sp', tabq[64:96, a, 0:npass], tabQm_d[a, 64:96, base:base + npass], writes=['tabq'])
                    PT = [C.sb(s2, "PT%d" % i, [128, 512], BF16) for i in range(4)]
                    rden = [C.sb(s2, "rden%d" % i, [128, 512], F32) for i in range(2)]
                    on = [C.sb(s2, "on%d" % i, [128, 512], F32) for i in range(2)]
                    t1 = C.sb(s2, "a_t1", [96, 512], F32)
                    t2 = C.sb(s2, "a_t2", [96, 512], F32)
                    Sring = Ring(C.psum[0:3])
                    Oring = Ring(C.psum[3:5])
                    Mring = Ring(C.psum[6:8])
                    cnt = {'pt': 0, 'nrm': 0}

                    def attention(Kap, Kkeys, qbuf, qkeys, kdim, prow, scale, Vfn, Vkeys, nrow, drow, cm):
                        def one(blk, gi):
                            (r0, n, j) = blk
                            l0 = r0 - base
                            nk = 2 if j == 1 else NKT
                            Sb = [C.psum[2 * gi], C.psum[2 * gi + 1]]
                            ops_, opk = C.psum[4 + gi]
                            C.mm(Sb[0][0][:, 0:n], Kap(0), qbuf(l0, n), True, True, reads=Kkeys + qkeys, writes=[Sb[0][1]])
                            for kt in range(nk + 1):
                                if kt + 1 < nk:
                                    sb_ = Sb[(kt + 1) % 2]
                                    C.mm(sb_[0][:, 0:n], Kap(kt + 1), qbuf(l0, n), True, True, reads=Kkeys + qkeys, writes=[sb_[1]])
                                if kt < nk:
                                    pi_ = 2 * gi + kt % 2
                                    C.act(PT[pi_][:, 0:n], Sb[kt % 2][0][:, 0:n], AF.Exp, reads=[Sb[kt % 2][1]],
                                          writes=[('PT', pi_)], scale=float(scale))
                                if kt >= 1:
                                    pj = 2 * gi + (kt - 1) % 2
                                    C.mm(ops_[:, 0:n], Vfn(kt - 1), PT[pj][:, 0:n], kt == 1, kt == nk,
                                         reads=Vkeys + [('PT', pj)], writes=[opk])
                                yield
                            ni = gi
                            C.recip(rden[ni][nrow, 0:n], ops_[drow, 0:n], reads=[opk], writes=[('rden', ni)])
                            C.tt('dve', on[ni][nrow, 0:n], ops_[nrow, 0:n], rden[ni][nrow, 0:n], ALU.mult,
                                 reads=[opk, ('rden', ni)], writes=[('on', ni)])
                            C.tt('pool', mixT[nrow, cm, l0:l0 + n], on[ni][nrow, 0:n], mixT[nrow, cm, l0:l0 + n], ALU.mult,
                                 reads=[('on', ni), ('mixT', cm)], writes=[('mixT', cm)])
                        lat = [b for b in blocks if b[2] == 0]
                        for b in blocks:
                            if b[2] == 1:
                                interleave([one(b, 0)])
                        for i_ in range(0, len(lat), 2):
                            interleave([one(b, gi) for gi, b in enumerate(lat[i_:i_ + 2])])

                    hcount = 0
                    for jp in range(4):
                        vp, vk = Vp[jp % 2], ('Vp', jp % 2)
                        for t0 in range(0, NKT, 4):
                            nt = min(4, NKT - t0)
                            ps, pk = Mring.next()
                            for tt in range(nt):
                                C.mm(ps[:, tt * 128:(tt + 1) * 128], ckvT[:, (t0 + tt) * 128:(t0 + tt + 1) * 128],
                                     wuv[:, 0, jp * 128:(jp + 1) * 128], True, True, reads=['ckvT', 'wuv'], writes=[pk])
                            for tt in range(nt):
                                C.copy('dve', vp[:, t0 + tt, 0:64], ps[:, tt * 128:tt * 128 + 64], reads=[pk], writes=[vk])
                                C.copy('dve', vp[:, t0 + tt, 128:192], ps[:, tt * 128 + 64:tt * 128 + 128], reads=[pk], writes=[vk])
                        for hh in range(2):
                            hd = 2 * jp + hh
                            kh, kk = Kh[hcount % 2], ('Kh', hcount % 2)
                            q_, qk = qh[hcount % 2], ('qh', hcount % 2)
                            hcount += 1
                            for (r0, n, j) in KV_BLOCKS:
                                ps, pk = Mring.next()
                                C.mm(ps[0:64, 0:n], wuk[:, 0, hd * 64:(hd + 1) * 64], ckvT[:, r0:r0 + n], True, True,
                                     reads=['wuk', 'ckvT'], writes=[pk])
                                C.copy('dve', kh[0:64, r0:r0 + n], ps[0:64, 0:n], reads=[pk], writes=[kk])
                            C.copy('pool', kh[64:96, :], krT[64:96, :], reads=['krT'], writes=[kk])
                            for (r0, n, j) in blocks:
                                l0 = r0 - base
                                psA, pkA = Mring.next()
                                psB, pkB = Mring.next()
                                for i in range(2):
                                    C.mm(psA[0:96, 0:n], wuqA[:, i, hd * 96:(hd + 1) * 96], cqT[:, i, l0:l0 + n], i == 0, i == 1,
                                         reads=['wuqA', 'cqT'], writes=[pkA])
                                for i in range(2):
                                    C.mm(psB[0:96, 0:n], wuqB[:, i, hd * 96:(hd + 1) * 96], cqT[:, i, l0:l0 + n], i == 0, i == 1,
                                         reads=['wuqB', 'cqT'], writes=[pkB])
                                C.copy('dve', q_[0:64, l0:l0 + n], psA[0:64, 0:n], reads=[pkA], writes=[qk])
                                C.tt('dve', t1[64:96, 0:n], psA[64:96, 0:n], tabq[64:96, 0, l0:l0 + n], ALU.mult,
                                     reads=[pkA, 'tabq'], writes=['a_t1'])
                                C.tt('dve', t2[64:96, 0:n], psB[64:96, 0:n], tabq[64:96, 1, l0:l0 + n], ALU.mult,
                                     reads=[pkB, 'tabq'], writes=['a_t2'])
                                C.tt('pool', q_[64:96, l0:l0 + n], t1[64:96, 0:n], t2[64:96, 0:n], ALU.add,
                                     reads=['a_t1', 'a_t2'], writes=[qk])
                            nrow = slice(0, 64) if hh == 0 else slice(64, 128)
                            drow = slice(64, 128) if hh == 0 else slice(0, 64)
                            attention(lambda kt, kh=kh: kh[0:96, kt * 128:(kt + 1) * 128], [kk],
                                      lambda l0, n, q_=q_: q_[0:96, l0:l0 + n], [qk], 96, None, 96.0 ** -0.5,
                                      lambda kt, vp=vp, hh=hh: vp[:, kt, hh * 64:hh * 64 + 128], [vk], nrow, drow, jp)
                    for c in range(4):
                        for g in range(2):
                            rows = slice(64 * g, 64 * g + 64)
                            drow = slice(64, 128) if g == 0 else slice(0, 64)
                            attention(lambda kt, rows=rows: kTg[rows, kt * 128:(kt + 1) * 128], ['kTg'],
                                      lambda l0, n, rows=rows, c=c: qTg[rows, c, l0:l0 + n], [('qTg', c)], 64, None, 0.125,
                                      lambda kt, g=g: Vg[:, kt, g * 64:g * 64 + 128], ['Vg'], rows, drow, 4 + c)
                    _drain(C)

                with ExitStack() as s2:
                    wout = C.sb(s2, "wout", [128, 8, 1024], BF16)
                    load_w_bf16(C, wout, 'wout', wout_d, 8, 1024)
                    E = alloc_epilogue(C, s2)
                    ti = 0
                    for (r0, n, j) in blocks:
                        for tt in range(n // 128):
                            l0 = r0 - base + tt * 128
                            rr = r0 + tt * 128
                            emit_epilogue_tile(
                                C, E, lambda c, l0=l0: (mixT[:, c, l0:l0 + 128], [('mixT', c)]), wout,
                                xown[rr:rr + 128, :], xo[rr:rr + 128, :], j, gate_bc, lnbc, allps, ti)
                            ti += 1
                    _drain(C)


_CACHE = {}


def _get(name, fn):
    if name not in _CACHE:
        _CACHE[name] = fn()
    return _CACHE[name]


def run_att_layer(inp, l, x, ctx):
    sh, (tabKg, tabKm) = prep_att_layer(inp, l)
    nc = _get('att', build_att)
    in_maps = []
    for core in range(8):
        b, hf = core // 2, core % 2
        xall = np.concatenate([ctx[b], x[b]], axis=0)
        xown = np.concatenate([ctx[b], x[b, hf * OWN:(hf + 1) * OWN]], axis=0)
        sel = np.concatenate([np.arange(TC), TC + hf * OWN + np.arange(OWN)])
        m = dict(sh)
        m.update(xall=np.ascontiguousarray(xall), xown=np.ascontiguousarray(xown),
                 cc=_cc_cols(inp['c'][b], inp['c_ctx']), tabKg=tabKg, tabKm=tabKm,
                 tabQg=np.ascontiguousarray(tabKg[:, :, sel]), tabQm=np.ascontiguousarray(tabKm[:, :, sel]))
        in_maps.append(m)
    res = run_bass_kernel_spmd(nc, in_maps, core_ids=list(range(8)))
    xn = np.empty_like(x)
    cn = np.empty_like(ctx)
    for core in range(8):
        b, hf = core // 2, core % 2
        o = res.results[core]["xo"]
        xn[b, hf * OWN:(hf + 1) * OWN] = o[TC:]
        if hf == 0:
            cn[b] = o[:TC]
    return xn, cn


def _rec_consts():
    idx = np.arange(128)
    same = (idx[:, None] // 64) == (idx[None, :] // 64)
    cm = np.zeros((10, 128, 128), np.float32)
    cm[0] = same & (idx[:, None] <= idx[None, :])
    cm[1] = same & (idx[:, None] >= idx[None, :])
    cm[2] = (idx[:, None] < 64) * np.ones((1, 128))
    cm[3] = (idx[:, None] >= 64) * np.ones((1, 128))
    BIG = 30000.0
    cm[4] = np.where(same & (idx[None, :] < idx[:, None]), 0.0, BIG)
    cm[5] = np.where(same & (idx[None, :] > idx[:, None]), 0.0, BIG)
    cm[6] = np.where(same & (idx[:, None] <= idx[None, :]), 0.0, -BIG)
    cm[7] = np.where(same & (idx[:, None] >= idx[None, :]), 0.0, -BIG)
    cm[8] = same
    return cm


def prep_rec_layer(inp, l, hf):
    li = l // 2
    w_in = np.asarray(inp['rec_w_in'][li], np.float32)
    hs = [2 * hf, 2 * hf + 1]
    cols = []
    for base in (0, 512, 1024, 1536):
        for h in hs:
            cols.append(np.arange(base + h * 128, base + (h + 1) * 128))
    for base in (2064, 2576):
        for i in range(2):
            cols.append(np.arange(base + 256 * hf + 128 * i, base + 256 * hf + 128 * (i + 1)))
    wmain = w_in[:, np.concatenate(cols)]
    bcols = [2048 + d * 4 + h for d in range(2) for h in hs]
    acols = [2056 + d * 4 + h for d in range(2) for h in hs]
    wba = w_in[:, bcols + acols]
    gcw = np.asarray(inp['gdn_conv_w'][li], np.float32)
    lcw = np.asarray(inp['lru_conv_w'][li], np.float32)
    lcb = np.asarray(inp['lru_conv_b'][li], np.float32)
    convw = np.zeros((128, 8, 4), np.float32)
    for ci in range(6):
        convw[:, ci, :] = gcw[:, cols[ci]].T
    convb = np.zeros((128, 2), np.float32)
    for i in range(2):
        ch = 256 * hf + 128 * i + np.arange(128)
        convw[:, 6 + i, :] = lcw[:, ch].T
        convb[:, i] = lcb[ch]
    dtb = np.asarray(inp['gdn_dt_bias'][li], np.float32)
    alog = np.asarray(inp['gdn_a_log'][li], np.float32)
    dtb_row = np.tile(np.array([dtb[d, h] for d in range(2) for h in hs], np.float32), NKT)[None, :]
    alog_row = np.tile(np.array([alog[d, h] for d in range(2) for h in hs], np.float32), NKT)[None, :]
    gnorm = np.asarray(inp['gdn_norm'][li], np.float32)[:, None]
    gw = np.asarray(inp['lru_gate_w'][li], np.float32)
    gb = np.asarray(inp['lru_gate_b'][li], np.float32)
    lam = np.asarray(inp['lru_lambda'][li], np.float32)
    wgate = np.zeros((128, 8, 128), np.float32)
    gbcol = np.zeros((128, 8), np.float32)
    lamcol = np.zeros((128, 4), np.float32)
    for d in range(2):
        for i in range(2):
            ch = 256 * hf + 128 * i + np.arange(128)
            lamcol[:, d * 2 + i] = lam[d, ch]
            for g in range(2):
                e = (d * 2 + g) * 2 + i
                n0 = 4 * hf + 2 * i
                wgate[0:64, e, 0:64] = gw[d, g, n0]
                wgate[64:128, e, 64:128] = gw[d, g, n0 + 1]
                gbcol[:, e] = gb[d, g, ch]
    return dict(wmain=np.ascontiguousarray(wmain), wba=np.ascontiguousarray(wba), convw=convw, convb=convb,
                dtb=dtb_row, alog=alog_row, gnorm=gnorm, wgate=wgate, gbcol=gbcol, lamcol=lamcol, cmats=_rec_consts())


_STOP = {}


def _poff(r0):
    return r0 + 1 if r0 == 0 else r0 + 4


def _csrc(r0):
    return r0 if r0 == 0 else r0 + 3


def build_recA():
    nc = bass.Bass("TRN2", target_bir_lowering=False)
    with ExitStack() as st:
        C = Ctx(nc, st)
        emit_recA(C, st)
        C.P.finish()
    return nc


def emit_recA(C, st, K=None, mix_dst=None):
    if True:
        P = C.P
        xall = C.dram_in("xall", [TA, D])
        cc_d = C.dram_in("cc", [128, 16])
        modw_d = C.dram_in("modw", [D, 3 * D])
        modbcol_d = C.dram_in("modb_col", [128, 24])
        modbgate_d = C.dram_in("modb_gate", [1, D])
        ident_d = C.dram_in("ident", [128, 128])
        wmain_d = C.dram_in("wmain", [D, 1536])
        wba_d = C.dram_in("wba", [D, 8])
        convw_d = C.dram_in("convw", [128, 8, 4])
        convb_d = C.dram_in("convb", [128, 2])
        dtb_d = C.dram_in("dtb", [1, NKT * 4])
        alog_d = C.dram_in("alog", [1, NKT * 4])
        gnorm_d = C.dram_in("gnorm", [128, 1])
        wgate_d = C.dram_in("wgate", [128, 8, 128])
        gbcol_d = C.dram_in("gbcol", [128, 8])
        lamcol_d = C.dram_in("lamcol", [128, 4])
        cm_d = C.dram_in("cmats", [10, 128, 128])
        if mix_dst is None:
            mixo = C.dram_out("mixo", [4, 128, TA], BF16)
            mix_dst = lambda idx, c0, n: mixo[idx, :, c0:c0 + n]

        K = K or emit_consts(C, st, ident_d)
        modcol, _ = emit_modulation(C, st, K, cc_d, modw_d, modbcol_d, modbgate_d, need_gate=False)
        cms = C.sb(st, "cms", [128, 10, 128], F32)
        for i in range(9):
            P.dma('sp', cms[:, i, :], cm_d[i], writes=['cms'])
        convw = C.sb(st, "convw", [128, 8, 4], F32)
        convb = C.sb(st, "convb", [128, 2], F32)
        gnorm = C.sb(st, "gnorm", [128, 1], F32)
        gbcol = C.sb(st, "gbcol", [128, 8], F32)
        lamc = C.sb(st, "lamc", [128, 4], F32)
        c8 = C.sb(st, "c8", [128, 4], F32)
        P.dma('sp', convw[:], convw_d, writes=['cw'])
        P.dma('sp', convb[:], convb_d, writes=['cw'])
        P.dma('sp', gnorm[:], gnorm_d, writes=['cw'])
        P.dma('sp', gbcol[:], gbcol_d, writes=['cw'])
        P.dma('sp', lamc[:], lamcol_d, writes=['lamc'])
        wgate = C.sb(st, "wgate", [128, 8, 128], BF16)
        P.dma('pool', wgate[:], wgate_d, writes=['wgate'])
        C.act(c8[:], lamc[:], AF.Exp, reads=['lamc'], writes=['c8'], scale=-1.0)
        C.act(c8[:], c8[:], AF.Ln, reads=['c8'], writes=['c8'], bias=1.0, scale=1.0)
        C.ts('dve', c8[:], c8[:], -8.0, None, ALU.mult, ALU.bypass, reads=['c8'], writes=['c8'])
        beta = C.sb(st, "beta", [128, NKT, 4], F32)
        nbeta = C.sb(st, "nbeta", [128, NKT, 4], F32)
        gall = C.sb(st, "gall", [128, NKT, 8], F32)
        P.op('pool', 'memset', gall[:], 0.0, writes=['gall'])
        allps = Ring(C.psum)

        def build_hT(s2):
            hT_all = C.sb(s2, "hT_all", [128, 8, TA], BF16)
            with ExitStack() as s3:
                xs = C.sb(s3, "xs", [128, 4, 1024], F32)
                for (r0, n, j) in KV_BLOCKS:
                    emit_hT(C, K, xs, 'xs', xall[r0:r0 + n, :], n, j, modcol, None, ('hT', r0), allps,
                            dst=lambda kc, r0=r0, n=n: hT_all[:, kc, r0:r0 + n])
                _drain(C)
            return hT_all

        def conv_block(pre, cvb, ci, r0, n, bias=None):
            sb_ = _csrc(r0)
            C.ts('dve', cvb[:, 0:n], pre[:, sb_:sb_ + n], convw[:, ci, 0:1], None, ALU.mult, ALU.bypass,
                 reads=['pre', 'cw'], writes=['cvb'])
            for k in range(1, 4):
                C.stt(cvb[:, 0:n], pre[:, sb_ + k:sb_ + k + n], convw[:, ci, k:k + 1], cvb[:, 0:n], ALU.mult, ALU.add,
                      reads=['pre', 'cw', 'cvb'], writes=['cvb'])
            if bias is not None:
                C.ts('dve', cvb[:, 0:n], cvb[:, 0:n], bias, None, ALU.add, ALU.bypass, reads=['cvb', 'cw'], writes=['cvb'])

        def fill_pre(pre, wch, hT_all, ci):
            for (r0, n, j) in KV_BLOCKS:
                ps, pk = allps.next()
                for kc in range(8):
                    C.mm(ps[:, 0:n], wch[:, kc, :], hT_all[:, kc, r0:r0 + n], kc == 0, kc == 7,
                         reads=[('wch', ci % 2), ('hT', r0)], writes=[pk])
                C.copy('dve', pre[:, _poff(r0):_poff(r0) + n], ps[:, 0:n], reads=[pk], writes=['pre'])

        def load_chunk_w(wchs, ci):
            w = wchs[ci % 2]
            for kc in range(8):
                P.dma('pool', w[:, kc, :], wmain_d[kc * 128:(kc + 1) * 128, ci * 128:(ci + 1) * 128], writes=[('wch', ci % 2)])
            return w

        with ExitStack() as s2:
            hT_all = build_hT(s2)
            wchs = [C.sb(s2, "wch%d" % i, [128, 8, 128], BF16) for i in range(2)]
            wba = C.sb(s2, "wba", [128, 8, 8], BF16)
            load_w_bf16(C, wba, 'wba', wba_d, 8, 8)
            ba = C.sb(s2, "ba", [128, NKT, 8], F32)
            dtb = C.sb(s2, "dtb", [128, NKT, 4], F32)
            nea = C.sb(s2, "nea", [128, NKT, 4], F32)
            t4a = C.sb(s2, "t4a", [128, NKT, 4], F32)
            t4b = C.sb(s2, "t4b", [128, NKT, 4], F32)
            P.dma('sp', dtb[:].rearrange("p a b -> p (a b)"), dtb_d[0:1, :].broadcast_to([128, NKT * 4]), writes=['dtb'])
            P.dma('sp', nea[:].rearrange("p a b -> p (a b)"), alog_d[0:1, :].broadcast_to([128, NKT * 4]), writes=['nea'])
            ps, pk = allps.next()
            for t in range(NKT):
                for kc in range(8):
                    C.mm(ps[:, 8 * t:8 * t + 8], hT_all[:, kc, t * 128:(t + 1) * 128], wba[:, kc, :], kc == 0, kc == 7,
                         reads=['wba'] + [('hT', b[0]) for b in KV_BLOCKS], writes=[pk])
            C.copy('dve', ba[:].rearrange("p a b -> p (a b)"), ps[:, 0:NKT * 8], reads=[pk], writes=['ba'])
            C.act(beta[:], ba[:, :, 0:4], AF.Sigmoid, reads=['ba'], writes=['beta'])
            C.ts('dve', nbeta[:], beta[:], -1.0, None, ALU.mult, ALU.bypass, reads=['beta'], writes=['nbeta'])
            C.tt('dve', t4a[:], ba[:, :, 4:8], dtb[:], ALU.add, reads=['ba', 'dtb'], writes=['t4a'])
            C.act(t4b[:], t4a[:], AF.Abs, reads=['t4a'], writes=['t4b'])
            C.act(t4b[:], t4b[:], AF.Exp, reads=['t4b'], writes=['t4b'], scale=-1.0)
            C.act(t4b[:], t4b[:], AF.Ln, reads=['t4b'], writes=['t4b'], bias=1.0, scale=1.0)
            C.ts('dve', t4a[:], t4a[:], 0.0, None, ALU.max, ALU.bypass, reads=['t4a'], writes=['t4a'])
            C.tt('dve', t4a[:], t4a[:], t4b[:], ALU.add, reads=['t4a', 't4b'], writes=['t4a'])
            C.act(nea[:], nea[:], AF.Exp, reads=['nea'], writes=['nea'])
            C.ts('dve', nea[:], nea[:], -1.0, None, ALU.mult, ALU.bypass, reads=['nea'], writes=['nea'])
            C.tt('dve', gall[:, :, 0:4], t4a[:], nea[:], ALU.mult, reads=['t4a', 'nea', 'gall'], writes=['gall'])
            pre = C.sb(s2, "pre", [128, TA + 6], F32)
            P.op('pool', 'memset', pre[:], 0.0, writes=['pre'])
            grs = C.sb(s2, "grs", [128, TA], BF16)
            hfw = C.sb(s2, "hfw", [128, TA], F32)
            hbw = [C.sb(s2, "hbw%d" % i, [128, 512], F32) for i in range(2)]
            hctx = C.sb(s2, "hctx", [128, 1], F32)
            L = {}
            for nm in ['xlb', 'r', 'ii', 'a', 'a2', 'bb']:
                L[nm] = C.sb(s2, "l_" + nm, [128, 512], F32)
            xlbf = C.sb(s2, "l_xlbf", [128, 512], BF16)
            ymix = [C.sb(s2, "ymix%d" % i, [128, 512], BF16) for i in range(2)]
            for i in range(2):
                wx = load_chunk_w(wchs, 8 + i)
                fill_pre(pre, wx, hT_all, 8 + i)
                wg = load_chunk_w(wchs, 10 + i)
                for (r0, n, j) in KV_BLOCKS:
                    ps, pk = allps.next()
                    for kc in range(8):
                        C.mm(ps[:, 0:n], wg[:, kc, :], hT_all[:, kc, r0:r0 + n], kc == 0, kc == 7,
                             reads=[('wch', (10 + i) % 2), ('hT', r0)], writes=[pk])
                    C.act(grs[:, r0:r0 + n], ps[:, 0:n], AF.Silu, reads=[pk], writes=['grs'])
                for d in range(2):
                    order = KV_BLOCKS if d == 0 else [KV_BLOCKS[0]] + KV_BLOCKS[:0:-1]
                    for bi, (r0, n, j) in enumerate(order):
                        conv_block(pre, L['xlb'], 6 + i, r0, n, bias=convb[:, i:i + 1])
                        C.copy('pool', xlbf[:, 0:n], L['xlb'][:, 0:n], reads=['cvb'], writes=['xlbf'])
                        er, ei = (d * 2 + 0) * 2 + i, (d * 2 + 1) * 2 + i
                        psr, pkr = allps.next()
                        C.mm(psr[:, 0:n], wgate[:, er, :], xlbf[:, 0:n], True, True, reads=['wgate', 'xlbf'], writes=[pkr])
                        psi, pki = allps.next()
                        C.mm(psi[:, 0:n], wgate[:, ei, :], xlbf[:, 0:n], True, True, reads=['wgate', 'xlbf'], writes=[pki])
                        C.act(L['r'][:, 0:n], psr[:, 0:n], AF.Sigmoid, reads=[pkr, 'cw'], writes=['l_r'], bias=gbcol[:, er:er + 1], scale=1.0)
                        C.act(L['ii'][:, 0:n], psi[:, 0:n], AF.Sigmoid, reads=[pki, 'cw'], writes=['l_ii'], bias=gbcol[:, ei:ei + 1], scale=1.0)
                        C.ts('dve', L['r'][:, 0:n], L['r'][:, 0:n], c8[:, d * 2 + i:d * 2 + i + 1], None, ALU.mult, ALU.bypass,
                             reads=['l_r', 'c8'], writes=['l_r'])
                        C.act(L['a'][:, 0:n], L['r'][:, 0:n], AF.Exp, reads=['l_r'], writes=['l_a'])
                        C.act(L['a2'][:, 0:n], L['r'][:, 0:n], AF.Exp, reads=['l_r'], writes=['l_a2'], scale=2.0)
                        C.ts('dve', L['a2'][:, 0:n], L['a2'][:, 0:n], -1.0, 1.0, ALU.mult, ALU.add, reads=['l_a2'], writes=['l_a2'])
                        C.act(L['a2'][:, 0:n], L['a2'][:, 0:n], AF.Sqrt, reads=['l_a2'], writes=['l_a2'])
                        C.tt('pool', L['ii'][:, 0:n], L['ii'][:, 0:n], L['xlb'][:, 0:n], ALU.mult, reads=['l_ii', 'cvb'], writes=['l_ii'])
                        C.tt('pool', L['bb'][:, 0:n], L['a2'][:, 0:n], L['ii'][:, 0:n], ALU.mult, reads=['l_a2', 'l_ii'], writes=['l_bb'])
                        if d == 0:
                            init = 0.0 if bi == 0 else hfw[:, r0 - 1:r0]
                            P.op('dve', 'tensor_tensor_scan', out=hfw[:, r0:r0 + n], data0=L['a'][:, 0:n], data1=L['bb'][:, 0:n],
                                 initial=init, op0=ALU.mult, op1=ALU.add, reads=['l_a', 'l_bb', 'hfw'], writes=['hfw'])
                        else:
                            hb, hk = hbw[bi % 2], ('hbw', bi % 2)
                            if bi == 0:
                                init = 0.0
                            elif bi == 1:
                                init = hctx[:, 0:1]
                            else:
                                init = hbw[(bi - 1) % 2][:, 0:1]
                            P.op('dve', 'tensor_tensor_scan', out=hb[:, 0:n][:, ::-1], data0=L['a'][:, 0:n][:, ::-1],
                                 data1=L['bb'][:, 0:n][:, ::-1], initial=init, op0=ALU.mult, op1=ALU.add,
                                 reads=['l_a', 'l_bb', ('hbw', (bi - 1) % 2), 'hctx'], writes=[hk])
                            if bi == 0:
                                C.copy('dve', hctx[:, 0:1], hb[:, 0:1], reads=[hk], writes=['hctx'])
                            ym, yk = ymix[bi % 2], ('ymix', bi % 2)
                            C.tt('dve', L['xlb'][:, 0:n], hfw[:, r0:r0 + n], hb[:, 0:n], ALU.add, reads=['hfw', hk, 'cvb'], writes=['cvb'])
                            C.tt('pool', ym[:, 0:n], L['xlb'][:, 0:n], grs[:, r0:r0 + n], ALU.mult, reads=['cvb', 'grs'], writes=[yk])
                            P.dma('sp', mix_dst(2 + i, r0, n), ym[:, 0:n], reads=[yk])
            _drain(C)

        qnT = C.sb(st, "qnT", [128, 2, TA], BF16)
        knT = C.sb(st, "knT", [128, 2, TA], BF16)
        ktm = C.sb(st, "ktm", [128, 2, NKT, 128], BF16)
        vtm = C.sb(st, "vtm", [128, 2, NKT, 128], BF16)
        zsT = C.sb(st, "zsT", [128, 2, TA], BF16)
        with ExitStack() as s2:
            hT_all = build_hT(s2)
            wchs = [C.sb(s2, "wch%d" % i, [128, 8, 128], BF16) for i in range(2)]
            pre = C.sb(s2, "pre", [128, TA + 6], F32)
            P.op('pool', 'memset', pre[:], 0.0, writes=['pre'])
            cvb = C.sb(s2, "cvb", [128, 512], F32)
            sq = C.sb(s2, "g_sq", [128, 512], BF16)
            rt = C.sb(s2, "g_rt", [128, 512], F32)
            knf = C.sb(s2, "g_knf", [128, 512], F32)
            for hh in range(2):
                wz = load_chunk_w(wchs, 6 + hh)
                for (r0, n, j) in KV_BLOCKS:
                    ps, pk = allps.next()
                    for kc in range(8):
                        C.mm(ps[:, 0:n], wz[:, kc, :], hT_all[:, kc, r0:r0 + n], kc == 0, kc == 7,
                             reads=[('wch', (6 + hh) % 2), ('hT', r0)], writes=[pk])
                    C.act(zsT[:, hh, r0:r0 + n], ps[:, 0:n], AF.Silu, reads=[pk], writes=['zsT'])
            for ci in range(6):
                kind, hh = ci // 2, ci % 2
                wc = load_chunk_w(wchs, ci)
                fill_pre(pre, wc, hT_all, ci)
                for (r0, n, j) in KV_BLOCKS:
                    conv_block(pre, cvb, ci, r0, n)
                    C.act(cvb[:, 0:n], cvb[:, 0:n], AF.Silu, reads=['cvb'], writes=['cvb'])
                    src = cvb
                    if kind < 2:
                        C.act(sq[:, 0:n], cvb[:, 0:n], AF.Square, reads=['cvb'], writes=['g_sq'])
                        ps2, pk2 = allps.next()
                        C.mm(ps2[:, 0:n], K['ones_bf'][:], sq[:, 0:n], True, True, reads=['ones_bf', 'g_sq'], writes=[pk2])
                        C.act(rt[:, 0:n], ps2[:, 0:n], AF.Sqrt, reads=[pk2], writes=['g_rt'], bias=float(EPS), scale=1.0)
                        C.recip(rt[:, 0:n], rt[:, 0:n], reads=['g_rt'], writes=['g_rt'])
                        if kind == 0:
                            C.stt(qnT[:, hh, r0:r0 + n], cvb[:, 0:n], float(128.0 ** -0.5), rt[:, 0:n], ALU.mult, ALU.mult,
                                  reads=['cvb', 'g_rt'], writes=['qnT'])
                            continue
                        C.tt('dve', knf[:, 0:n], cvb[:, 0:n], rt[:, 0:n], ALU.mult, reads=['cvb', 'g_rt'], writes=['g_knf'])
                        C.copy('pool', knT[:, hh, r0:r0 + n], knf[:, 0:n], reads=['g_knf'], writes=['knT'])
                        src = knf
                    skey = 'g_knf' if kind == 1 else 'cvb'
                    dstm, dkey = (ktm, 'ktm') if kind == 1 else (vtm, 'vtm')
                    ps, pk = allps.next()
                    nt = n // 128
                    for tt in range(nt):
                        P.op('pe', 'transpose', ps[:, tt * 128:(tt + 1) * 128], src[:, tt * 128:(tt + 1) * 128], K['ident'][:],
                             reads=[skey, 'ident'], writes=[pk])
                    for tt in range(nt):
                        C.copy('dve' if tt % 2 == 0 else 'act', dstm[:, hh, r0 // 128 + tt, :], ps[:, tt * 128:(tt + 1) * 128],
                               reads=[pk], writes=[dkey]) if tt % 2 == 0 else \
                            C.act(dstm[:, hh, r0 // 128 + tt, :], ps[:, tt * 128:(tt + 1) * 128], AF.Copy, reads=[pk], writes=[dkey])
            _drain(C)

        with ExitStack() as s2:
            oacc = C.sb(s2, "oacc", [128, 2, NKT, 128], F32)
            regs = [(C.psum[bnk][0][:, 0:128], C.psum[bnk][1]) for bnk in range(8)]
            RR = Ring(regs)
            NS = 3
            S = {}
            for m in range(4):
                S[m] = dict(
                    wT=[C.sb(s2, "wT%d_%d" % (m, i), [128, 128], BF16) for i in range(NS)],
                    aT=[C.sb(s2, "aT%d_%d" % (m, i), [128, 128], BF16) for i in range(NS)],
                    u=[C.sb(s2, "u%d_%d" % (m, i), [128, 128], F32) for i in range(NS)],
                    kd=[C.sb(s2, "kd%d_%d" % (m, i), [128, 2, 128], BF16) for i in range(NS)],
                    gsm=[C.sb(s2, "gsm%d_%d" % (m, i), [128, 10], F32) for i in range(NS)],
                    Z=[C.sb(s2, "Z%d_%d" % (m, i), [128, 128], F32) for i in range(2)],
                    ZT=[C.sb(s2, "ZT%d_%d" % (m, i), [128, 128], F32) for i in range(2)],
                    Y=[C.sb(s2, "Y%d_%d" % (m, i), [128, 128], F32) for i in range(2)],
                    grep=C.sb(s2, "grep%d" % m, [128, 128], F32),
                    DS=C.sb(s2, "DS%d" % m, [128, 128], F32),
                    DT=C.sb(s2, "DT%d" % m, [128, 128], F32),
                    bv=C.sb(s2, "bv%d" % m, [128, 128], F32),
                    kbg=C.sb(s2, "kbg%d" % m, [128, 128], F32),
                    Sf=[C.sb(s2, "Sf%d_%d" % (m, i), [128, 128], F32) for i in range(2)],
                    Sb=[C.sb(s2, "Sb%d_%d" % (m, i), [128, 128], BF16) for i in range(2)],
                    vn=[C.sb(s2, "vn%d_%d" % (m, i), [128, 128], BF16) for i in range(2)],
                    tB=C.sb(s2, "tB%d" % m, [128, 128], F32),
                    oo=C.sb(s2, "oo%d" % m, [128, 128], F32),
                    cur=0, nblk=0,
                )
                P.op('dve', 'memset', S[m]['Sf'][0][:], 0.0, writes=[('Sf', m, 0)])
                P.op('dve', 'memset', S[m]['Sb'][0][:], 0.0, writes=[('Sb', m, 0)])
                for i_ in range(2):
                    P.op('dve', 'memset', S[m]['vn'][i_][:], 0.0, writes=[('vn', m, i_)])
            identf = K['ident']

            def prep(m, t, sl):
                d, hh = m // 2, m % 2
                X = S[m]
                kn = knT[:, hh, t * 128:(t + 1) * 128]
                qn = qnT[:, hh, t * 128:(t + 1) * 128]
                kt_ = ktm[:, hh, t, :]
                vt_ = vtm[:, hh, t, :]
                gcol, bcol, nbcol = gall[:, t, m:m + 1], beta[:, t, m:m + 1], nbeta[:, t, m:m + 1]
                tri, penS, penT = cms[:, d, :], cms[:, 4 + d, :], cms[:, 6 + d, :]
                gsm, gk = X['gsm'][sl], ('gsm', m, sl)
                bank, bkey = C.psum[m]
                Q = lambda i: bank[:, i * 128:(i + 1) * 128]
                pKK, kKK = Q(0), bkey
                C.mm(pKK, kn, kn, True, True, reads=['knT'], writes=[kKK])
                pQK, kQK = Q(1), bkey
                C.mm(pQK, kn, qn, True, True, reads=['knT', 'qnT'], writes=[kQK])
                pg, kg = Q(2), bkey
                for ci_, lh in enumerate([tri, cms[:, 8, :], cms[:, 2, :], cms[:, 3, :]]):
                    C.mm(pg[:, 2 * ci_:2 * ci_ + 2], lh, gall[:, t, m:m + 2], True, True, reads=['cms', 'gall'], writes=[kg])
                C.ts('dve', X['grep'][:], K['ones_f'][:], gcol, None, ALU.mult, ALU.bypass, reads=['ones_f', 'gall'], writes=[('grep', m)])
                yield
                C.copy('dve', gsm[:, 0:4], pg[:, 0:8:2], reads=[kg], writes=[gk])
                yield
                pD, kD = Q(2), bkey
                C.mm(pD, X['grep'][:], tri, True, False, reads=[('grep', m), 'cms'], writes=[kD])
                C.mm(pD, identf[:], penS, False, True, reads=['ident', 'cms'], writes=[kD])
                pDT, kDT = Q(3), bkey
                C.mm(pDT, X['grep'][:], tri, True, False, reads=[('grep', m), 'cms'], writes=[kDT])
                C.mm(pDT, identf[:], penT, False, True, reads=['ident', 'cms'], writes=[kDT])
                yield
                C.act(gsm[:, 4:5], gsm[:, 0:1], AF.Exp, reads=[gk], writes=[gk])
                C.act(gsm[:, 5:6], gsm[:, 0:1], AF.Exp, reads=[gk], writes=[gk], scale=-1.0, bias=gsm[:, 1:2])
                C.act(gsm[:, 7:9], gsm[:, 2:4], AF.Exp, reads=[gk], writes=[gk])
                C.tt('dve', gsm[:, 6:7], bcol, gsm[:, 4:5], ALU.mult, reads=[gk, 'beta'], writes=[gk])
                yield
                C.ts('dve', X['DS'][:], pD, gsm[:, 0:1], 0.0, ALU.subtract, ALU.max, reads=[kD, gk], writes=[('DS', m)])
                C.act(X['DS'][:], X['DS'][:], AF.Exp, reads=[('DS', m)], writes=[('DS', m)], scale=-1.0)
                C.ts('dve', X['DT'][:], pDT, gsm[:, 0:1], 0.0, ALU.subtract, ALU.min, reads=[kDT, gk], writes=[('DT', m)])
                yield
                C.act(X['DT'][:], X['DT'][:], AF.Exp, reads=[('DT', m)], writes=[('DT', m)])
                yield
                Z, ZT, Y = X['Z'], X['ZT'], X['Y']
                zk = lambda i: ('Z', m, i)
                ztk = lambda i: ('ZT', m, i)
                yk = lambda i: ('Y', m, i)
                C.stt(ZT[0][:], pKK, nbcol, X['DS'][:], ALU.mult, ALU.mult, reads=[kKK, 'nbeta', ('DS', m)], writes=[ztk(0)])
                C.tt('dve', X['aT'][sl][:], pQK, X['DT'][:], ALU.mult, reads=[kQK, ('DT', m)], writes=[('aT', m, sl)])
                yield
                pZ, kZ = Q(0), bkey
                P.op('pe', 'transpose', pZ, ZT[0][:], identf[:], reads=[ztk(0), 'ident'], writes=[kZ])
                yield
                C.act(Z[0][:], pZ, AF.Copy, reads=[kZ], writes=[zk(0)])
                C.tt('dve', Y[0][:], pZ, identf[:], ALU.add, reads=[kZ, 'ident'], writes=[yk(0)])
                yield
                for k in range(5):
                    cur, nxt = k % 2, 1 - k % 2
                    p1, k1 = Q(0), bkey
                    C.mm(p1, Z[cur][:], ZT[cur][:], True, True, reads=[zk(cur), ztk(cur)], writes=[k1])
                    if k < 4:
                        p2, k2 = Q(1), bkey
                        C.mm(p2, ZT[cur][:], Z[cur][:], True, True, reads=[zk(cur), ztk(cur)], writes=[k2])
                        yield
                    if k == 4:
                        yield
                    C.act(ZT[nxt][:], p1, AF.Copy, reads=[k1], writes=[ztk(nxt)])
                    if k == 4:
                        yield
                    if k < 4:
                        C.copy('dve', Z[nxt][:], p2, reads=[k2], writes=[zk(nxt)])
                        yield
                    p3, k3 = Q(2), bkey
                    C.mm(p3, ZT[nxt][:], Y[cur][:], True, True, reads=[ztk(nxt), yk(cur)], writes=[k3])
                    yield
                    C.tt('dve', Y[nxt][:], Y[cur][:], p3, ALU.add, reads=[yk(cur), k3], writes=[yk(nxt)])
                    yield
                Yf, Yk = Y[1], yk(1)
                C.ts('dve', X['bv'][:], vt_, bcol, None, ALU.mult, ALU.bypass, reads=['vtm', 'beta'], writes=[('bv', m)])
                C.ts('dve', X['kbg'][:], kt_, gsm[:, 6:7], None, ALU.mult, ALU.bypass, reads=['ktm', gk], writes=[('kbg', m)])
                pU, kU = Q(1), bkey
                C.mm(pU, Yf[:], X['bv'][:], True, True, reads=[Yk, ('bv', m)], writes=[kU])
                pW, kW = Q(2), bkey
                C.mm(pW, X['kbg'][:], Yf[:], True, True, reads=[Yk, ('kbg', m)], writes=[kW])
                yield
                C.copy('dve', X['u'][sl][:], pU, reads=[kU], writes=[('u', m, sl)])
                C.act(X['wT'][sl][:], pW, AF.Copy, reads=[kW], writes=[('wT', m, sl)])
                for blk_ in range(2):
                    C.tt('dve', gsm[:, 9:10], gsm[:, 5:6], cms[:, 2 + blk_, 0:1], ALU.mult, reads=[gk, 'cms'], writes=[gk])
                    C.ts('dve', X['kd'][sl][:, blk_, :], kt_, gsm[:, 9:10], None, ALU.mult, ALU.bypass, reads=['ktm', gk],
                         writes=[('kd', m, sl)])

            def seq(m, t, sl, first):
                d, hh = m // 2, m % 2
                X = S[m]
                gsm, gk = X['gsm'][sl], ('gsm', m, sl)
                bank, bkey = C.psum[4 + m]
                Q = lambda i: bank[:, i * 128:(i + 1) * 128]
                for blk in ([0, 1] if d == 0 else [1, 0]):
                    R = slice(64 * blk, 64 * blk + 64)
                    cur = X['cur']
                    nxt = 1 - cur
                    vi = X['nblk'] % 2
                    X['nblk'] += 1
                    vn, vk = X['vn'][vi], ('vn', m, vi)
                    Sb, Sbk = X['Sb'][cur], ('Sb', m, cur)
                    p1, k1 = Q(0), bkey
                    C.mm(p1, X['wT'][sl][:], Sb[:], True, True, reads=[('wT', m, sl), Sbk], writes=[k1])
                    yield
                    C.tt('dve', vn[R, :], X['u'][sl][R, :], p1[R, :], ALU.subtract, reads=[('u', m, sl), k1], writes=[vk])
                    yield
                    pA, kA = Q(1), bkey
                    C.mm(pA, qnT[:, hh, t * 128:(t + 1) * 128], Sb[:], True, True, reads=['qnT', Sbk], writes=[kA])
                    pB, kB = Q(2), bkey
                    C.mm(pB, X['aT'][sl][:], vn[:], True, True, reads=[('aT', m, sl), vk], writes=[kB])
                    pS, kS = Q(3), bkey
                    C.mm(pS, X['kd'][sl][:, blk, :], vn[:], True, True, reads=[('kd', m, sl), vk], writes=[kS])
                    yield
                    C.stt(X['Sf'][nxt][:], X['Sf'][cur][:], gsm[:, 7 + blk:8 + blk], pS, ALU.mult, ALU.add,
                          reads=[('Sf', m, cur), gk, kS], writes=[('Sf', m, nxt)])
                    C.act(X['Sb'][nxt][:], X['Sf'][nxt][:], AF.Copy, reads=[('Sf', m, nxt)], writes=[('Sb', m, nxt)])
                    X['cur'] = nxt
                    C.act(X['tB'][R, :], pB[R, :], AF.Copy, reads=[kB], writes=[('tB', m)])
                    yield
                    ok = ('oacc', hh, t)
                    if first:
                        C.stt(oacc[R, hh, t, :], pA[R, :], gsm[R, 4:5], X['tB'][R, :], ALU.mult, ALU.add,
                              reads=[kA, gk, ('tB', m)], writes=[ok])
                    else:
                        C.stt(X['oo'][R, :], pA[R, :], gsm[R, 4:5], X['tB'][R, :], ALU.mult, ALU.add,
                              reads=[kA, gk, ('tB', m)], writes=[('oo', m)])
                        C.tt('pool', oacc[R, hh, t, :], oacc[R, hh, t, :], X['oo'][R, :], ALU.add, reads=[ok, ('oo', m)], writes=[ok])

            ORD = [list(range(NKT)), [1, 0] + list(range(NKT - 1, 1, -1))]
            STEP = [{t: i for i, t in enumerate(o)} for o in ORD]

            def tile_of(m, s):
                return ORD[m // 2][s]
            def interleave(gens):
                gens = list(gens)
                while gens:
                    for g in list(gens):
                        try:
                            next(g)
                        except StopIteration:
                            gens.remove(g)
            interleave([prep(m, tile_of(m, 0), 0) for m in range(4)])
            for s in range(NKT):
                gens = []
                for m in range(4):
                    t = tile_of(m, s)
                    d_ = m // 2
                    first = STEP[d_][t] < STEP[1 - d_][t] or (STEP[d_][t] == STEP[1 - d_][t] and d_ == 0)
                    gens.append(seq(m, t, s % NS, first))
                    if s + 1 < NKT:
                        gens.append(prep(m, tile_of(m, s + 1), (s + 1) % NS))
                interleave(gens)
            ssq = C.sb(s2, "f_ssq", [128, 2], F32)
            junk = C.sb(s2, "f_junk", [128, 128], F32)
            on = C.sb(s2, "f_on", [128, 128], F32)
            stg = [C.sb(s2, "f_stg%d" % i, [128, 512], BF16) for i in range(2)]
            gi = 0
            for hh in range(2):
                for t0 in range(0, NKT, 4):
                    nt = min(4, NKT - t0)
                    sg, sk = stg[gi % 2], ('stg', gi % 2)
                    gi += 1
                    for tt in range(nt):
                        t = t0 + tt
                        ok = ('oacc', hh, t)
                        C.act(junk[:], oacc[:, hh, t, :], AF.Square, reads=[ok], writes=['f_junk', 'f_ssq'], accum_out=ssq[:, 0:1])
                        C.act(ssq[:, 1:2], ssq[:, 0:1], AF.Sqrt, reads=['f_ssq'], writes=['f_ssq'], bias=float(EPS), scale=1.0 / 128)
                        C.recip(ssq[:, 1:2], ssq[:, 1:2], reads=['f_ssq'], writes=['f_ssq'])
                        C.ts('dve', on[:], oacc[:, hh, t, :], ssq[:, 1:2], None, ALU.mult, ALU.bypass, reads=[ok, 'f_ssq'], writes=['f_on'])
                        pT, kT = RR.next()
                        P.op('pe', 'transpose', pT, on[:], identf[:], reads=['f_on', 'ident'], writes=[kT])
                        C.stt(sg[:, tt * 128:(tt + 1) * 128], pT, gnorm[:, 0:1], zsT[:, hh, t * 128:(t + 1) * 128], ALU.mult, ALU.mult,
                              reads=[kT, 'cw', 'zsT'], writes=[sk])
                    P.dma('sp', mix_dst(hh, t0 * 128, nt * 128), sg[:, 0:nt * 128], reads=[sk])
            _drain(C)


def build_recB():
    nc = bass.Bass("TRN2", target_bir_lowering=False)
    with ExitStack() as st:
        C = Ctx(nc, st)
        emit_recB(C, st)
        C.P.finish()
    return nc


def emit_recB(C, st, K=None, ntok=NOWN):
    if True:
        P = C.P
        xown = C.dram_in("xown", [ntok, D])
        mixT_d = C.dram_in("mixT", [8, 128, ntok], BF16)
        cc_d = C.dram_in("cc", [128, 16])
        modw_d = C.dram_in("modw", [D, 3 * D])
        modbcol_d = C.dram_in("modb_col", [128, 24])
        modbgate_d = C.dram_in("modb_gate", [1, D])
        lng_d = C.dram_in("lng", [1, D])
        lnb_d = C.dram_in("lnb", [1, D])
        wout_d = C.dram_in("wout", [D, D])
        ident_d = C.dram_in("ident", [128, 128])
        xo = C.dram_out("xo", [ntok, D])
        K = K or emit_consts(C, st, ident_d)
        lnbc = C.sb(st, "lnbc", [128, 2, 1024], F32)
        P.dma('sp', lnbc[:, 0, :], lng_d[0:1, :].broadcast_to([128, D]), writes=['lnbc'])
        P.dma('sp', lnbc[:, 1, :], lnb_d[0:1, :].broadcast_to([128, D]), writes=['lnbc'])
        _, gate_bc = emit_modulation(C, st, K, cc_d, modw_d, modbcol_d, modbgate_d, need_cols=False)
        wout = C.sb(st, "wout", [128, 8, 1024], BF16)
        load_w_bf16(C, wout, 'wout', wout_d, 8, 1024)
        mixT = C.sb(st, "mixT", [128, 8, ntok], BF16)
        for c in range(8):
            P.dma('sp', mixT[:, c, :], mixT_d[c], writes=[('mixT', c)])
        E = alloc_epilogue(C, st)
        allps = Ring(C.psum)
        for ti in range(ntok // 128):
            r0 = ti * 128
            j = 1 if r0 < TC else 0
            emit_epilogue_tile(C, E, lambda c, r0=r0: (mixT[:, c, r0:r0 + 128], [('mixT', c)]), wout,
                               xown[r0:r0 + 128, :], xo[r0:r0 + 128, :], j, gate_bc, lnbc, allps, ti)
        _drain(C)


def run_rec_layer(inp, l, x, ctx):
    li = l // 2
    com = _common_layer(inp, l)
    ncA = _get('recA', build_recA)
    in_maps = []
    for core in range(8):
        b, hf = core // 2, core % 2
        m = dict(xall=np.ascontiguousarray(np.concatenate([ctx[b], x[b]], axis=0)), cc=_cc_cols(inp['c'][b], inp['c_ctx']),
                 modw=com['modw'], modb_col=com['modb_col'], modb_gate=com['modb_gate'], ident=com['ident'])
        m.update(prep_rec_layer(inp, l, hf))
        in_maps.append(m)
    resA = run_bass_kernel_spmd(ncA, in_maps, core_ids=list(range(8)))
    ncB = _get('recB', build_recB)
    wout = np.asarray(inp['rec_w_out'][li], np.float32)
    in_maps = []
    for core in range(8):
        b, hf = core // 2, core % 2
        mo = [resA.results[2 * b + k]["mixo"] for k in range(2)]
        full = np.concatenate([mo[0][0:2], mo[1][0:2], mo[0][2:4], mo[1][2:4]], axis=0)
        sel = np.concatenate([np.arange(TC), TC + hf * OWN + np.arange(OWN)])
        m = dict(xown=np.ascontiguousarray(np.concatenate([ctx[b], x[b, hf * OWN:(hf + 1) * OWN]], axis=0)),
                 mixT=np.ascontiguousarray(full[:, :, sel]), cc=_cc_cols(inp['c'][b], inp['c_ctx']), wout=wout)
        m.update(com)
        in_maps.append(m)
    resB = run_bass_kernel_spmd(ncB, in_maps, core_ids=list(range(8)))
    xn = np.empty_like(x)
    cn = np.empty_like(ctx)
    for core in range(8):
        b, hf = core // 2, core % 2
        o = resB.results[core]["xo"]
        xn[b, hf * OWN:(hf + 1) * OWN] = o[TC:]
        if hf == 0:
            cn[b] = o[:TC]
    return xn, cn


def build_fused():
    nc = bass.Bass("TRN2", target_bir_lowering=False)
    with ExitStack() as st:
        C = Ctx(nc, st)
        ident_d = C.dram_in("ident", [128, 128])
        K = emit_consts(C, st, ident_d)
        x_in = C.dram_in("x_in", [TA, D])
        out = C.dram_out("xfin", [TA, D])
        xb = [nc.dram_tensor("xb%d" % i, [TA, D], F32, kind="Internal").ap() for i in range(2)]
        mixb = nc.dram_tensor("mixb", [8, 128, TA], BF16, kind="Internal").ap()
        for l in range(4):
            src = x_in if l == 0 else xb[(l - 1) % 2]
            dst = out if l == 3 else xb[l % 2]
            C.lsfx = "_L%d" % l
            if l % 2 == 0:
                C.sfx = "_L%d" % l
                C.override = dict(xall=src, xown=src, xo=dst)
                with ExitStack() as ph:
                    emit_att(C, ph, K, full=True)
            else:
                for hfr in range(2):
                    C.sfx = "_L%d_%d" % (l, hfr)
                    C.override = dict(xall=src)
                    with ExitStack() as ph:
                        emit_recA(C, ph, K, mix_dst=lambda idx, c0, n, hfr=hfr: mixb[
                            (2 * hfr + idx) if idx < 2 else (4 + 2 * hfr + idx - 2), :, c0:c0 + n])
                C.sfx = "_L%d" % l
                C.override = dict(xown=src, mixT=mixb, xo=dst)
                with ExitStack() as ph:
                    emit_recB(C, ph, K, ntok=TA)
        C.P.finish()
    return nc


def fused_inputs(inp, b):
    m = dict(ident=np.eye(128, dtype=np.float32), cc=_cc_cols(inp['c'][b], inp['c_ctx']),
             x_in=np.ascontiguousarray(np.concatenate([inp['ctx'][b], inp['x'][b]], axis=0), dtype=np.float32))
    LAYER = Ctx.LAYER
    for l in range(4):
        com = _common_layer(inp, l)
        for k in LAYER:
            m[k + "_L%d" % l] = com[k]
        if l % 2 == 0:
            sh, (tabKg, tabKm) = prep_att_layer(inp, l)
            for k, v in sh.items():
                if k not in LAYER and k != 'ident':
                    m[k + "_L%d" % l] = v
            m["tabKg_L%d" % l] = tabKg
            m["tabKm_L%d" % l] = tabKm
        else:
            for hfr in range(2):
                for k, v in prep_rec_layer(inp, l, hfr).items():
                    m[k + "_L%d_%d" % (l, hfr)] = v
            m["wout_L%d" % l] = np.asarray(inp['rec_w_out'][l // 2], np.float32)
    return m


def kernel(**inputs):
    inp = {k: np.asarray(v) for k, v in inputs.items()}
    nc = _get('fused', build_fused)
    per_b = [fused_inputs(inp, b) for b in range(NB)]
    in_maps = [per_b[core // 2] for core in range(8)]
    res = run_bass_kernel_spmd(nc, in_maps, core_ids=list(range(8)))
    out = np.empty((NB, TL, D), np.float32)
    for b in range(NB):
        out[b] = res.results[2 * b]["xfin"][TC:]
    return out


def kernel_unfused(**inputs):
    inp = {k: np.asarray(v) for k, v in inputs.items()}
    x = np.ascontiguousarray(inp['x'], dtype=np.float32)
    ctx = np.ascontiguousarray(inp['ctx'], dtype=np.float32)
    for l in range(4):
        if l % 2 == 0:
            x, ctx = run_att_layer(inp, l, x, ctx)
        else:
            x, ctx = run_rec_layer(inp, l, x, ctx)
    return x
```

```python
import numpy as np
from contextlib import ExitStack
import concourse.bass as bass
import concourse.mybir as mybir
from concourse.bass_utils import run_bass_kernel_spmd

F32 = mybir.dt.float32
BF16 = mybir.dt.bfloat16
AF = mybir.ActivationFunctionType
ALU = mybir.AluOpType

D = 1024
NB = 4
TL = 4096
TC = 256
TA = TL + TC
OWN = 2048
NOWN = OWN + TC
EPS = 1e-6
ALPHA = 8.0 ** 0.25
THETA = 10000.0


class Prog:
    def __init__(self, nc, stack, n_dma=12, same_engine_sync=True):
        self.nc = nc
        self.stack = stack
        self.eng = {'pe': nc.tensor, 'act': nc.scalar, 'dve': nc.vector, 'pool': nc.gpsimd, 'sp': nc.sync}
        self.sem = {}
        for e in ['pe', 'act', 'dve', 'pool']:
            self.sem[e] = stack.enter_context(nc.semaphore("s_" + e))
        self.cnt = {e: 0 for e in ['pe', 'act', 'dve', 'pool']}
        self.n_dma = n_dma
        self.dq = {}
        for q in ['sp', 'pool']:
            sems = [stack.enter_context(nc.semaphore("d_%s_%d" % (q, i))) for i in range(n_dma)]
            for i, s in enumerate(sems):
                self.sem[('d', q, i)] = s
            self.dq[q] = dict(targets=[0] * n_dma, next=0)
        self.waited = {e: {} for e in self.eng}
        self.lastw = {}
        self.readers = {}
        self.same = same_engine_sync
        self.n_inst = 0

    def _wait(self, E, tok):
        if tok is None:
            return
        key, val = tok
        if key == E and (E == 'pe' or not self.same):
            return
        if self.waited[E].get(key, 0) >= val:
            return
        self.eng[E].wait_ge(self.sem[key], val)
        self.waited[E][key] = val
        self.n_inst += 1

    def _deps(self, E, reads, writes):
        for b in reads:
            self._wait(E, self.lastw.get(b))
            if isinstance(b, str) and b.startswith('ps'):
                for t in self.readers.get(b, ()):
                    if t[0] != E:
                        self._wait(E, t)
        for b in writes:
            self._wait(E, self.lastw.get(b))
            for t in self.readers.get(b, ()):
                self._wait(E, t)

    def _record(self, tok, reads, writes):
        for b in reads:
            lst = self.readers.setdefault(b, [])
            lst[:] = [t for t in lst if t[0] != tok[0]]
            lst.append(tok)
        for b in writes:
            self.lastw[b] = tok
            self.readers[b] = []

    def op(self, E, name, *args, reads=(), writes=(), **kw):
        self._deps(E, reads, writes)
        inst = getattr(self.eng[E], name)(*args, **kw)
        self.cnt[E] += 1
        inst.then_inc(self.sem[E], 1)
        self.n_inst += 1
        self._record((E, self.cnt[E]), reads, writes)
        return inst

    def dma(self, q, out, in_, reads=(), writes=(), **kw):
        self._deps(q, reads, writes)
        d = self.dq[q]
        idx = d['next'] % self.n_dma
        d['next'] += 1
        key = ('d', q, idx)
        if d['targets'][idx] > 0:
            self._wait(q, (key, d['targets'][idx]))
        inst = self.eng[q].dma_start(out=out, in_=in_, **kw)
        d['targets'][idx] += 16
        inst.then_inc(self.sem[key], 16)
        self.n_inst += 1
        self._record((key, d['targets'][idx]), reads, writes)
        return inst

    def finish(self):
        for q in self.dq:
            d = self.dq[q]
            for idx in range(self.n_dma):
                if d['targets'][idx] > 0:
                    self._wait('sp', (('d', q, idx), d['targets'][idx]))
        for e in ['pe', 'act', 'dve', 'pool']:
            if self.cnt[e] > 0:
                self._wait('sp', (e, self.cnt[e]))


def interleave(gens):
    gens = list(gens)
    while gens:
        for g in list(gens):
            try:
                next(g)
            except StopIteration:
                gens.remove(g)


class Ring:
    def __init__(self, items):
        self.items = items
        self.i = 0

    def next(self):
        it = self.items[self.i % len(self.items)]
        self.i += 1
        return it


class Ctx:
    def __init__(self, nc, st):
        self.nc = nc
        self.st = st
        self.P = Prog(nc, st)
        self.psum = []
        for i in range(8):
            t = st.enter_context(nc.psum_tensor("ps%d" % i, [128, 512], F32))
            self.psum.append((t, "ps%d" % i))

    def sb(self, stack, name, shape, dt):
        self.uid = getattr(self, 'uid', 0) + 1
        return stack.enter_context(self.nc.sbuf_tensor("%s_%d" % (name, self.uid), shape, dt))

    SHARED = ('ident', 'cc')
    LAYER = ('modw', 'modb_col', 'modb_gate', 'lng', 'lnb')

    def _dram(self, name, shape, dt, kind):
        ov = getattr(self, 'override', {})
        if name in ov:
            return ov[name]
        if name in self.SHARED:
            full = name
        elif name in self.LAYER:
            full = name + getattr(self, 'lsfx', getattr(self, 'sfx', ''))
        else:
            full = name + getattr(self, 'sfx', '')
        cache = self.__dict__.setdefault('decl', {})
        if full not in cache:
            cache[full] = self.nc.dram_tensor(full, list(shape), dt, kind=kind).ap()
        return cache[full]

    def dram_in(self, name, shape, dt=F32):
        return self._dram(name, shape, dt, "ExternalInput")

    def dram_out(self, name, shape, dt=F32):
        return self._dram(name, shape, dt, "ExternalOutput")

    def mm(self, out, lhsT, rhs, start, stop, reads, writes):
        return self.P.op('pe', 'matmul', out, lhsT, rhs, start=start, stop=stop, reads=reads, writes=writes)

    def act(self, out, in_, func, reads, writes, **kw):
        return self.P.op('act', 'activation', out=out, in_=in_, func=func, reads=reads, writes=writes, **kw)

    def tt(self, E, out, in0, in1, op, reads, writes):
        return self.P.op(E, 'tensor_tensor', out=out, in0=in0, in1=in1, op=op, reads=reads, writes=writes)

    def stt(self, out, in0, scalar, in1, op0, op1, reads, writes):
        return self.P.op('dve', 'scalar_tensor_tensor', out=out, in0=in0, scalar=scalar, in1=in1,
                         op0=op0, op1=op1, reads=reads, writes=writes)

    def ts(self, E, out, in0, s1, s2, op0, op1, reads, writes):
        return self.P.op(E, 'tensor_scalar', out=out, in0=in0, scalar1=s1, scalar2=s2, op0=op0, op1=op1,
                         reads=reads, writes=writes)

    def copy(self, E, out, in_, reads, writes):
        return self.P.op(E, 'tensor_copy', out=out, in_=in_, reads=reads, writes=writes)

    def recip(self, out, in_, reads, writes):
        return self.P.op('dve', 'reciprocal', out=out, in_=in_, reads=reads, writes=writes)


def _rope_tabs(pos_row, pos_col, dim):
    half = dim // 2
    q = half // 2
    inv = THETA ** (-np.arange(0, half, 2, dtype=np.float32) / np.float32(half))
    inv = inv.astype(np.float32)
    ang_r = pos_row.astype(np.float32)[:, None] * inv
    ang_c = pos_col.astype(np.float32)[:, None] * inv
    cr, sr, cc_, sc_ = np.cos(ang_r), np.sin(ang_r), np.cos(ang_c), np.sin(ang_c)
    T = pos_row.shape[0]
    cos = np.zeros((dim, T), np.float32)
    sin = np.zeros((dim, T), np.float32)
    partner = np.zeros(dim, np.int64)
    for d in range(dim):
        grp, i = d // q, d % q
        c, s = (cr, sr) if grp < 2 else (cc_, sc_)
        cos[d] = c[:, i]
        if grp % 2 == 0:
            sin[d] = -s[:, i]
            partner[d] = d + q
        else:
            sin[d] = s[:, i]
            partner[d] = d - q
    return cos, sin, partner


def _att_tables():
    t = np.arange(TL)
    row, col = t // 64, t % 64
    cg, sg, pg = _rope_tabs(row, col, 64)
    cm, sm, pm = _rope_tabs(row, col, 32)
    return cg, sg, pg, cm, sm, pm


def prep_att_layer(inp, l):
    li = l // 2
    cg, sg, pg, cm, sm, pm = _att_tables()
    w_in = np.asarray(inp['att_w_in'][li], np.float32)
    o = np.cumsum([0, 256, 128, 32, 512, 512, 128, 128, 512])
    cq, ckv, kr, ga, qb, kb, vb, gb = [w_in[:, o[i]:o[i + 1]] for i in range(8)]
    hperm = np.concatenate([np.concatenate([np.arange(64) + 64 * c, np.arange(64) + 64 * (4 + c)]) for c in range(4)])
    sw64 = np.concatenate([pg + 64 * h for h in range(8)])
    qb_sw = qb[:, sw64]
    wq = np.concatenate([cq, ga, gb[:, hperm], qb[:, hperm], qb_sw[:, hperm]], axis=1)
    kb_sw = kb[:, np.concatenate([pg, pg + 64])]
    z64 = np.zeros((D, 64), np.float32)
    wkv = np.concatenate([ckv, kb, kb_sw, z64, kr, z64, kr[:, pm], vb], axis=1)
    w_uq = np.asarray(inp['mla_w_uq'][li], np.float32).reshape(256, 8, 96)
    wuqA = w_uq.reshape(256, 768)
    wuqB = np.concatenate([w_uq[:, :, :64], w_uq[:, :, 64:][:, :, pm]], axis=2).reshape(256, 768)
    w_ukv = np.asarray(inp['mla_w_ukv'][li], np.float32).reshape(128, 8, 128)
    wuk = w_ukv[:, :, :64].reshape(128, 512)
    wuv = w_ukv[:, :, 64:].reshape(128, 512)
    w_out = np.asarray(inp['att_w_out'][li], np.float32)
    wout = np.concatenate([w_out[:512], w_out[512:][hperm]], axis=0)
    gq = np.asarray(inp['gqa_q_norm'][li], np.float32)
    gk = np.asarray(inp['gqa_k_norm'][li], np.float32)
    gcols = np.zeros((128, 8), np.float32)
    gcols[:, 0:2] = np.asarray(inp['mla_q_norm'][li], np.float32).reshape(2, 128).T
    gcols[:, 2] = np.asarray(inp['mla_kv_norm'][li], np.float32)
    gcols[:, 3] = np.tile(gq, 2)
    gcols[:, 4] = np.tile(gq[pg], 2)
    gcols[:, 5] = np.tile(gk, 2)
    gcols[:, 6] = np.tile(gk[pg], 2)
    tabKg = np.zeros((2, 128, TA), np.float32)
    tabKg[0, :, :TC] = 1.0
    tabKg[0, :, TC:] = np.tile(cg, (2, 1))
    tabKg[1, :, TC:] = np.tile(sg, (2, 1))
    tabKm = np.zeros((2, 96, TA), np.float32)
    tabKm[0, 64:, :TC] = 1.0
    tabKm[0, 64:, TC:] = cm
    tabKm[1, 64:, TC:] = sm
    sh = dict(wq=wq, wkv=wkv, wuqA=wuqA, wuqB=wuqB, wuk=wuk, wuv=wuv, wout=wout, gcols=gcols)
    sh.update(_common_layer(inp, l))
    return sh, (tabKg, tabKm)


def _common_layer(inp, l):
    mod_b = np.asarray(inp['mod_b'][l], np.float32)
    return dict(
        modw=np.asarray(inp['mod_w'][l], np.float32),
        modb_col=np.ascontiguousarray(mod_b.reshape(24, 128).T),
        modb_gate=np.ascontiguousarray(mod_b[None, 2048:]),
        lng=np.asarray(inp['ln_g'][l], np.float32)[None, :],
        lnb=np.asarray(inp['ln_b'][l], np.float32)[None, :],
        ident=np.eye(128, dtype=np.float32),
    )


def _cc_cols(c_b, c_ctx):
    cc = np.zeros((128, 16), np.float32)
    cc[:, 0::2] = np.asarray(c_b, np.float32).reshape(8, 128).T
    cc[:, 1::2] = np.asarray(c_ctx, np.float32).reshape(8, 128).T
    return cc


def emit_consts(C, st, ident_d):
    P = C.P
    k = {}
    k['ident'] = C.sb(st, "ident", [128, 128], F32)
    P.dma('sp', k['ident'][:], ident_d, writes=['ident'])
    k['ones_f'] = C.sb(st, "ones_f", [128, 128], F32)
    P.op('dve', 'memset', k['ones_f'][:], 1.0, writes=['ones_f'])
    k['ones_bf'] = C.sb(st, "ones_bf", [128, 128], BF16)
    P.op('dve', 'memset', k['ones_bf'][:], 1.0, writes=['ones_bf'])
    k['onesbd'] = C.sb(st, "onesbd", [128, 128], BF16)
    P.op('dve', 'memset', k['onesbd'][:], 0.0, writes=['onesbd'])
    P.op('dve', 'memset', k['onesbd'][0:64, 0:64], 1.0, reads=['onesbd'], writes=['onesbd'])
    P.op('dve', 'memset', k['onesbd'][64:128, 64:128], 1.0, reads=['onesbd'], writes=['onesbd'])
    return k


def emit_modulation(C, st, K, cc_d, modw_d, modbcol_d, modbgate_d, need_cols=True, need_gate=True):
    P = C.P
    modcol = C.sb(st, "modcol", [128, 16, 2], F32)
    gate_bc = C.sb(st, "gate_bc", [128, 2, 1024], F32) if need_gate else None
    with ExitStack() as s2:
        cc = C.sb(s2, "cc", [128, 16], F32)
        sc = C.sb(s2, "sc", [128, 16], F32)
        mbc = C.sb(s2, "mbc", [128, 24], F32)
        mbg = C.sb(s2, "mbg", [128, 1024], F32)
        rep = C.sb(s2, "rep", [128, 2, 8, 128], F32)
        mw = C.sb(s2, "mw", [128, 8, 1024], F32)
        P.dma('sp', cc[:], cc_d, writes=['cc'])
        P.dma('sp', mbc[:], modbcol_d, writes=['mbc'])
        P.dma('sp', mbg[:], modbgate_d[0:1, :].broadcast_to([128, 1024]), writes=['mbg'])
        C.act(sc[:], cc[:], AF.Silu, reads=['cc'], writes=['sc'])
        for j in range(2):
            for kc in range(8):
                C.ts('dve', rep[:, j, kc, :], K['ones_f'][:], sc[:, 2 * kc + j:2 * kc + j + 1], None, ALU.mult,
                     ALU.bypass, reads=['ones_f', 'sc'], writes=['rep'])
        for t in range(3):
            if (t < 2 and not need_cols) or (t == 2 and not need_gate):
                continue
            for kc in range(8):
                P.dma('sp', mw[:, kc, :], modw_d[kc * 128:(kc + 1) * 128, t * 1024:(t + 1) * 1024], writes=['mw'])
            if t < 2:
                ps, pk = C.psum[t]
                for oc in range(8):
                    for kc in range(8):
                        C.mm(ps[:, 2 * oc:2 * oc + 2], mw[:, kc, oc * 128:(oc + 1) * 128], sc[:, 2 * kc:2 * kc + 2],
                             kc == 0, kc == 7, reads=['mw', 'sc'], writes=[pk])
                    C.ts('dve', modcol[:, t * 8 + oc, :], ps[:, 2 * oc:2 * oc + 2],
                         mbc[:, t * 8 + oc:t * 8 + oc + 1], float(t), ALU.add, ALU.add,
                         reads=[pk, 'mbc'], writes=['modcol'])
            else:
                for j in range(2):
                    for half in range(2):
                        ps, pk = C.psum[2 + 2 * j + half]
                        for kc in range(8):
                            C.mm(ps[:, :], rep[:, j, kc, :], mw[:, kc, half * 512:(half + 1) * 512], kc == 0, kc == 7,
                                 reads=['mw', 'rep'], writes=[pk])
                        C.tt('dve', gate_bc[:, j, half * 512:(half + 1) * 512], ps[:, :],
                             mbg[:, half * 512:(half + 1) * 512], ALU.add, reads=[pk, 'mbg'], writes=['gate_bc'])
        _drain(C)
    return modcol, gate_bc


def _drain(C):
    P = C.P
    toks = [(e, P.cnt[e]) for e in ['pe', 'act', 'dve', 'pool'] if P.cnt[e] > 0]
    dtoks = []
    for q in P.dq:
        d = P.dq[q]
        for idx in range(P.n_dma):
            if d['targets'][idx] > 0:
                dtoks.append((('d', q, idx), d['targets'][idx]))
    for E in ['pe', 'act', 'dve', 'pool', 'sp']:
        for t in toks + dtoks:
            P._wait(E, t)


def emit_hT(C, K, xs, xs_key, x_rows, ntok, j, modcol, hT, hT_key, ring, dst=None):
    P = C.P
    nt = ntok // 128
    for tt in range(nt):
        P.dma('sp', xs[:, tt, :], x_rows[tt * 128:(tt + 1) * 128, :], writes=[(xs_key, tt)])
    for kc in range(8):
        ps, pk = ring.next()
        for tt in range(nt):
            P.op('pe', 'transpose', ps[:, tt * 128:(tt + 1) * 128], xs[:, tt, kc * 128:(kc + 1) * 128], K['ident'][:],
                 reads=[(xs_key, tt), 'ident'], writes=[pk])
        C.act(hT[:, kc, 0:ntok] if dst is None else dst(kc), ps[:, 0:ntok], AF.Identity, reads=[pk, 'modcol'], writes=[hT_key],
              scale=modcol[:, 8 + kc, j:j + 1], bias=modcol[:, kc, j:j + 1])


def emit_epilogue_tile(C, E, mixT_tile_fn, wout, x_rows, out_rows, j, gate_bc, lnbc, ring, tagi):
    P = C.P
    xt, tb, st6, mv, sm = E['xt'][tagi % 2], E['tb'][tagi % 2], E['st6'][tagi % 2], E['mv'][tagi % 2], E['sm'][tagi % 2]
    kx, kt, ks = ('xt', tagi % 2), ('tb', tagi % 2), ('sm', tagi % 2)
    P.dma('sp', xt[:], x_rows, writes=[kx])
    for half in range(2):
        ps, pk = ring.next()
        for c in range(8):
            lhsT, rk = mixT_tile_fn(c)
            C.mm(ps[:, :], lhsT, wout[:, c, half * 512:(half + 1) * 512], c == 0, c == 7, reads=rk + ['wout'], writes=[pk])
        C.tt('dve', tb[:, half * 512:(half + 1) * 512], ps[:, :], gate_bc[:, j, half * 512:(half + 1) * 512], ALU.mult,
             reads=[pk, 'gate_bc'], writes=[kt])
    C.stt(tb[:], xt[:], float(ALPHA), tb[:], ALU.mult, ALU.add, reads=[kx, kt], writes=[kt])
    for half in range(2):
        P.op('dve', 'bn_stats', out=st6[:, half, :], in_=tb[:, half * 512:(half + 1) * 512], reads=[kt], writes=[ks])
    P.op('dve', 'bn_aggr', out=mv[:], in_=st6[:].rearrange("p a b -> p (a b)"), reads=[ks], writes=[ks])
    C.act(sm[:, 0:1], mv[:, 1:2], AF.Sqrt, reads=[ks], writes=[ks], bias=float(EPS), scale=1.0)
    C.recip(sm[:, 1:2], sm[:, 0:1], reads=[ks], writes=[ks])
    C.ts('dve', sm[:, 2:3], mv[:, 0:1], sm[:, 1:2], -1.0, ALU.mult, ALU.mult, reads=[ks], writes=[ks])
    C.act(tb[:], tb[:], AF.Identity, reads=[kt, ks], writes=[kt], scale=sm[:, 1:2], bias=sm[:, 2:3])
    C.tt('pool', tb[:], tb[:], lnbc[:, 0, :], ALU.mult, reads=[kt, 'lnbc'], writes=[kt])
    C.tt('pool', tb[:], tb[:], lnbc[:, 1, :], ALU.add, reads=[kt, 'lnbc'], writes=[kt])
    P.dma('sp', out_rows, tb[:], reads=[kt])


def alloc_epilogue(C, st):
    E = dict(xt=[], tb=[], st6=[], mv=[], sm=[])
    for i in range(2):
        E['xt'].append(C.sb(st, "e_xt%d" % i, [128, 1024], F32))
        E['tb'].append(C.sb(st, "e_tb%d" % i, [128, 1024], F32))
        E['st6'].append(C.sb(st, "e_st%d" % i, [128, 2, 6], F32))
        E['mv'].append(C.sb(st, "e_mv%d" % i, [128, 2], F32))
        E['sm'].append(C.sb(st, "e_sm%d" % i, [128, 4], F32))
    return E


def load_w_bf16(C, dst, dst_key, src, kcs, ncols):
    for kc in range(kcs):
        for c0 in range(0, ncols, 1024):
            c1 = min(ncols, c0 + 1024)
            C.P.dma('pool', dst[:, kc, c0:c1], src[kc * 128:(kc + 1) * 128, c0:c1], writes=[dst_key])


KV_BLOCKS = [(0, 256, 1)] + [(256 + 512 * i, 512, 0) for i in range(8)]
Q_PASSES = [[(0, 256, 1), (256, 512, 0), (768, 512, 0)], [(1280, 512, 0), (1792, 512, 0)]]
FULL_PASSES = [[(0, 256, 1), (256, 512, 0), (768, 512, 0)]] + [[(1280 + 1024 * i, 512, 0), (1792 + 1024 * i, 512, 0)] for i in range(3)]
NKT = TA // 128


def build_att():
    nc = bass.Bass("TRN2", target_bir_lowering=False)
    with ExitStack() as st:
        C = Ctx(nc, st)
        emit_att(C, st)
        C.P.finish()
    return nc


def emit_att(C, st, K=None, full=False):
    if True:
        P = C.P
        Q_PASSES_ = FULL_PASSES if full else Q_PASSES
        xall = C.dram_in("xall", [TA, D])
        xown = C.dram_in("xown", [NOWN, D])
        cc_d = C.dram_in("cc", [128, 16])
        modw_d = C.dram_in("modw", [D, 3 * D])
        modbcol_d = C.dram_in("modb_col", [128, 24])
        modbgate_d = C.dram_in("modb_gate", [1, D])
        lng_d = C.dram_in("lng", [1, D])
        lnb_d = C.dram_in("lnb", [1, D])
        wkv_d = C.dram_in("wkv", [D, 704])
        wq_d = C.dram_in("wq", [D, 2304])
        wuqA_d = C.dram_in("wuqA", [256, 768])
        wuqB_d = C.dram_in("wuqB", [256, 768])
        wuk_d = C.dram_in("wuk", [128, 512])
        wuv_d = C.dram_in("wuv", [128, 512])
        wout_d = C.dram_in("wout", [D, D])
        gcols_d = C.dram_in("gcols", [128, 8])
        tabKg_d = C.dram_in("tabKg", [2, 128, TA])
        tabKm_d = C.dram_in("tabKm", [2, 96, TA])
        tabQg_d = None if full else C.dram_in("tabQg", [2, 128, NOWN])
        tabQm_d = None if full else C.dram_in("tabQm", [2, 96, NOWN])
        ident_d = C.dram_in("ident", [128, 128])
        xo = C.dram_out("xo", [NOWN, D])
        if full:
            tabQg_d, tabQm_d = tabKg_d, tabKm_d

        K = K or emit_consts(C, st, ident_d)
        gcols = C.sb(st, "gcols", [128, 8], F32)
        P.dma('sp', gcols[:], gcols_d, writes=['gcols'])
        lnbc = C.sb(st, "lnbc", [128, 2, 1024], F32)
        P.dma('sp', lnbc[:, 0, :], lng_d[0:1, :].broadcast_to([128, D]), writes=['lnbc'])
        P.dma('sp', lnbc[:, 1, :], lnb_d[0:1, :].broadcast_to([128, D]), writes=['lnbc'])
        modcol, gate_bc = emit_modulation(C, st, K, cc_d, modw_d, modbcol_d, modbgate_d)

        ckvT = C.sb(st, "ckvT", [128, TA], BF16)
        krT = C.sb(st, "krT", [96, TA], BF16)
        kTg = C.sb(st, "kTg", [128, TA], BF16)
        Vg = C.sb(st, "Vg", [128, NKT, 192], BF16)
        P.op('pool', 'memset', Vg[:, :, 64:128], 1.0, writes=['Vg'])
        allps = Ring(C.psum)

        def rms_rope_chunk(s2, psA, pkA, psB, pkB, n, gA, gB, tab, tabk, dst, dst_key, tmp):
            sq, rt, t1, t2 = tmp
            C.act(sq[:, 0:n], psA[:, 0:n], AF.Square, reads=[pkA], writes=['sq'])
            ps2, pk2 = allps.next()
            C.mm(ps2[:, 0:n], K['onesbd'][:], sq[:, 0:n], True, True, reads=['onesbd', 'sq'], writes=[pk2])
            C.act(rt[:, 0:n], ps2[:, 0:n], AF.Sqrt, reads=[pk2], writes=['rt'], bias=float(EPS), scale=1.0 / 64)
            C.recip(rt[:, 0:n], rt[:, 0:n], reads=['rt'], writes=['rt'])
            C.stt(t1[:, 0:n], psA[:, 0:n], gA, tab[:, 0, 0:n], ALU.mult, ALU.mult, reads=[pkA, tabk, 'gcols'], writes=['t1'])
            C.stt(t2[:, 0:n], psB[:, 0:n], gB, tab[:, 1, 0:n], ALU.mult, ALU.mult, reads=[pkB, tabk, 'gcols'], writes=['t2'])
            C.tt('pool', t1[:, 0:n], t1[:, 0:n], t2[:, 0:n], ALU.add, reads=['t1', 't2'], writes=['t1'])
            C.tt('pool', dst, t1[:, 0:n], rt[:, 0:n], ALU.mult, reads=['t1', 'rt'], writes=[dst_key])

        with ExitStack() as s2:
            wkv = C.sb(s2, "wkv", [128, 8, 704], BF16)
            load_w_bf16(C, wkv, 'wkv', wkv_d, 8, 704)
            xs = C.sb(s2, "xs", [128, 4, 1024], F32)
            hT = [C.sb(s2, "hT%d" % i, [128, 8, 512], BF16) for i in range(2)]
            tabg = [C.sb(s2, "tabg%d" % i, [128, 2, 512], F32) for i in range(2)]
            tabm = [C.sb(s2, "tabm%d" % i, [96, 2, 512], F32) for i in range(2)]
            tmp = (C.sb(s2, "sq", [128, 512], BF16), C.sb(s2, "rt", [128, 512], F32),
                   C.sb(s2, "t1", [128, 512], F32), C.sb(s2, "t2", [128, 512], F32))
            for bi, (r0, n, j) in enumerate(KV_BLOCKS):
                h, hk = hT[bi % 2], ('hT', bi % 2)
                tg, tgk = tabg[bi % 2], ('tabg', bi % 2)
                tm, tmk = tabm[bi % 2], ('tabm', bi % 2)
                for a in range(2):
                    P.dma('sp', tg[:, a, 0:n], tabKg_d[a, :, r0:r0 + n], writes=[tgk])
                    P.dma('sp', tm[64:96, a, 0:n], tabKm_d[a, 64:96, r0:r0 + n], writes=[tmk])
                emit_hT(C, K, xs, 'xs', xall[r0:r0 + n, :], n, j, modcol, h, hk, allps)

                def proj(off, M):
                    ps, pk = allps.next()
                    for kc in range(8):
                        C.mm(ps[0:M, 0:n], wkv[:, kc, off:off + M], h[:, kc, 0:n], kc == 0, kc == 7,
                             reads=['wkv', hk], writes=[pk])
                    return ps, pk
                ps, pk = proj(0, 128)
                sq, rt, t1, t2 = tmp
                C.act(sq[:, 0:n], ps[:, 0:n], AF.Square, reads=[pk], writes=['sq'])
                ps2, pk2 = allps.next()
                C.mm(ps2[:, 0:n], K['ones_bf'][:], sq[:, 0:n], True, True, reads=['ones_bf', 'sq'], writes=[pk2])
                C.act(rt[:, 0:n], ps2[:, 0:n], AF.Sqrt, reads=[pk2], writes=['rt'], bias=float(EPS), scale=1.0 / 128)
                C.recip(rt[:, 0:n], rt[:, 0:n], reads=['rt'], writes=['rt'])
                C.stt(ckvT[:, r0:r0 + n], ps[:, 0:n], gcols[:, 2:3], rt[:, 0:n], ALU.mult, ALU.mult,
                      reads=[pk, 'rt', 'gcols'], writes=['ckvT'])
                psA, pkA = proj(128, 128)
                psB, pkB = proj(256, 128)
                rms_rope_chunk(s2, psA, pkA, psB, pkB, n, gcols[:, 5:6], gcols[:, 6:7], tg, tgk, kTg[:, r0:r0 + n], 'kTg', tmp)
                psA, pkA = proj(384, 96)
                psB, pkB = proj(480, 96)
                C.tt('dve', t1[64:96, 0:n], psA[64:96, 0:n], tm[64:96, 0, 0:n], ALU.mult, reads=[pkA, tmk], writes=['t1'])
                C.tt('dve', t2[64:96, 0:n], psB[64:96, 0:n], tm[64:96, 1, 0:n], ALU.mult, reads=[pkB, tmk], writes=['t2'])
                C.tt('pool', krT[64:96, r0:r0 + n], t1[64:96, 0:n], t2[64:96, 0:n], ALU.add, reads=['t1', 't2'], writes=['krT'])
                nt = n // 128
                ps, pk = allps.next()
                for tt in range(nt):
                    for kc in range(8):
                        C.mm(ps[:, tt * 128:(tt + 1) * 128], h[:, kc, tt * 128:(tt + 1) * 128], wkv[:, kc, 576:704],
                             kc == 0, kc == 7, reads=['wkv', hk], writes=[pk])
                t0 = r0 // 128
                for tt in range(nt):
                    C.copy('dve', Vg[:, t0 + tt, 0:64], ps[:, tt * 128:tt * 128 + 64], reads=[pk], writes=['Vg'])
                    C.copy('dve', Vg[:, t0 + tt, 128:192], ps[:, tt * 128 + 64:tt * 128 + 128], reads=[pk], writes=['Vg'])
            _drain(C)

        for pi, blocks in enumerate(Q_PASSES_):
            base = blocks[0][0]
            npass = sum(b[1] for b in blocks)
            with ExitStack() as sp:
                cqT = C.sb(sp, "cqT", [128, 2, 1280], BF16)
                qTg = C.sb(sp, "qTg", [128, 4, 1280], BF16)
                mixT = C.sb(sp, "mixT", [128, 8, 1280], BF16)
                with ExitStack() as s2:
                    wq = C.sb(s2, "wq", [128, 8, 2304], BF16)
                    load_w_bf16(C, wq, 'wq', wq_d, 8, 2304)
                    xs = C.sb(s2, "xs", [128, 4, 1024], F32)
                    hT = [C.sb(s2, "hT%d" % i, [128, 8, 512], BF16) for i in range(2)]
                    tabg = [C.sb(s2, "tabg%d" % i, [128, 2, 512], F32) for i in range(2)]
                    tmp = (C.sb(s2, "sq", [128, 512], BF16), C.sb(s2, "rt", [128, 512], F32),
                           C.sb(s2, "t1", [128, 512], F32), C.sb(s2, "t2", [128, 512], F32))
                    sq2 = C.sb(s2, "sq2", [128, 512], BF16)
                    for bi, (r0, n, j) in enumerate(blocks):
                        h, hk = hT[bi % 2], ('hT', bi % 2)
                        tg, tgk = tabg[bi % 2], ('tabg', bi % 2)
                        l0 = r0 - base
                        for a in range(2):
                            P.dma('sp', tg[:, a, 0:n], tabQg_d[a, :, r0:r0 + n], writes=[tgk])
                        emit_hT(C, K, xs, 'xs', xown[r0:r0 + n, :], n, j, modcol, h, hk, allps)

                        def proj(ci):
                            ps, pk = allps.next()
                            for kc in range(8):
                                C.mm(ps[:, 0:n], wq[:, kc, ci * 128:(ci + 1) * 128], h[:, kc, 0:n], kc == 0, kc == 7,
                                     reads=['wq', hk], writes=[pk])
                            return ps, pk
                        sq, rt, t1, t2 = tmp
                        ps0, pk0 = proj(0)
                        ps1, pk1 = proj(1)
                        C.act(sq[:, 0:n], ps0[:, 0:n], AF.Square, reads=[pk0], writes=['sq'])
                        C.act(sq2[:, 0:n], ps1[:, 0:n], AF.Square, reads=[pk1], writes=['sq2'])
                        ps2, pk2 = allps.next()
                        C.mm(ps2[:, 0:n], K['ones_bf'][:], sq[:, 0:n], True, False, reads=['ones_bf', 'sq'], writes=[pk2])
                        C.mm(ps2[:, 0:n], K['ones_bf'][:], sq2[:, 0:n], False, True, reads=['ones_bf', 'sq2'], writes=[pk2])
                        C.act(rt[:, 0:n], ps2[:, 0:n], AF.Sqrt, reads=[pk2], writes=['rt'], bias=float(EPS), scale=1.0 / 256)
                        C.recip(rt[:, 0:n], rt[:, 0:n], reads=['rt'], writes=['rt'])
                        C.stt(cqT[:, 0, l0:l0 + n], ps0[:, 0:n], gcols[:, 0:1], rt[:, 0:n], ALU.mult, ALU.mult,
                              reads=[pk0, 'rt', 'gcols'], writes=['cqT'])
                        C.stt(cqT[:, 1, l0:l0 + n], ps1[:, 0:n], gcols[:, 1:2], rt[:, 0:n], ALU.mult, ALU.mult,
                              reads=[pk1, 'rt', 'gcols'], writes=['cqT'])
                        for c in range(8):
                            ps, pk = proj(2 + c)
                            C.act(mixT[:, c, l0:l0 + n], ps[:, 0:n], AF.Silu, reads=[pk], writes=[('mixT', c)])
                        for c in range(4):
                            psA, pkA = proj(10 + c)
                            psB, pkB = proj(14 + c)
                            rms_rope_chunk(s2, psA, pkA, psB, pkB, n, gcols[:, 3:4], gcols[:, 4:5], tg, tgk,
                                           qTg[:, c, l0:l0 + n], ('qTg', c), tmp)
                    _drain(C)

                with ExitStack() as s2:
                    wuqA = C.sb(s2, "wuqA", [128, 2, 768], BF16)
                    wuqB = C.sb(s2, "wuqB", [128, 2, 768], BF16)
                    wuk = C.sb(s2, "wuk", [128, 1, 512], BF16)
                    wuv = C.sb(s2, "wuv", [128, 1, 512], BF16)
                    load_w_bf16(C, wuqA, 'wuqA', wuqA_d, 2, 768)
                    load_w_bf16(C, wuqB, 'wuqB', wuqB_d, 2, 768)
                    load_w_bf16(C, wuk, 'wuk', wuk_d, 1, 512)
                    load_w_bf16(C, wuv, 'wuv', wuv_d, 1, 512)
                    Kh = [C.sb(s2, "Kh%d" % i, [96, TA], BF16) for i in range(2)]
                    for i in range(2):
                        C.copy('pool', Kh[i][64:96, :], krT[64:96, :], reads=['krT'], writes=[('Kh', i)])
                    qh = [C.sb(s2, "qh%d" % i, [96, 1280], BF16) for i in range(2)]
                    Vp = [C.sb(s2, "Vp%d" % i, [128, NKT, 192], BF16) for i in range(2)]
                    for i in range(2):
                        P.op('pool', 'memset', Vp[i][:, :, 64:128], 1.0, writes=[('Vp', i)])
                    tabq = C.sb(s2, "tabq", [96, 2, 1280], F32)
                    for a in range(2):
                        P.dma('sp', tabq[64:96, a, 0:npass], tabQm_d[a, 64:96, base:base + npass], writes=['tabq'])
                    PT = [C.sb(s2, "PT%d" % i, [128, 512], BF16) for i in range(4)]
                    rden = [C.sb(s2, "rden%d" % i, [128, 512], F32) for i in range(2)]
                    on = [C.sb(s2, "on%d" % i, [128, 512], F32) for i in range(2)]
                    t1 = C.sb(s2, "a_t1", [96, 512], F32)
                    t2 = C.sb(s2, "a_t2", [96, 512], F32)
                    Sring = Ring(C.psum[0:3])
                    Oring = Ring(C.psum[3:5])
                    Mring = Ring(C.psum[6:8])
                    cnt = {'pt': 0, 'nrm': 0}

                    def attention(Kap, Kkeys, qbuf, qkeys, kdim, prow, scale, Vfn, Vkeys, nrow, drow, cm):
                        def one(blk, gi):
                            (r0, n, j) = blk
                            l0 = r0 - base
                            nk = 2 if j == 1 else NKT
                            Sb = [C.psum[2 * gi], C.psum[2 * gi + 1]]
                            ops_, opk = C.psum[4 + gi]
                            C.mm(Sb[0][0][:, 0:n], Kap(0), qbuf(l0, n), True, True, reads=Kkeys + qkeys, writes=[Sb[0][1]])
                            for kt in range(nk + 1):
                                if kt + 1 < nk:
                                    sb_ = Sb[(kt + 1) % 2]
                                    C.mm(sb_[0][:, 0:n], Kap(kt + 1), qbuf(l0, n), True, True, reads=Kkeys + qkeys, writes=[sb_[1]])
                                if kt < nk:
                                    pi_ = 2 * gi + kt % 2
                                    C.act(PT[pi_][:, 0:n], Sb[kt % 2][0][:, 0:n], AF.Exp, reads=[Sb[kt % 2][1]],
                                          writes=[('PT', pi_)], scale=float(scale))
                                if kt >= 1:
                                    pj = 2 * gi + (kt - 1) % 2
                                    C.mm(ops_[:, 0:n], Vfn(kt - 1), PT[pj][:, 0:n], kt == 1, kt == nk,
                                         reads=Vkeys + [('PT', pj)], writes=[opk])
                                yield
                            ni = gi
                            C.recip(rden[ni][nrow, 0:n], ops_[drow, 0:n], reads=[opk], writes=[('rden', ni)])
                            C.tt('dve', on[ni][nrow, 0:n], ops_[nrow, 0:n], rden[ni][nrow, 0:n], ALU.mult,
                                 reads=[opk, ('rden', ni)], writes=[('on', ni)])
                            C.tt('pool', mixT[nrow, cm, l0:l0 + n], on[ni][nrow, 0:n], mixT[nrow, cm, l0:l0 + n], ALU.mult,
                                 reads=[('on', ni), ('mixT', cm)], writes=[('mixT', cm)])
                        lat = [b for b in blocks if b[2] == 0]
                        for b in blocks:
                            if b[2] == 1:
                                interleave([one(b, 0)])
                        for i_ in range(0, len(lat), 2):
                            interleave([one(b, gi) for gi, b in enumerate(lat[i_:i_ + 2])])

                    hcount = 0
                    for jp in range(4):
                        vp, vk = Vp[jp % 2], ('Vp', jp % 2)
                        for t0 in range(0, NKT, 4):
                            nt = min(4, NKT - t0)
                            ps, pk = Mring.next()
                            for tt in range(nt):
                                C.mm(ps[:, tt * 128:(tt + 1) * 128], ckvT[:, (t0 + tt) * 128:(t0 + tt + 1) * 128],
                                     wuv[:, 0, jp * 128:(jp + 1) * 128], True, True, reads=['ckvT', 'wuv'], writes=[pk])
                            for tt in range(nt):
                                C.copy('dve', vp[:, t0 + tt, 0:64], ps[:, tt * 128:tt * 128 + 64], reads=[pk], writes=[vk])
                                C.copy('dve', vp[:, t0 + tt, 128:192], ps[:, tt * 128 + 64:tt * 128 + 128], reads=[pk], writes=[vk])
                        for hh in range(2):
                            hd = 2 * jp + hh
                            kh, kk = Kh[hcount % 2], ('Kh', hcount % 2)
                            q_, qk = qh[hcount % 2], ('qh', hcount % 2)
                            hcount += 1
                            for (r0, n, j) in KV_BLOCKS:
                                ps, pk = Mring.next()
                                C.mm(ps[0:64, 0:n], wuk[:, 0, hd * 64:(hd + 1) * 64], ckvT[:, r0:r0 + n], True, True,
                                     reads=['wuk', 'ckvT'], writes=[pk])
                                C.copy('dve', kh[0:64, r0:r0 + n], ps[0:64, 0:n], reads=[pk], writes=[kk])
                            for (r0, n, j) in blocks:
                                l0 = r0 - base
                                psA, pkA = Mring.next()
                                psB, pkB = Mring.next()
                                for i in range(2):
                                    C.mm(psA[0:96, 0:n], wuqA[:, i, hd * 96:(hd + 1) * 96], cqT[:, i, l0:l0 + n], i == 0, i == 1,
                                         reads=['wuqA', 'cqT'], writes=[pkA])
                                for i in range(2):
                                    C.mm(psB[0:96, 0:n], wuqB[:, i, hd * 96:(hd + 1) * 96], cqT[:, i, l0:l0 + n], i == 0, i == 1,
                                         reads=['wuqB', 'cqT'], writes=[pkB])
                                C.copy('dve', q_[0:64, l0:l0 + n], psA[0:64, 0:n], reads=[pkA], writes=[qk])
                                C.tt('dve', t1[64:96, 0:n], psA[64:96, 0:n], tabq[64:96, 0, l0:l0 + n], ALU.mult,
                                     reads=[pkA, 'tabq'], writes=['a_t1'])
                                C.tt('dve', t2[64:96, 0:n], psB[64:96, 0:n], tabq[64:96, 1, l0:l0 + n], ALU.mult,
                                     reads=[pkB, 'tabq'], writes=['a_t2'])
                                C.tt('pool', q_[64:96, l0:l0 + n], t1[64:96, 0:n], t2[64:96, 0:n], ALU.add,
                                     reads=['a_t1', 'a_t2'], writes=[qk])
                            nrow = slice(0, 64) if hh == 0 else slice(64, 128)
                            drow = slice(64, 128) if hh == 0 else slice(0, 64)
                            attention(lambda kt, kh=kh: kh[0:96, kt * 128:(kt + 1) * 128], [kk],
                                      lambda l0, n, q_=q_: q_[0:96, l0:l0 + n], [qk], 96, None, 96.0 ** -0.5,
                                      lambda kt, vp=vp, hh=hh: vp[:, kt, hh * 64:hh * 64 + 128], [vk], nrow, drow, jp)
                    for c in range(4):
                        for g in range(2):
                            rows = slice(64 * g, 64 * g + 64)
                            drow = slice(64, 128) if g == 0 else slice(0, 64)
                            attention(lambda kt, rows=rows: kTg[rows, kt * 128:(kt + 1) * 128], ['kTg'],
                                      lambda l0, n, rows=rows, c=c: qTg[rows, c, l0:l0 + n], [('qTg', c)], 64, None, 0.125,
                                      lambda kt, g=g: Vg[:, kt, g * 64:g * 64 + 128], ['Vg'], rows, drow, 4 + c)
                    _drain(C)

                with ExitStack() as s2:
                    wout = C.sb(s2, "wout", [128, 8, 1024], BF16)
                    load_w_bf16(C, wout, 'wout', wout_d, 8, 1024)
                    E = alloc_epilogue(C, s2)
                    ti = 0
                    for (r0, n, j) in blocks:
                        for tt in range(n // 128):
                            l0 = r0 - base + tt * 128
                            rr = r0 + tt * 128
                            emit_epilogue_tile(
                                C, E, lambda c, l0=l0: (mixT[:, c, l0:l0 + 128], [('mixT', c)]), wout,
                                xown[rr:rr + 128, :], xo[rr:rr + 128, :], j, gate_bc, lnbc, allps, ti)
                            ti += 1
                    _drain(C)


_CACHE = {}


def _get(name, fn):
    if name not in _CACHE:
        _CACHE[name] = fn()
    return _CACHE[name]


def run_att_layer(inp, l, x, ctx):
    sh, (tabKg, tabKm) = prep_att_layer(inp, l)
    nc = _get('att', build_att)
    in_maps = []
    for core in range(8):
        b, hf = core // 2, core % 2
        xall = np.concatenate([ctx[b], x[b]], axis=0)
        xown = np.concatenate([ctx[b], x[b, hf * OWN:(hf + 1) * OWN]], axis=0)
        sel = np.concatenate([np.arange(TC), TC + hf * OWN + np.arange(OWN)])
        m = dict(sh)
        m.update(xall=np.ascontiguousarray(xall), xown=np.ascontiguousarray(xown),
                 cc=_cc_cols(inp['c'][b], inp['c_ctx']), tabKg=tabKg, tabKm=tabKm,
                 tabQg=np.ascontiguousarray(tabKg[:, :, sel]), tabQm=np.ascontiguousarray(tabKm[:, :, sel]))
        in_maps.append(m)
    res = run_bass_kernel_spmd(nc, in_maps, core_ids=list(range(8)))
    xn = np.empty_like(x)
    cn = np.empty_like(ctx)
    for core in range(8):
        b, hf = core // 2, core % 2
        o = res.results[core]["xo"]
        xn[b, hf * OWN:(hf + 1) * OWN] = o[TC:]
        if hf == 0:
            cn[b] = o[:TC]
    return xn, cn


def _rec_consts():
    idx = np.arange(128)
    same = (idx[:, None] // 64) == (idx[None, :] // 64)
    cm = np.zeros((10, 128, 128), np.float32)
    cm[0] = same & (idx[:, None] <= idx[None, :])
    cm[1] = same & (idx[:, None] >= idx[None, :])
    cm[2] = (idx[:, None] < 64) * np.ones((1, 128))
    cm[3] = (idx[:, None] >= 64) * np.ones((1, 128))
    BIG = 30000.0
    cm[4] = np.where(same & (idx[None, :] < idx[:, None]), 0.0, BIG)
    cm[5] = np.where(same & (idx[None, :] > idx[:, None]), 0.0, BIG)
    cm[6] = np.where(same & (idx[:, None] <= idx[None, :]), 0.0, -BIG)
    cm[7] = np.where(same & (idx[:, None] >= idx[None, :]), 0.0, -BIG)
    cm[8] = same
    return cm


def prep_rec_layer(inp, l, hf):
    li = l // 2
    w_in = np.asarray(inp['rec_w_in'][li], np.float32)
    hs = [2 * hf, 2 * hf + 1]
    cols = []
    for base in (0, 512, 1024, 1536):
        for h in hs:
            cols.append(np.arange(base + h * 128, base + (h + 1) * 128))
    for base in (2064, 2576):
        for i in range(2):
            cols.append(np.arange(base + 256 * hf + 128 * i, base + 256 * hf + 128 * (i + 1)))
    wmain = w_in[:, np.concatenate(cols)]
    bcols = [2048 + d * 4 + h for d in range(2) for h in hs]
    acols = [2056 + d * 4 + h for d in range(2) for h in hs]
    wba = w_in[:, bcols + acols]
    gcw = np.asarray(inp['gdn_conv_w'][li], np.float32)
    lcw = np.asarray(inp['lru_conv_w'][li], np.float32)
    lcb = np.asarray(inp['lru_conv_b'][li], np.float32)
    convw = np.zeros((128, 8, 4), np.float32)
    for ci in range(6):
        convw[:, ci, :] = gcw[:, cols[ci]].T
    convb = np.zeros((128, 2), np.float32)
    for i in range(2):
        ch = 256 * hf + 128 * i + np.arange(128)
        convw[:, 6 + i, :] = lcw[:, ch].T
        convb[:, i] = lcb[ch]
    dtb = np.asarray(inp['gdn_dt_bias'][li], np.float32)
    alog = np.asarray(inp['gdn_a_log'][li], np.float32)
    dtb_row = np.tile(np.array([dtb[d, h] for d in range(2) for h in hs], np.float32), NKT)[None, :]
    alog_row = np.tile(np.array([alog[d, h] for d in range(2) for h in hs], np.float32), NKT)[None, :]
    gnorm = np.asarray(inp['gdn_norm'][li], np.float32)[:, None]
    gw = np.asarray(inp['lru_gate_w'][li], np.float32)
    gb = np.asarray(inp['lru_gate_b'][li], np.float32)
    lam = np.asarray(inp['lru_lambda'][li], np.float32)
    wgate = np.zeros((128, 8, 128), np.float32)
    gbcol = np.zeros((128, 8), np.float32)
    lamcol = np.zeros((128, 4), np.float32)
    for d in range(2):
        for i in range(2):
            ch = 256 * hf + 128 * i + np.arange(128)
            lamcol[:, d * 2 + i] = lam[d, ch]
            for g in range(2):
                e = (d * 2 + g) * 2 + i
                n0 = 4 * hf + 2 * i
                wgate[0:64, e, 0:64] = gw[d, g, n0]
                wgate[64:128, e, 64:128] = gw[d, g, n0 + 1]
                gbcol[:, e] = gb[d, g, ch]
    return dict(wmain=np.ascontiguousarray(wmain), wba=np.ascontiguousarray(wba), convw=convw, convb=convb,
                dtb=dtb_row, alog=alog_row, gnorm=gnorm, wgate=wgate, gbcol=gbcol, lamcol=lamcol, cmats=_rec_consts())


_STOP = {}


def _poff(r0):
    return r0 + 1 if r0 == 0 else r0 + 4


def _csrc(r0):
    return r0 if r0 == 0 else r0 + 3


def build_recA():
    nc = bass.Bass("TRN2", target_bir_lowering=False)
    with ExitStack() as st:
        C = Ctx(nc, st)
        emit_recA(C, st)
        C.P.finish()
    return nc


def emit_recA(C, st, K=None, mix_dst=None):
    if True:
        P = C.P
        xall = C.dram_in("xall", [TA, D])
        cc_d = C.dram_in("cc", [128, 16])
        modw_d = C.dram_in("modw", [D, 3 * D])
        modbcol_d = C.dram_in("modb_col", [128, 24])
        modbgate_d = C.dram_in("modb_gate", [1, D])
        ident_d = C.dram_in("ident", [128, 128])
        wmain_d = C.dram_in("wmain", [D, 1536])
        wba_d = C.dram_in("wba", [D, 8])
        convw_d = C.dram_in("convw", [128, 8, 4])
        convb_d = C.dram_in("convb", [128, 2])
        dtb_d = C.dram_in("dtb", [1, NKT * 4])
        alog_d = C.dram_in("alog", [1, NKT * 4])
        gnorm_d = C.dram_in("gnorm", [128, 1])
        wgate_d = C.dram_in("wgate", [128, 8, 128])
        gbcol_d = C.dram_in("gbcol", [128, 8])
        lamcol_d = C.dram_in("lamcol", [128, 4])
        cm_d = C.dram_in("cmats", [10, 128, 128])
        if mix_dst is None:
            mixo = C.dram_out("mixo", [4, 128, TA], BF16)
            mix_dst = lambda idx, c0, n: mixo[idx, :, c0:c0 + n]

        K = K or emit_consts(C, st, ident_d)
        modcol, _ = emit_modulation(C, st, K, cc_d, modw_d, modbcol_d, modbgate_d, need_gate=False)
        cms = C.sb(st, "cms", [128, 10, 128], F32)
        for i in range(9):
            P.dma('sp', cms[:, i, :], cm_d[i], writes=['cms'])
        convw = C.sb(st, "convw", [128, 8, 4], F32)
        convb = C.sb(st, "convb", [128, 2], F32)
        gnorm = C.sb(st, "gnorm", [128, 1], F32)
        gbcol = C.sb(st, "gbcol", [128, 8], F32)
        lamc = C.sb(st, "lamc", [128, 4], F32)
        c8 = C.sb(st, "c8", [128, 4], F32)
        P.dma('sp', convw[:], convw_d, writes=['cw'])
        P.dma('sp', convb[:], convb_d, writes=['cw'])
        P.dma('sp', gnorm[:], gnorm_d, writes=['cw'])
        P.dma('sp', gbcol[:], gbcol_d, writes=['cw'])
        P.dma('sp', lamc[:], lamcol_d, writes=['lamc'])
        wgate = C.sb(st, "wgate", [128, 8, 128], BF16)
        P.dma('pool', wgate[:], wgate_d, writes=['wgate'])
        C.act(c8[:], lamc[:], AF.Exp, reads=['lamc'], writes=['c8'], scale=-1.0)
        C.act(c8[:], c8[:], AF.Ln, reads=['c8'], writes=['c8'], bias=1.0, scale=1.0)
        C.ts('dve', c8[:], c8[:], -8.0, None, ALU.mult, ALU.bypass, reads=['c8'], writes=['c8'])
        beta = C.sb(st, "beta", [128, NKT, 4], F32)
        nbeta = C.sb(st, "nbeta", [128, NKT, 4], F32)
        gall = C.sb(st, "gall", [128, NKT, 8], F32)
        P.op('pool', 'memset', gall[:], 0.0, writes=['gall'])
        allps = Ring(C.psum)

        def build_hT(s2):
            hT_all = C.sb(s2, "hT_all", [128, 8, TA], BF16)
            with ExitStack() as s3:
                xs = C.sb(s3, "xs", [128, 4, 1024], F32)
                for (r0, n, j) in KV_BLOCKS:
                    emit_hT(C, K, xs, 'xs', xall[r0:r0 + n, :], n, j, modcol, None, ('hT', r0), allps,
                            dst=lambda kc, r0=r0, n=n: hT_all[:, kc, r0:r0 + n])
                _drain(C)
            return hT_all

        def conv_block(pre, cvb, ci, r0, n, bias=None):
            sb_ = _csrc(r0)
            C.ts('dve', cvb[:, 0:n], pre[:, sb_:sb_ + n], convw[:, ci, 0:1], None, ALU.mult, ALU.bypass,
                 reads=['pre', 'cw'], writes=['cvb'])
            for k in range(1, 4):
                C.stt(cvb[:, 0:n], pre[:, sb_ + k:sb_ + k + n], convw[:, ci, k:k + 1], cvb[:, 0:n], ALU.mult, ALU.add,
                      reads=['pre', 'cw', 'cvb'], writes=['cvb'])
            if bias is not None:
                C.ts('dve', cvb[:, 0:n], cvb[:, 0:n], bias, None, ALU.add, ALU.bypass, reads=['cvb', 'cw'], writes=['cvb'])

        def fill_pre(pre, wch, hT_all, ci):
            for (r0, n, j) in KV_BLOCKS:
                ps, pk = allps.next()
                for kc in range(8):
                    C.mm(ps[:, 0:n], wch[:, kc, :], hT_all[:, kc, r0:r0 + n], kc == 0, kc == 7,
                         reads=[('wch', ci % 2), ('hT', r0)], writes=[pk])
                C.copy('dve', pre[:, _poff(r0):_poff(r0) + n], ps[:, 0:n], reads=[pk], writes=['pre'])

        def load_chunk_w(wchs, ci):
            w = wchs[ci % 2]
            for kc in range(8):
                P.dma('pool', w[:, kc, :], wmain_d[kc * 128:(kc + 1) * 128, ci * 128:(ci + 1) * 128], writes=[('wch', ci % 2)])
            return w

        with ExitStack() as s2:
            hT_all = build_hT(s2)
            wchs = [C.sb(s2, "wch%d" % i, [128, 8, 128], BF16) for i in range(2)]
            wba = C.sb(s2, "wba", [128, 8, 8], BF16)
            load_w_bf16(C, wba, 'wba', wba_d, 8, 8)
            ba = C.sb(s2, "ba", [128, NKT, 8], F32)
            dtb = C.sb(s2, "dtb", [128, NKT, 4], F32)
            nea = C.sb(s2, "nea", [128, NKT, 4], F32)
            t4a = C.sb(s2, "t4a", [128, NKT, 4], F32)
            t4b = C.sb(s2, "t4b", [128, NKT, 4], F32)
            P.dma('sp', dtb[:].rearrange("p a b -> p (a b)"), dtb_d[0:1, :].broadcast_to([128, NKT * 4]), writes=['dtb'])
            P.dma('sp', nea[:].rearrange("p a b -> p (a b)"), alog_d[0:1, :].broadcast_to([128, NKT * 4]), writes=['nea'])
            ps, pk = allps.next()
            for t in range(NKT):
                for kc in range(8):
                    C.mm(ps[:, 8 * t:8 * t + 8], hT_all[:, kc, t * 128:(t + 1) * 128], wba[:, kc, :], kc == 0, kc == 7,
                         reads=['wba'] + [('hT', b[0]) for b in KV_BLOCKS], writes=[pk])
            C.copy('dve', ba[:].rearrange("p a b -> p (a b)"), ps[:, 0:NKT * 8], reads=[pk], writes=['ba'])
            C.act(beta[:], ba[:, :, 0:4], AF.Sigmoid, reads=['ba'], writes=['beta'])
            C.ts('dve', nbeta[:], beta[:], -1.0, None, ALU.mult, ALU.bypass, reads=['beta'], writes=['nbeta'])
            C.tt('dve', t4a[:], ba[:, :, 4:8], dtb[:], ALU.add, reads=['ba', 'dtb'], writes=['t4a'])
            C.act(t4b[:], t4a[:], AF.Abs, reads=['t4a'], writes=['t4b'])
            C.act(t4b[:], t4b[:], AF.Exp, reads=['t4b'], writes=['t4b'], scale=-1.0)
            C.act(t4b[:], t4b[:], AF.Ln, reads=['t4b'], writes=['t4b'], bias=1.0, scale=1.0)
            C.ts('dve', t4a[:], t4a[:], 0.0, None, ALU.max, ALU.bypass, reads=['t4a'], writes=['t4a'])
            C.tt('dve', t4a[:], t4a[:], t4b[:], ALU.add, reads=['t4a', 't4b'], writes=['t4a'])
            C.act(nea[:], nea[:], AF.Exp, reads=['nea'], writes=['nea'])
            C.ts('dve', nea[:], nea[:], -1.0, None, ALU.mult, ALU.bypass, reads=['nea'], writes=['nea'])
            C.tt('dve', gall[:, :, 0:4], t4a[:], nea[:], ALU.mult, reads=['t4a', 'nea', 'gall'], writes=['gall'])
            pre = C.sb(s2, "pre", [128, TA + 6], F32)
            P.op('pool', 'memset', pre[:], 0.0, writes=['pre'])
            grs = C.sb(s2, "grs", [128, TA], BF16)
            hfw = C.sb(s2, "hfw", [128, TA], F32)
            hbw = [C.sb(s2, "hbw%d" % i, [128, 512], F32) for i in range(2)]
            hctx = C.sb(s2, "hctx", [128, 1], F32)
            L = {}
            for nm in ['xlb', 'r', 'ii', 'a', 'a2', 'bb']:
                L[nm] = C.sb(s2, "l_" + nm, [128, 512], F32)
            xlbf = C.sb(s2, "l_xlbf", [128, 512], BF16)
            ymix = [C.sb(s2, "ymix%d" % i, [128, 512], BF16) for i in range(2)]
            for i in range(2):
                wx = load_chunk_w(wchs, 8 + i)
                fill_pre(pre, wx, hT_all, 8 + i)
                wg = load_chunk_w(wchs, 10 + i)
                for (r0, n, j) in KV_BLOCKS:
                    ps, pk = allps.next()
                    for kc in range(8):
                        C.mm(ps[:, 0:n], wg[:, kc, :], hT_all[:, kc, r0:r0 + n], kc == 0, kc == 7,
                             reads=[('wch', (10 + i) % 2), ('hT', r0)], writes=[pk])
                    C.act(grs[:, r0:r0 + n], ps[:, 0:n], AF.Silu, reads=[pk], writes=['grs'])
                for d in range(2):
                    order = KV_BLOCKS if d == 0 else [KV_BLOCKS[0]] + KV_BLOCKS[:0:-1]
                    for bi, (r0, n, j) in enumerate(order):
                        conv_block(pre, L['xlb'], 6 + i, r0, n, bias=convb[:, i:i + 1])
                        C.copy('pool', xlbf[:, 0:n], L['xlb'][:, 0:n], reads=['cvb'], writes=['xlbf'])
                        er, ei = (d * 2 + 0) * 2 + i, (d * 2 + 1) * 2 + i
                        psr, pkr = allps.next()
                        C.mm(psr[:, 0:n], wgate[:, er, :], xlbf[:, 0:n], True, True, reads=['wgate', 'xlbf'], writes=[pkr])
                        psi, pki = allps.next()
                        C.mm(psi[:, 0:n], wgate[:, ei, :], xlbf[:, 0:n], True, True, reads=['wgate', 'xlbf'], writes=[pki])
                        C.act(L['r'][:, 0:n], psr[:, 0:n], AF.Sigmoid, reads=[pkr, 'cw'], writes=['l_r'], bias=gbcol[:, er:er + 1], scale=1.0)
                        C.act(L['ii'][:, 0:n], psi[:, 0:n], AF.Sigmoid, reads=[pki, 'cw'], writes=['l_ii'], bias=gbcol[:, ei:ei + 1], scale=1.0)
                        C.ts('dve', L['r'][:, 0:n], L['r'][:, 0:n], c8[:, d * 2 + i:d * 2 + i + 1], None, ALU.mult, ALU.bypass,
                             reads=['l_r', 'c8'], writes=['l_r'])
                        C.act(L['a'][:, 0:n], L['r'][:, 0:n], AF.Exp, reads=['l_r'], writes=['l_a'])
                        C.act(L['a2'][:, 0:n], L['r'][:, 0:n], AF.Exp, reads=['l_r'], writes=['l_a2'], scale=2.0)
                        C.ts('dve', L['a2'][:, 0:n], L['a2'][:, 0:n], -1.0, 1.0, ALU.mult, ALU.add, reads=['l_a2'], writes=['l_a2'])
                        C.act(L['a2'][:, 0:n], L['a2'][:, 0:n], AF.Sqrt, reads=['l_a2'], writes=['l_a2'])
                        C.tt('pool', L['ii'][:, 0:n], L['ii'][:, 0:n], L['xlb'][:, 0:n], ALU.mult, reads=['l_ii', 'cvb'], writes=['l_ii'])
                        C.tt('pool', L['bb'][:, 0:n], L['a2'][:, 0:n], L['ii'][:, 0:n], ALU.mult, reads=['l_a2', 'l_ii'], writes=['l_bb'])
                        if d == 0:
                            init = 0.0 if bi == 0 else hfw[:, r0 - 1:r0]
                            P.op('dve', 'tensor_tensor_scan', out=hfw[:, r0:r0 + n], data0=L['a'][:, 0:n], data1=L['bb'][:, 0:n],
                                 initial=init, op0=ALU.mult, op1=ALU.add, reads=['l_a', 'l_bb', 'hfw'], writes=['hfw'])
                        else:
                            hb, hk = hbw[bi % 2], ('hbw', bi % 2)
                            if bi == 0:
                                init = 0.0
                            elif bi == 1:
                                init = hctx[:, 0:1]
                            else:
                                init = hbw[(bi - 1) % 2][:, 0:1]
                            P.op('dve', 'tensor_tensor_scan', out=hb[:, 0:n][:, ::-1], data0=L['a'][:, 0:n][:, ::-1],
                                 data1=L['bb'][:, 0:n][:, ::-1], initial=init, op0=ALU.mult, op1=ALU.add,
                                 reads=['l_a', 'l_bb', ('hbw', (bi - 1) % 2), 'hctx'], writes=[hk])
                            if bi == 0:
                                C.copy('dve', hctx[:, 0:1], hb[:, 0:1], reads=[hk], writes=['hctx'])
                            ym, yk = ymix[bi % 2], ('ymix', bi % 2)
                            C.tt('dve', L['xlb'][:, 0:n], hfw[:, r0:r0 + n], hb[:, 0:n], ALU.add, reads=['hfw', hk, 'cvb'], writes=['cvb'])
                            C.tt('pool', ym[:, 0:n], L['xlb'][:, 0:n], grs[:, r0:r0 + n], ALU.mult, reads=['cvb', 'grs'], writes=[yk])
                            P.dma('sp', mix_dst(2 + i, r0, n), ym[:, 0:n], reads=[yk])
            _drain(C)

        qnT = C.sb(st, "qnT", [128, 2, TA], BF16)
        knT = C.sb(st, "knT", [128, 2, TA], BF16)
        ktm = C.sb(st, "ktm", [128, 2, NKT, 128], BF16)
        vtm = C.sb(st, "vtm", [128, 2, NKT, 128], BF16)
        zsT = C.sb(st, "zsT", [128, 2, TA], BF16)
        with ExitStack() as s2:
            hT_all = build_hT(s2)
            wchs = [C.sb(s2, "wch%d" % i, [128, 8, 128], BF16) for i in range(2)]
            pre = C.sb(s2, "pre", [128, TA + 6], F32)
            P.op('pool', 'memset', pre[:], 0.0, writes=['pre'])
            cvb = C.sb(s2, "cvb", [128, 512], F32)
            sq = C.sb(s2, "g_sq", [128, 512], BF16)
            rt = C.sb(s2, "g_rt", [128, 512], F32)
            knf = C.sb(s2, "g_knf", [128, 512], F32)
            for hh in range(2):
                wz = load_chunk_w(wchs, 6 + hh)
                for (r0, n, j) in KV_BLOCKS:
                    ps, pk = allps.next()
                    for kc in range(8):
                        C.mm(ps[:, 0:n], wz[:, kc, :], hT_all[:, kc, r0:r0 + n], kc == 0, kc == 7,
                             reads=[('wch', (6 + hh) % 2), ('hT', r0)], writes=[pk])
                    C.act(zsT[:, hh, r0:r0 + n], ps[:, 0:n], AF.Silu, reads=[pk], writes=['zsT'])
            for ci in range(6):
                kind, hh = ci // 2, ci % 2
                wc = load_chunk_w(wchs, ci)
                fill_pre(pre, wc, hT_all, ci)
                for (r0, n, j) in KV_BLOCKS:
                    conv_block(pre, cvb, ci, r0, n)
                    C.act(cvb[:, 0:n], cvb[:, 0:n], AF.Silu, reads=['cvb'], writes=['cvb'])
                    src = cvb
                    if kind < 2:
                        C.act(sq[:, 0:n], cvb[:, 0:n], AF.Square, reads=['cvb'], writes=['g_sq'])
                        ps2, pk2 = allps.next()
                        C.mm(ps2[:, 0:n], K['ones_bf'][:], sq[:, 0:n], True, True, reads=['ones_bf', 'g_sq'], writes=[pk2])
                        C.act(rt[:, 0:n], ps2[:, 0:n], AF.Sqrt, reads=[pk2], writes=['g_rt'], bias=float(EPS), scale=1.0)
                        C.recip(rt[:, 0:n], rt[:, 0:n], reads=['g_rt'], writes=['g_rt'])
                        if kind == 0:
                            C.stt(qnT[:, hh, r0:r0 + n], cvb[:, 0:n], float(128.0 ** -0.5), rt[:, 0:n], ALU.mult, ALU.mult,
                                  reads=['cvb', 'g_rt'], writes=['qnT'])
                            continue
                        C.tt('dve', knf[:, 0:n], cvb[:, 0:n], rt[:, 0:n], ALU.mult, reads=['cvb', 'g_rt'], writes=['g_knf'])
                        C.copy('pool', knT[:, hh, r0:r0 + n], knf[:, 0:n], reads=['g_knf'], writes=['knT'])
                        src = knf
                    skey = 'g_knf' if kind == 1 else 'cvb'
                    dstm, dkey = (ktm, 'ktm') if kind == 1 else (vtm, 'vtm')
                    ps, pk = allps.next()
                    nt = n // 128
                    for tt in range(nt):
                        P.op('pe', 'transpose', ps[:, tt * 128:(tt + 1) * 128], src[:, tt * 128:(tt + 1) * 128], K['ident'][:],
                             reads=[skey, 'ident'], writes=[pk])
                    for tt in range(nt):
                        C.copy('dve' if tt % 2 == 0 else 'act', dstm[:, hh, r0 // 128 + tt, :], ps[:, tt * 128:(tt + 1) * 128],
                               reads=[pk], writes=[dkey]) if tt % 2 == 0 else \
                            C.act(dstm[:, hh, r0 // 128 + tt, :], ps[:, tt * 128:(tt + 1) * 128], AF.Copy, reads=[pk], writes=[dkey])
            _drain(C)

        with ExitStack() as s2:
            oacc = C.sb(s2, "oacc", [128, 2, NKT, 128], F32)
            regs = [(C.psum[bnk][0][:, 0:128], C.psum[bnk][1]) for bnk in range(8)]
            RR = Ring(regs)
            NS = 3
            S = {}
            for m in range(4):
                S[m] = dict(
                    wT=[C.sb(s2, "wT%d_%d" % (m, i), [128, 128], BF16) for i in range(NS)],
                    aT=[C.sb(s2, "aT%d_%d" % (m, i), [128, 128], BF16) for i in range(NS)],
                    u=[C.sb(s2, "u%d_%d" % (m, i), [128, 128], F32) for i in range(NS)],
                    kd=[C.sb(s2, "kd%d_%d" % (m, i), [128, 2, 128], BF16) for i in range(NS)],
                    gsm=[C.sb(s2, "gsm%d_%d" % (m, i), [128, 10], F32) for i in range(NS)],
                    Z=[C.sb(s2, "Z%d_%d" % (m, i), [128, 128], F32) for i in range(2)],
                    ZT=[C.sb(s2, "ZT%d_%d" % (m, i), [128, 128], F32) for i in range(2)],
                    Y=[C.sb(s2, "Y%d_%d" % (m, i), [128, 128], F32) for i in range(2)],
                    grep=C.sb(s2, "grep%d" % m, [128, 128], F32),
                    DS=C.sb(s2, "DS%d" % m, [128, 128], F32),
                    DT=C.sb(s2, "DT%d" % m, [128, 128], F32),
                    bv=C.sb(s2, "bv%d" % m, [128, 128], F32),
                    kbg=C.sb(s2, "kbg%d" % m, [128, 128], F32),
                    Sf=[C.sb(s2, "Sf%d_%d" % (m, i), [128, 128], F32) for i in range(2)],
                    Sb=[C.sb(s2, "Sb%d_%d" % (m, i), [128, 128], BF16) for i in range(2)],
                    vn=[C.sb(s2, "vn%d_%d" % (m, i), [128, 128], BF16) for i in range(2)],
                    tB=C.sb(s2, "tB%d" % m, [128, 128], F32),
                    oo=C.sb(s2, "oo%d" % m, [128, 128], F32),
                    cur=0, nblk=0,
                )
                P.op('dve', 'memset', S[m]['Sf'][0][:], 0.0, writes=[('Sf', m, 0)])
                P.op('dve', 'memset', S[m]['Sb'][0][:], 0.0, writes=[('Sb', m, 0)])
                for i_ in range(2):
                    P.op('dve', 'memset', S[m]['vn'][i_][:], 0.0, writes=[('vn', m, i_)])
            identf = K['ident']

            def prep(m, t, sl):
                d, hh = m // 2, m % 2
                X = S[m]
                kn = knT[:, hh, t * 128:(t + 1) * 128]
                qn = qnT[:, hh, t * 128:(t + 1) * 128]
                kt_ = ktm[:, hh, t, :]
                vt_ = vtm[:, hh, t, :]
                gcol, bcol, nbcol = gall[:, t, m:m + 1], beta[:, t, m:m + 1], nbeta[:, t, m:m + 1]
                tri, penS, penT = cms[:, d, :], cms[:, 4 + d, :], cms[:, 6 + d, :]
                gsm, gk = X['gsm'][sl], ('gsm', m, sl)
                bank, bkey = C.psum[m]
                Q = lambda i: bank[:, i * 128:(i + 1) * 128]
                pKK, kKK = Q(0), bkey
                C.mm(pKK, kn, kn, True, True, reads=['knT'], writes=[kKK])
                pQK, kQK = Q(1), bkey
                C.mm(pQK, kn, qn, True, True, reads=['knT', 'qnT'], writes=[kQK])
                pg, kg = Q(2), bkey
                for ci_, lh in enumerate([tri, cms[:, 8, :], cms[:, 2, :], cms[:, 3, :]]):
                    C.mm(pg[:, 2 * ci_:2 * ci_ + 2], lh, gall[:, t, m:m + 2], True, True, reads=['cms', 'gall'], writes=[kg])
                C.ts('dve', X['grep'][:], K['ones_f'][:], gcol, None, ALU.mult, ALU.bypass, reads=['ones_f', 'gall'], writes=[('grep', m)])
                yield
                C.copy('dve', gsm[:, 0:4], pg[:, 0:8:2], reads=[kg], writes=[gk])
                yield
                pD, kD = Q(2), bkey
                C.mm(pD, X['grep'][:], tri, True, False, reads=[('grep', m), 'cms'], writes=[kD])
                C.mm(pD, identf[:], penS, False, True, reads=['ident', 'cms'], writes=[kD])
                pDT, kDT = Q(3), bkey
                C.mm(pDT, X['grep'][:], tri, True, False, reads=[('grep', m), 'cms'], writes=[kDT])
                C.mm(pDT, identf[:], penT, False, True, reads=['ident', 'cms'], writes=[kDT])
                yield
                C.act(gsm[:, 4:5], gsm[:, 0:1], AF.Exp, reads=[gk], writes=[gk])
                C.act(gsm[:, 5:6], gsm[:, 0:1], AF.Exp, reads=[gk], writes=[gk], scale=-1.0, bias=gsm[:, 1:2])
                C.act(gsm[:, 7:9], gsm[:, 2:4], AF.Exp, reads=[gk], writes=[gk])
                C.tt('dve', gsm[:, 6:7], bcol, gsm[:, 4:5], ALU.mult, reads=[gk, 'beta'], writes=[gk])
                yield
                C.ts('dve', X['DS'][:], pD, gsm[:, 0:1], 0.0, ALU.subtract, ALU.max, reads=[kD, gk], writes=[('DS', m)])
                C.act(X['DS'][:], X['DS'][:], AF.Exp, reads=[('DS', m)], writes=[('DS', m)], scale=-1.0)
                C.ts('dve', X['DT'][:], pDT, gsm[:, 0:1], 0.0, ALU.subtract, ALU.min, reads=[kDT, gk], writes=[('DT', m)])
                yield
                C.act(X['DT'][:], X['DT'][:], AF.Exp, reads=[('DT', m)], writes=[('DT', m)])
                yield
                Z, ZT, Y = X['Z'], X['ZT'], X['Y']
                zk = lambda i: ('Z', m, i)
                ztk = lambda i: ('ZT', m, i)
                yk = lambda i: ('Y', m, i)
                C.stt(ZT[0][:], pKK, nbcol, X['DS'][:], ALU.mult, ALU.mult, reads=[kKK, 'nbeta', ('DS', m)], writes=[ztk(0)])
                C.tt('dve', X['aT'][sl][:], pQK, X['DT'][:], ALU.mult, reads=[kQK, ('DT', m)], writes=[('aT', m, sl)])
                yield
                pZ, kZ = Q(0), bkey
                P.op('pe', 'transpose', pZ, ZT[0][:], identf[:], reads=[ztk(0), 'ident'], writes=[kZ])
                yield
                C.act(Z[0][:], pZ, AF.Copy, reads=[kZ], writes=[zk(0)])
                C.tt('dve', Y[0][:], pZ, identf[:], ALU.add, reads=[kZ, 'ident'], writes=[yk(0)])
                yield
                for k in range(5):
                    cur, nxt = k % 2, 1 - k % 2
                    p1, k1 = Q(0), bkey
                    C.mm(p1, Z[cur][:], ZT[cur][:], True, True, reads=[zk(cur), ztk(cur)], writes=[k1])
                    if k < 4:
                        p2, k2 = Q(1), bkey
                        C.mm(p2, ZT[cur][:], Z[cur][:], True, True, reads=[zk(cur), ztk(cur)], writes=[k2])
                        yield
                    if k == 4:
                        yield
                    C.act(ZT[nxt][:], p1, AF.Copy, reads=[k1], writes=[ztk(nxt)])
                    if k == 4:
                        yield
                    if k < 4:
                        C.copy('dve', Z[nxt][:], p2, reads=[k2], writes=[zk(nxt)])
                        yield
                    p3, k3 = Q(2), bkey
                    C.mm(p3, ZT[nxt][:], Y[cur][:], True, True, reads=[ztk(nxt), yk(cur)], writes=[k3])
                    yield
                    C.tt('dve', Y[nxt][:], Y[cur][:], p3, ALU.add, reads=[yk(cur), k3], writes=[yk(nxt)])
                    yield
                Yf, Yk = Y[1], yk(1)
                C.ts('dve', X['bv'][:], vt_, bcol, None, ALU.mult, ALU.bypass, reads=['vtm', 'beta'], writes=[('bv', m)])
                C.ts('dve', X['kbg'][:], kt_, gsm[:, 6:7], None, ALU.mult, ALU.bypass, reads=['ktm', gk], writes=[('kbg', m)])
                pU, kU = Q(1), bkey
                C.mm(pU, Yf[:], X['bv'][:], True, True, reads=[Yk, ('bv', m)], writes=[kU])
                pW, kW = Q(2), bkey
                C.mm(pW, X['kbg'][:], Yf[:], True, True, reads=[Yk, ('kbg', m)], writes=[kW])
                yield
                C.copy('dve', X['u'][sl][:], pU, reads=[kU], writes=[('u', m, sl)])
                C.act(X['wT'][sl][:], pW, AF.Copy, reads=[kW], writes=[('wT', m, sl)])
                for blk_ in range(2):
                    C.tt('dve', gsm[:, 9:10], gsm[:, 5:6], cms[:, 2 + blk_, 0:1], ALU.mult, reads=[gk, 'cms'], writes=[gk])
                    C.ts('dve', X['kd'][sl][:, blk_, :], kt_, gsm[:, 9:10], None, ALU.mult, ALU.bypass, reads=['ktm', gk],
                         writes=[('kd', m, sl)])

            def seq(m, t, sl, first):
                d, hh = m // 2, m % 2
                X = S[m]
                gsm, gk = X['gsm'][sl], ('gsm', m, sl)
                bank, bkey = C.psum[4 + m]
                Q = lambda i: bank[:, i * 128:(i + 1) * 128]
                for blk in ([0, 1] if d == 0 else [1, 0]):
                    R = slice(64 * blk, 64 * blk + 64)
                    cur = X['cur']
                    nxt = 1 - cur
                    vi = X['nblk'] % 2
                    X['nblk'] += 1
                    vn, vk = X['vn'][vi], ('vn', m, vi)
                    Sb, Sbk = X['Sb'][cur], ('Sb', m, cur)
                    p1, k1 = Q(0), bkey
                    C.mm(p1, X['wT'][sl][:], Sb[:], True, True, reads=[('wT', m, sl), Sbk], writes=[k1])
                    yield
                    C.tt('dve', vn[R, :], X['u'][sl][R, :], p1[R, :], ALU.subtract, reads=[('u', m, sl), k1], writes=[vk])
                    yield
                    pA, kA = Q(1), bkey
                    C.mm(pA, qnT[:, hh, t * 128:(t + 1) * 128], Sb[:], True, True, reads=['qnT', Sbk], writes=[kA])
                    pB, kB = Q(2), bkey
                    C.mm(pB, X['aT'][sl][:], vn[:], True, True, reads=[('aT', m, sl), vk], writes=[kB])
                    pS, kS = Q(3), bkey
                    C.mm(pS, X['kd'][sl][:, blk, :], vn[:], True, True, reads=[('kd', m, sl), vk], writes=[kS])
                    yield
                    C.stt(X['Sf'][nxt][:], X['Sf'][cur][:], gsm[:, 7 + blk:8 + blk], pS, ALU.mult, ALU.add,
                          reads=[('Sf', m, cur), gk, kS], writes=[('Sf', m, nxt)])
                    C.act(X['Sb'][nxt][:], X['Sf'][nxt][:], AF.Copy, reads=[('Sf', m, nxt)], writes=[('Sb', m, nxt)])
                    X['cur'] = nxt
                    C.act(X['tB'][R, :], pB[R, :], AF.Copy, reads=[kB], writes=[('tB', m)])
                    yield
                    ok = ('oacc', hh, t)
                    if first:
                        C.stt(oacc[R, hh, t, :], pA[R, :], gsm[R, 4:5], X['tB'][R, :], ALU.mult, ALU.add,
                              reads=[kA, gk, ('tB', m)], writes=[ok])
                    else:
                        C.stt(X['oo'][R, :], pA[R, :], gsm[R, 4:5], X['tB'][R, :], ALU.mult, ALU.add,
                              reads=[kA, gk, ('tB', m)], writes=[('oo', m)])
                        C.tt('pool', oacc[R, hh, t, :], oacc[R, hh, t, :], X['oo'][R, :], ALU.add, reads=[ok, ('oo', m)], writes=[ok])

            ORD = [list(range(NKT)), [1, 0] + list(range(NKT - 1, 1, -1))]
            STEP = [{t: i for i, t in enumerate(o)} for o in ORD]

            def tile_of(m, s):
                return ORD[m // 2][s]
            def interleave(gens):
                gens = list(gens)
                while gens:
                    for g in list(gens):
                        try:
                            next(g)
                        except StopIteration:
                            gens.remove(g)
            interleave([prep(m, tile_of(m, 0), 0) for m in range(4)])
            for s in range(NKT):
                gens = []
                for m in range(4):
                    t = tile_of(m, s)
                    d_ = m // 2
                    first = STEP[d_][t] < STEP[1 - d_][t] or (STEP[d_][t] == STEP[1 - d_][t] and d_ == 0)
                    gens.append(seq(m, t, s % NS, first))
                    if s + 1 < NKT:
                        gens.append(prep(m, tile_of(m, s + 1), (s + 1) % NS))
                interleave(gens)
            ssq = C.sb(s2, "f_ssq", [128, 2], F32)
            junk = C.sb(s2, "f_junk", [128, 128], F32)
            on = C.sb(s2, "f_on", [128, 128], F32)
            stg = [C.sb(s2, "f_stg%d" % i, [128, 512], BF16) for i in range(2)]
            gi = 0
            for hh in range(2):
                for t0 in range(0, NKT, 4):
                    nt = min(4, NKT - t0)
                    sg, sk = stg[gi % 2], ('stg', gi % 2)
                    gi += 1
                    for tt in range(nt):
                        t = t0 + tt
                        ok = ('oacc', hh, t)
                        C.act(junk[:], oacc[:, hh, t, :], AF.Square, reads=[ok], writes=['f_junk', 'f_ssq'], accum_out=ssq[:, 0:1])
                        C.act(ssq[:, 1:2], ssq[:, 0:1], AF.Sqrt, reads=['f_ssq'], writes=['f_ssq'], bias=float(EPS), scale=1.0 / 128)
                        C.recip(ssq[:, 1:2], ssq[:, 1:2], reads=['f_ssq'], writes=['f_ssq'])
                        C.ts('dve', on[:], oacc[:, hh, t, :], ssq[:, 1:2], None, ALU.mult, ALU.bypass, reads=[ok, 'f_ssq'], writes=['f_on'])
                        pT, kT = RR.next()
                        P.op('pe', 'transpose', pT, on[:], identf[:], reads=['f_on', 'ident'], writes=[kT])
                        C.stt(sg[:, tt * 128:(tt + 1) * 128], pT, gnorm[:, 0:1], zsT[:, hh, t * 128:(t + 1) * 128], ALU.mult, ALU.mult,
                              reads=[kT, 'cw', 'zsT'], writes=[sk])
                    P.dma('sp', mix_dst(hh, t0 * 128, nt * 128), sg[:, 0:nt * 128], reads=[sk])
            _drain(C)


def build_recB():
    nc = bass.Bass("TRN2", target_bir_lowering=False)
    with ExitStack() as st:
        C = Ctx(nc, st)
        emit_recB(C, st)
        C.P.finish()
    return nc


def emit_recB(C, st, K=None, ntok=NOWN):
    if True:
        P = C.P
        xown = C.dram_in("xown", [ntok, D])
        mixT_d = C.dram_in("mixT", [8, 128, ntok], BF16)
        cc_d = C.dram_in("cc", [128, 16])
        modw_d = C.dram_in("modw", [D, 3 * D])
        modbcol_d = C.dram_in("modb_col", [128, 24])
        modbgate_d = C.dram_in("modb_gate", [1, D])
        lng_d = C.dram_in("lng", [1, D])
        lnb_d = C.dram_in("lnb", [1, D])
        wout_d = C.dram_in("wout", [D, D])
        ident_d = C.dram_in("ident", [128, 128])
        xo = C.dram_out("xo", [ntok, D])
        K = K or emit_consts(C, st, ident_d)
        lnbc = C.sb(st, "lnbc", [128, 2, 1024], F32)
        P.dma('sp', lnbc[:, 0, :], lng_d[0:1, :].broadcast_to([128, D]), writes=['lnbc'])
        P.dma('sp', lnbc[:, 1, :], lnb_d[0:1, :].broadcast_to([128, D]), writes=['lnbc'])
        _, gate_bc = emit_modulation(C, st, K, cc_d, modw_d, modbcol_d, modbgate_d, need_cols=False)
        wout = C.sb(st, "wout", [128, 8, 1024], BF16)
        load_w_bf16(C, wout, 'wout', wout_d, 8, 1024)
        mixT = C.sb(st, "mixT", [128, 8, ntok], BF16)
        for c in range(8):
            P.dma('sp', mixT[:, c, :], mixT_d[c], writes=[('mixT', c)])
        E = alloc_epilogue(C, st)
        allps = Ring(C.psum)
        for ti in range(ntok // 128):
            r0 = ti * 128
            j = 1 if r0 < TC else 0
            emit_epilogue_tile(C, E, lambda c, r0=r0: (mixT[:, c, r0:r0 + 128], [('mixT', c)]), wout,
                               xown[r0:r0 + 128, :], xo[r0:r0 + 128, :], j, gate_bc, lnbc, allps, ti)
        _drain(C)


def run_rec_layer(inp, l, x, ctx):
    li = l // 2
    com = _common_layer(inp, l)
    ncA = _get('recA', build_recA)
    in_maps = []
    for core in range(8):
        b, hf = core // 2, core % 2
        m = dict(xall=np.ascontiguousarray(np.concatenate([ctx[b], x[b]], axis=0)), cc=_cc_cols(inp['c'][b], inp['c_ctx']),
                 modw=com['modw'], modb_col=com['modb_col'], modb_gate=com['modb_gate'], ident=com['ident'])
        m.update(prep_rec_layer(inp, l, hf))
        in_maps.append(m)
    resA = run_bass_kernel_spmd(ncA, in_maps, core_ids=list(range(8)))
    ncB = _get('recB', build_recB)
    wout = np.asarray(inp['rec_w_out'][li], np.float32)
    in_maps = []
    for core in range(8):
        b, hf = core // 2, core % 2
        mo = [resA.results[2 * b + k]["mixo"] for k in range(2)]
        full = np.concatenate([mo[0][0:2], mo[1][0:2], mo[0][2:4], mo[1][2:4]], axis=0)
        sel = np.concatenate([np.arange(TC), TC + hf * OWN + np.arange(OWN)])
        m = dict(xown=np.ascontiguousarray(np.concatenate([ctx[b], x[b, hf * OWN:(hf + 1) * OWN]], axis=0)),
                 mixT=np.ascontiguousarray(full[:, :, sel]), cc=_cc_cols(inp['c'][b], inp['c_ctx']), wout=wout)
        m.update(com)
        in_maps.append(m)
    resB = run_bass_kernel_spmd(ncB, in_maps, core_ids=list(range(8)))
    xn = np.empty_like(x)
    cn = np.empty_like(ctx)
    for core in range(8):
        b, hf = core // 2, core % 2
        o = resB.results[core]["xo"]
        xn[b, hf * OWN:(hf + 1) * OWN] = o[TC:]
        if hf == 0:
            cn[b] = o[:TC]
    return xn, cn


def build_fused():
    nc = bass.Bass("TRN2", target_bir_lowering=False)
    with ExitStack() as st:
        C = Ctx(nc, st)
        ident_d = C.dram_in("ident", [128, 128])
        K = emit_consts(C, st, ident_d)
        x_in = C.dram_in("x_in", [TA, D])
        out = C.dram_out("xfin", [TA, D])
        xb = [nc.dram_tensor("xb%d" % i, [TA, D], F32, kind="Internal").ap() for i in range(2)]
        mixb = nc.dram_tensor("mixb", [8, 128, TA], BF16, kind="Internal").ap()
        for l in range(4):
            src = x_in if l == 0 else xb[(l - 1) % 2]
            dst = out if l == 3 else xb[l % 2]
            C.lsfx = "_L%d" % l
            if l % 2 == 0:
                C.sfx = "_L%d" % l
                C.override = dict(xall=src, xown=src, xo=dst)
                with ExitStack() as ph:
                    emit_att(C, ph, K, full=True)
            else:
                for hfr in range(2):
                    C.sfx = "_L%d_%d" % (l, hfr)
                    C.override = dict(xall=src)
                    with ExitStack() as ph:
                        emit_recA(C, ph, K, mix_dst=lambda idx, c0, n, hfr=hfr: mixb[
                            (2 * hfr + idx) if idx < 2 else (4 + 2 * hfr + idx - 2), :, c0:c0 + n])
                C.sfx = "_L%d" % l
                C.override = dict(xown=src, mixT=mixb, xo=dst)
                with ExitStack() as ph:
                    emit_recB(C, ph, K, ntok=TA)
        C.P.finish()
    return nc


def fused_inputs(inp, b):
    m = dict(ident=np.eye(128, dtype=np.float32), cc=_cc_cols(inp['c'][b], inp['c_ctx']),
             x_in=np.ascontiguousarray(np.concatenate([inp['ctx'][b], inp['x'][b]], axis=0), dtype=np.float32))
    LAYER = Ctx.LAYER
    for l in range(4):
        com = _common_layer(inp, l)
        for k in LAYER:
            m[k + "_L%d" % l] = com[k]
        if l % 2 == 0:
            sh, (tabKg, tabKm) = prep_att_layer(inp, l)
            for k, v in sh.items():
                if k not in LAYER and k != 'ident':
                    m[k + "_L%d" % l] = v
            m["tabKg_L%d" % l] = tabKg
            m["tabKm_L%d" % l] = tabKm
        else:
            for hfr in range(2):
                for k, v in prep_rec_layer(inp, l, hfr).items():
                    m[k + "_L%d_%d" % (l, hfr)] = v
            m["wout_L%d" % l] = np.asarray(inp['rec_w_out'][l // 2], np.float32)
    return m


def kernel(**inputs):
    inp = {k: np.asarray(v) for k, v in inputs.items()}
    nc = _get('fused', build_fused)
    per_b = [fused_inputs(inp, b) for b in range(NB)]
    in_maps = [per_b[core // 2] for core in range(8)]
    res = run_bass_kernel_spmd(nc, in_maps, core_ids=list(range(8)))
    out = np.empty((NB, TL, D), np.float32)
    for b in range(NB):
        out[b] = res.results[2 * b]["xfin"][TC:]
    return out


def kernel_unfused(**inputs):
    inp = {k: np.asarray(v) for k, v in inputs.items()}
    x = np.ascontiguousarray(inp['x'], dtype=np.float32)
    ctx = np.ascontiguousarray(inp['ctx'], dtype=np.float32)
    for l in range(4):
        if l % 2 == 0:
            x, ctx = run_att_layer(inp, l, x, ctx)
        else:
            x, ctx = run_rec_layer(inp, l, x, ctx)
    return x
```
